# Optimizing a Trainium2 kernel written in Bass

```python
import jax, jax.numpy as jnp
from jax import lax
import numpy as np

D_MODEL = 4096
BATCH = 4
SEQ = 4096
DEPTH = 1

HEAD_DIM = 128
ATTN_WIDTH = D_MODEL // 2
N_ATTN_HEADS = ATTN_WIDTH // HEAD_DIM
DILATED_CONFIGS = ((128, 1), (512, 4), (2048, 16))
N_DIL = len(DILATED_CONFIGS)
BLOCK = 128
POOL_WIDTH = D_MODEL - ATTN_WIDTH
POOL_WINDOWS = (2, 4, 8, 16)
N_POOL_GROUPS = len(POOL_WINDOWS)
POOL_GROUP = POOL_WIDTH // N_POOL_GROUPS
ATTN_IN = N_DIL * 3 * ATTN_WIDTH
MIX_IN = ATTN_IN + POOL_WIDTH
MIX_OUT = ATTN_WIDTH + POOL_WIDTH
D_FF = ((8 * D_MODEL // 3 + 255) // 256) * 256
MEM_LEN = 256
CROSS_HEADS = 4
CROSS_DIM = 128
ROPE_THETA = 10000.0
EPS = 1e-6
NEG_INF = -1e30

kernel_name = "hymba_pool_dilated_macaron_layer"


def rmsnorm(x, g):
    xf = x.astype(jnp.float32)
    y = xf * lax.rsqrt(jnp.mean(xf * xf, axis=-1, keepdims=True) + EPS)
    return (y * g.astype(jnp.float32)).astype(x.dtype)


def swiglu(u, w_in, w_out):
    a, b = jnp.split(u @ w_in, 2, axis=-1)
    return (jax.nn.silu(a) * b) @ w_out


def rope_tables(positions):
    inv = 1.0 / (ROPE_THETA ** (jnp.arange(0, HEAD_DIM, 2, dtype=jnp.float32) / HEAD_DIM))
    ang = positions.astype(jnp.float32)[..., None] * inv
    ang = jnp.concatenate([ang, ang], axis=-1)[:, :, None, :]
    return jnp.cos(ang), jnp.sin(ang)


def apply_rope(t, cos, sin):
    tf = t.astype(jnp.float32)
    t1, t2 = jnp.split(tf, 2, axis=-1)
    rot = jnp.concatenate([-t2, t1], axis=-1)
    return (tf * cos + rot * sin).astype(t.dtype)


def dilated_window_attention(q, k, v, dil, sub_window):
    B, S, H, Dh = q.shape
    L = S // dil
    N = B * dil

    def to_sub(t):
        return t.reshape(B, L, dil, H, Dh).transpose(0, 2, 1, 3, 4).reshape(N, L, H, Dh)

    qs, ks, vs = to_sub(q), to_sub(k), to_sub(v)
    nb = -(-L // BLOCK)
    Lp = nb * BLOCK
    qs = jnp.pad(qs, ((0, 0), (0, Lp - L), (0, 0), (0, 0)))
    ks = jnp.pad(ks, ((0, 0), (BLOCK, Lp - L), (0, 0), (0, 0)))
    vs = jnp.pad(vs, ((0, 0), (BLOCK, Lp - L), (0, 0), (0, 0)))

    qb = qs.reshape(N, nb, BLOCK, H, Dh)

    def band(t):
        prev = t[:, :Lp].reshape(N, nb, BLOCK, H, Dh)
        cur = t[:, BLOCK:].reshape(N, nb, BLOCK, H, Dh)
        return jnp.concatenate([prev, cur], axis=2)

    kb, vb = band(ks), band(vs)
    scores = jnp.einsum('nbqhd,nbkhd->nbhqk', qb, kb,
                        preferred_element_type=jnp.float32) * (HEAD_DIM ** -0.5)
    qi = jnp.arange(BLOCK)[:, None]
    kj = jnp.arange(2 * BLOCK)[None, :]
    dist = qi + BLOCK - kj
    key_idx = jnp.arange(nb)[:, None, None] * BLOCK - BLOCK + kj[None]
    mask = (dist >= 0)[None] & (dist <= sub_window)[None] & (key_idx >= 0)
    scores = jnp.where(mask[None, :, None], scores, NEG_INF)
    lse = jax.nn.logsumexp(scores, axis=-1)
    p = jnp.exp(scores - lse[..., None])
    o = jnp.einsum('nbhqk,nbkhd->nbqhd', p.astype(v.dtype), vb)

    o = o.reshape(N, Lp, H, Dh)[:, :L].reshape(B, dil, L, H, Dh)
    o = o.transpose(0, 2, 1, 3, 4).reshape(B, S, H, Dh)
    lse = lse.transpose(0, 1, 3, 2).reshape(N, Lp, H)[:, :L].reshape(B, dil, L, H)
    lse = lse.transpose(0, 2, 1, 3).reshape(B, S, H)
    return o, lse


def pooling_mixer(zp, w_pool, pool_scale):
    B, S, _ = zp.shape
    zg = zp.reshape(B, S, N_POOL_GROUPS, POOL_GROUP).astype(jnp.float32)
    csum = jnp.cumsum(zg, axis=1)
    t = jnp.arange(S)
    groups = []
    for g, w in enumerate(POOL_WINDOWS):
        c = csum[:, :, g]
        lag = jnp.pad(c, ((0, 0), (w, 0), (0, 0)))[:, :S]
        cnt = jnp.minimum(t + 1, w).astype(jnp.float32)[None, :, None]
        groups.append((c - lag) / cnt - zg[:, :, g])
    y = jnp.stack(groups, axis=2).astype(zp.dtype)
    y = jnp.einsum('bsgc,gcd->bsgd', y, w_pool).reshape(B, S, POOL_WIDTH)
    return y * pool_scale


def hybrid_mixer(u, w_mix_in, w_pool, pool_scale, w_mix_out, cos, sin):
    B, S, _ = u.shape
    z = u @ w_mix_in
    za = z[..., :ATTN_IN].reshape(B, S, N_DIL, 3, N_ATTN_HEADS, HEAD_DIM)
    zp = z[..., ATTN_IN:]
    outs, lses = [], []
    for i, (win, dil) in enumerate(DILATED_CONFIGS):
        q = apply_rope(za[:, :, i, 0], cos, sin)
        k = apply_rope(za[:, :, i, 1], cos, sin)
        v = za[:, :, i, 2]
        o, lse = dilated_window_attention(q, k, v, dil, win // dil)
        outs.append(o)
        lses.append(lse)
    wts = jax.nn.softmax(jnp.stack(lses, axis=0), axis=0)
    o_attn = jnp.einsum('gbsh,gbshd->bshd', wts, jnp.stack(outs, axis=0).astype(jnp.float32))
    o_attn = o_attn.astype(u.dtype).reshape(B, S, ATTN_WIDTH)
    o_pool = pooling_mixer(zp, w_pool, pool_scale)
    return jnp.concatenate([o_attn, o_pool], axis=-1) @ w_mix_out


def memory_cross_attention(u, m, w_q, w_kv, w_o):
    B, S, _ = u.shape
    M = m.shape[1]
    q = (u @ w_q).reshape(B, S, CROSS_HEADS, CROSS_DIM)
    kv = (m @ w_kv).reshape(B, M, 2, CROSS_HEADS, CROSS_DIM)
    k, v = kv[:, :, 0], kv[:, :, 1]
    s = jnp.einsum('bshd,bmhd->bhsm', q, k, preferred_element_type=jnp.float32) * (CROSS_DIM ** -0.5)
    p = jax.nn.softmax(s, axis=-1)
    o = jnp.einsum('bhsm,bmhd->bshd', p.astype(v.dtype), v).reshape(B, S, CROSS_HEADS * CROSS_DIM)
    return o @ w_o


def setup_inputs(seed: int = 0) -> dict:
    key = jax.random.key(seed)
    ks = jax.random.split(key, 24)
    f32 = jnp.float32

    def nrm(k, shape, fan_in):
        return jax.random.normal(k, shape, f32) * (fan_in ** -0.5)

    def gain(k, shape):
        return jnp.ones(shape, f32) + 0.02 * jax.random.normal(k, shape, f32)

    x = jax.random.normal(ks[0], (BATCH, SEQ, D_MODEL), f32)
    mem = jax.random.normal(ks[1], (BATCH, MEM_LEN, D_MODEL), f32)
    offset = jax.random.randint(ks[2], (BATCH, 1), 0, 1024, dtype=jnp.int32)
    positions = jnp.arange(SEQ, dtype=jnp.int32)[None, :] + offset
    return {
        "x": x,
        "mem": mem,
        "positions": positions,
        "g_ffn1": gain(ks[3], (DEPTH, D_MODEL)),
        "w_ffn1_in": nrm(ks[4], (DEPTH, D_MODEL, 2 * D_FF), D_MODEL),
        "w_ffn1_out": nrm(ks[5], (DEPTH, D_FF, D_MODEL), D_FF),
        "g_mix": gain(ks[6], (DEPTH, D_MODEL)),
        "w_mix_in": nrm(ks[7], (DEPTH, D_MODEL, MIX_IN), D_MODEL),
        "w_pool": nrm(ks[8], (DEPTH, N_POOL_GROUPS, POOL_GROUP, POOL_GROUP), POOL_GROUP),
        "pool_scale": gain(ks[9], (DEPTH, POOL_WIDTH)),
        "w_mix_out": nrm(ks[10], (DEPTH, MIX_OUT, D_MODEL), MIX_OUT),
        "g_cross": gain(ks[11], (DEPTH, D_MODEL)),
        "g_mem": gain(ks[12], (DEPTH, D_MODEL)),
        "w_cross_q": nrm(ks[13], (DEPTH, D_MODEL, CROSS_HEADS * CROSS_DIM), D_MODEL),
        "w_cross_kv": nrm(ks[14], (DEPTH, D_MODEL, 2 * CROSS_HEADS * CROSS_DIM), D_MODEL),
        "w_cross_o": nrm(ks[15], (DEPTH, CROSS_HEADS * CROSS_DIM, D_MODEL), CROSS_HEADS * CROSS_DIM),
        "g_ffn2": gain(ks[16], (DEPTH, D_MODEL)),
        "w_ffn2_in": nrm(ks[17], (DEPTH, D_MODEL, 2 * D_FF), D_MODEL),
        "w_ffn2_out": nrm(ks[18], (DEPTH, D_FF, D_MODEL), D_FF),
        "g_final": gain(ks[19], (D_MODEL,)),
    }


def reference(x, mem, positions, g_ffn1, w_ffn1_in, w_ffn1_out, g_mix, w_mix_in, w_pool,
              pool_scale, w_mix_out, g_cross, g_mem, w_cross_q, w_cross_kv, w_cross_o,
              g_ffn2, w_ffn2_in, w_ffn2_out, g_final):
    cos, sin = rope_tables(positions)
    h = x
    for l in range(DEPTH):
        h = h + 0.5 * swiglu(rmsnorm(h, g_ffn1[l]), w_ffn1_in[l], w_ffn1_out[l])
        h = h + hybrid_mixer(rmsnorm(h, g_mix[l]), w_mix_in[l], w_pool[l], pool_scale[l],
                             w_mix_out[l], cos, sin)
        h = h + memory_cross_attention(rmsnorm(h, g_cross[l]), rmsnorm(mem, g_mem[l]),
                                       w_cross_q[l], w_cross_kv[l], w_cross_o[l])
        h = h + 0.5 * swiglu(rmsnorm(h, g_ffn2[l]), w_ffn2_in[l], w_ffn2_out[l])
    return rmsnorm(h, g_final)
```

```python
import math
from contextlib import ExitStack

import numpy as np
import concourse.bass as bass
import concourse.mybir as mybir
from concourse.bass_utils import run_bass_kernel_spmd

F32 = mybir.dt.float32
BF16 = mybir.dt.bfloat16
I32 = mybir.dt.int32
AF = mybir.ActivationFunctionType
ALU = mybir.AluOpType

EPS = 1e-6
ROPE_THETA = 10000.0
DILS = (1, 4, 16)
POOL_W = (2, 4, 8, 16)
MEM = 256
T = 512
NST = 4


class Cfg:
    def __init__(self, D=4096, S=4096, B=4):
        self.D, self.SF, self.B = D, S, B
        self.S = S // 2
        S = self.S
        self.KC = D // 128
        self.AW = D // 2
        self.H = self.AW // 128
        self.PW = D - self.AW
        self.PG = self.PW // 4
        self.PGC = self.PG // 128
        self.F = ((8 * D // 3 + 255) // 256) * 256
        self.FC = self.F // 128
        self.FH = self.FC // 2
        self.NB = D // 512
        self.NT = S // T
        self.CB = min(512, self.AW)
        self.QB = self.AW // self.CB
        self.ATTN_IN = 9 * self.AW


class Buf:
    __slots__ = ("w", "rs")

    def __init__(self):
        self.w = None
        self.rs = []


class Op:
    __slots__ = ("eng", "fn", "deps", "sig", "kind", "sem", "val", "prevslot")

    def __init__(self, eng, fn, kind):
        self.eng, self.fn, self.kind = eng, fn, kind
        self.deps = []
        self.sig = False
        self.sem = None
        self.val = 0
        self.prevslot = None


ENGS = ("pe", "act", "dve", "pool", "sp")
KQ = 8


class Prog:
    def __init__(self):
        self.ops = {e: [] for e in ENGS}
        self.lastc = {e: None for e in ENGS}
        self.dmas = {e: [] for e in ENGS}
        self.ccs = []
        self.nobar = set()

    def add(self, eng, fn, reads=(), writes=(), kind="c"):
        op = Op(eng, fn, kind)
        deps = []
        for b in reads:
            if b.w is not None:
                deps.append(b.w)
        for b in writes:
            if b.w is not None:
                deps.append(b.w)
            deps.extend(b.rs)
        for b in reads:
            b.rs.append(op)
        for b in writes:
            b.w = op
            b.rs = []
        seen = set()
        for d in deps:
            if id(d) in seen or d is op:
                continue
            seen.add(id(d))
            if d.eng == "pe" and eng == "pe" and d.kind == "c" and kind == "c":
                continue
            op.deps.append(d)
            d.sig = True
        self.ops[eng].append(op)
        if kind == "c":
            self.lastc[eng] = op
        elif kind == "d":
            self.dmas[eng].append(op)
        return op

    def dma(self, eng, fn, reads=(), writes=(), bar=True):
        op = self.add(eng, fn, reads, writes, kind="d")
        if not bar:
            self.nobar.add(id(op))
        return op

    def cc(self, fn, writes=(), first=True):
        if first:
            self.barrier(full=True)
        op = Op("pool", fn, "cc")
        if self.ccs:
            op.deps.append(self.ccs[-1])
        for b in writes:
            b.w = op
            b.rs = []
        self.ops["pool"].append(op)
        self.ccs.append(op)
        return op

    def barrier(self, full=False):
        deps = []
        for e in ENGS:
            if self.lastc[e] is not None:
                deps.append(self.lastc[e])
            deps.extend(d for d in self.dmas[e][-KQ:] if full or id(d) not in self.nobar)
        deps.extend(self.ccs)
        for e in ENGS:
            op = Op(e, None, "f")
            for d in deps:
                op.deps.append(d)
                d.sig = True
            self.ops[e].append(op)

    def emit(self, nc, es):
        semh = {}
        for e in ENGS:
            semh[("c", e)] = es.enter_context(nc.semaphore(f"c_{e}"))
            if e in ("sp", "pool"):
                for k in range(KQ):
                    semh[("d", e, k)] = es.enter_context(nc.semaphore(f"d_{e}{k}"))
        for k, op in enumerate(self.ccs):
            op.sem = ("cc", k)
            op.val = 1
            semh[op.sem] = es.enter_context(nc.semaphore(f"cc_{k}"))
        for e in ENGS:
            cnt = 0
            dl = []
            for op in self.ops[e]:
                if op.kind == "c" and op.sig:
                    cnt += 1
                    op.sem = ("c", e)
                    op.val = cnt
                elif op.kind == "d":
                    n = len(dl)
                    op.sem = ("d", e, n % KQ)
                    op.val = 16 * (n // KQ + 1)
                    if n >= KQ:
                        op.prevslot = dl[n - KQ]
                    dl.append(op)
        block = es.enter_context(nc.Block())

        def run(e, eng):
            waited = {}
            for op in self.ops[e]:
                deps = op.deps if op.prevslot is None else op.deps + [op.prevslot]
                if len(deps) > 1:
                    deps = sorted(deps, key=lambda d: -d.val)
                for d in deps:
                    if waited.get(d.sem, 0) >= d.val:
                        continue
                    eng.wait_ge(semh[d.sem], d.val)
                    waited[d.sem] = d.val
                if op.kind == "f":
                    continue
                ins = op.fn(eng)
                if op.kind == "d":
                    ins.then_inc(semh[op.sem], 16)
                elif op.kind == "cc":
                    ins.then_inc(semh[op.sem])
                elif op.sig:
                    ins.then_inc(semh[op.sem], 1)

        block.tensor(lambda eng: run("pe", eng))
        block.scalar(lambda eng: run("act", eng))
        block.vector(lambda eng: run("dve", eng))
        block.gpsimd(lambda eng: run("pool", eng))
        block.sync(lambda eng: run("sp", eng))


def build(cfg):
    D, S, KC, H, FC, FH, NB, NT = cfg.D, cfg.S, cfg.KC, cfg.H, cfg.FC, cfg.FH, cfg.NB, cfg.NT
    AW, CB, QB, PGC = cfg.AW, cfg.CB, cfg.QB, cfg.PGC
    NPC = 4 * PGC
    nc = bass.Bass("TRN2", target_bir_lowering=False)
    P = Prog()

    def din(name, shape, dt=F32):
        return nc.dram_tensor(name, list(shape), dt, kind="ExternalInput").ap()

    x_d = din("x", [S, D])
    mem_d = din("mem", [MEM, D])
    pos_d = din("pos", [128, S // 128], I32)
    gfm_d = din("gfm", [128, 5 * KC])
    gfin_d = din("gfin", [1, D])
    psc_d = din("psc", [128, NPC])
    cmask_d = din("cmask", [128, 1024], BF16)
    cf32_d = din("cf32", [128, 128 + 64 + 64 + 2])
    w1i_d = din("w1i", [2 * FC, 128, KC * 128])
    w1o_d = din("w1o", [NB, FC, 128, 512])
    w2i_d = din("w2i", [2 * FC, 128, KC * 128])
    w2o_d = din("w2o", [NB, FC, 128, 512])
    wqkv_d = din("wqkv", [9 * QB, KC, 128, CB])
    wzp_d = din("wzp", [NPC, 128, KC * 128])
    wpl_d = din("wpl", [4, 128, PGC * cfg.PG])
    wmo_d = din("wmo", [NB, KC, 128, 512])
    wcq_d = din("wcq", [4, 128, KC * 128])
    wck_d = din("wck", [4, 128, KC * 128])
    wcv_d = din("wcv", [1, KC, 128, 512])
    wco_d = din("wco", [NB, 4, 128, 512])
    y_d = nc.dram_tensor("y", [S, D], F32, kind="ExternalOutput").ap()

    h1_s = nc.dram_tensor("h1_s", [S, D], F32).ap()
    qkv_s = [[nc.dram_tensor(f"qkv_s{g}{r}", [S, AW], BF16).ap() for r in range(3)] for g in range(3)]
    zp_s = nc.dram_tensor("zp_s", [NPC, 128, S], F32).ap()
    o_s = [nc.dram_tensor(f"o_s{g}", [S, H * 129], F32).ap() for g in range(3)]
    XROWS = 2 * S + 4 * T
    XR = {(2, 1): 0, (2, 2): S, (1, 1): 2 * S, (1, 2): 2 * S + T, (0, 1): 2 * S + 2 * T, (0, 2): 2 * S + 3 * T}
    xin_h = nc.dram_tensor("xin", [XROWS, AW], BF16)
    xout_h = nc.dram_tensor("xout", [XROWS, AW], BF16)
    xzin_h = nc.dram_tensor("xzin", [NPC * 128, 16], F32)
    xzout_h = nc.dram_tensor("xzout", [NPC * 128, 16], F32)
    xin, xout = xin_h.ap(), xout_h.ap()
    xzin = xzin_h.ap().rearrange("(c p) t -> c p t", p=128)
    xzout = xzout_h.ap().rearrange("(c p) t -> c p t", p=128)
    pairs = [[2 * k, 2 * k + 1] for k in range(cfg.B)]

    es = ExitStack()
    with es:
        def sb(name, shape, dt):
            return es.enter_context(nc.sbuf_tensor("s_" + name, list(shape), dt))

        h_t = sb("h", [128, NST * D], F32)
        uT_t = sb("uT", [128, KC * T], BF16)
        WSL = 3
        wr_t = [sb(f"wr{i}", [128, 4096], BF16) for i in range(WSL)]
        ARENA = 13312
        ar_t = sb("arena", [128, ARENA], F32)
        cons_t = sb("consf", [128, 258], F32)
        gfm_t = sb("gfm", [128, 5 * KC], F32)
        psc_t = sb("psc", [128, NPC], F32)
        cmask_t = sb("cmask", [128, 1024], BF16)
        identb_t = sb("identb", [128, 128], BF16)
        onesb_t = sb("onesb", [128, 128], BF16)
        posi_t = sb("posi", [128, S // 128], I32)
        posf_t = sb("posf", [128, S // 128], F32)
        cs_t = sb("cs", [128, 2 * NST * 64], F32)
        st_t = sb("stat", [128, 256], F32)
        ri_t = sb("rint", [128, 64], I32)
        diag_t = sb("diag", [128, 2 * 128], F32)
        kmT_t = sb("kmT", [128, 4 * MEM], BF16)
        vm_t = sb("vm", [128, 2 * 512], BF16)
        negpi_t = sb("negpi", [128, 1], F32)
        eps_t = sb("epst", [128, 1], F32)
        fin_t = sb("fint", [128, 4], F32)
        ps_t = [es.enter_context(nc.psum_tensor(f"ps{i}", [128, 512], F32)) for i in range(8)]

        h = [h_t[:, st * D:(st + 1) * D] for st in range(NST)]
        hB = [Buf() for _ in range(NST)]
        uT = uT_t[:].rearrange("p (k t) -> p k t", t=T)
        uTB = Buf()
        wrB = [Buf() for _ in range(WSL)]
        psB = [Buf() for _ in range(8)]
        consB, statB, diagB, csB, memB, arB = Buf(), Buf(), [Buf(), Buf()], Buf(), Buf(), Buf()
        xoutB, xzB = Buf(), Buf()
        identf = cons_t[:, 0:128]
        invf = cons_t[:, 128:192]
        invc = cons_t[:, 192:256]
        gfm = gfm_t[:].rearrange("p (n k) -> p n k", k=KC)
        cmask = cmask_t[:, 0:512]
        cmask1 = cmask_t[:, 512:1024]
        flag0 = cons_t[:, 256:257]
        flag1 = cons_t[:, 257:258]
        cs = cs_t[:].rearrange("p (c s j) -> p c s j", c=2, s=NST)

        state = {"w": 0, "ps": 0, "diag": 0}

        def wslot():
            i = state["w"] % WSL
            state["w"] += 1
            return wr_t[i], wrB[i]

        def pbank():
            i = state["ps"] % 8
            state["ps"] += 1
            return ps_t[i], psB[i]

        def af32(off, n, t=None):
            t = ar_t if t is None else t
            return t[:, off:off + n]

        def abf(off, n, t=None):
            t = ar_t if t is None else t
            return t[:, off:off + n].bitcast(BF16)

        for (dst, src) in ((cons_t[:], cf32_d), (gfm_t[:], gfm_d), (psc_t[:], psc_d),
                           (cmask_t[:], cmask_d), (posi_t[:], pos_d)):
            P.dma("sp", lambda e, d=dst, s=src: e.dma_start(out=d, in_=s), writes=[consB])
        P.add("dve", lambda e: e.tensor_copy(out=posf_t[:], in_=posi_t[:]), reads=[consB], writes=[consB])
        P.add("dve", lambda e: e.tensor_copy(out=identb_t[:], in_=identf), reads=[consB], writes=[consB])
        P.add("dve", lambda e: e.memset(onesb_t[:], 1.0), writes=[consB])
        P.add("dve", lambda e: e.memset(negpi_t[:], -math.pi), writes=[consB])
        P.add("dve", lambda e: e.memset(eps_t[:], EPS), writes=[consB])

        def norm_sub(hap, hb, gi, col0, junk_off):
            junk = abf(junk_off, D // 2)
            ssq = st_t[:, 0:1]
            rstd = st_t[:, 1:2]
            di = state["diag"] % 2
            state["diag"] += 1
            dg = diag_t[:, di * 128:(di + 1) * 128]
            P.add("dve", lambda e: e.memset(ssq, 0.0), writes=[statB])
            P.add("act", lambda e: e.activation(out=junk, in_=hap, func=AF.Square, accum_out=ssq),
                  reads=[hb], writes=[statB, arB])
            P.add("act", lambda e: e.activation(out=rstd, in_=ssq, func=AF.Sqrt, bias=eps_t[:], scale=1.0 / D),
                  reads=[consB], writes=[statB])
            P.add("dve", lambda e: e.reciprocal(out=rstd, in_=rstd), writes=[statB])
            P.add("dve", lambda e: e.tensor_scalar_mul(out=dg, in0=identf, scalar1=rstd),
                  reads=[statB, consB], writes=[diagB[di]])
            for k0 in range(0, KC, 4):
                pt, pb = pbank()

                def mm(e, k0=k0, pt=pt):
                    for j in range(4):
                        ins = e.matmul(pt[:, j * 128:(j + 1) * 128], lhsT=hap[:, (k0 + j) * 128:(k0 + j + 1) * 128],
                                       rhs=dg, start=True, stop=True)
                    return ins
                P.add("pe", mm, reads=[hb, diagB[di]], writes=[pb])
                if gi is None:
                    P.add("dve", lambda e, k0=k0, pt=pt: e.tensor_copy(
                        out=uT[:, k0:k0 + 4, col0:col0 + 128], in_=pt[:].rearrange("p (a b) -> p a b", b=128)),
                        reads=[pb], writes=[uTB])
                else:
                    P.add("dve", lambda e, k0=k0, pt=pt: e.tensor_tensor(
                        out=uT[:, k0:k0 + 4, col0:col0 + 128], in0=pt[:].rearrange("p (a b) -> p a b", b=128),
                        in1=gfm[:, gi, k0:k0 + 4].to_broadcast([128, 4, 128]), op=ALU.mult),
                        reads=[pb, consB], writes=[uTB])
            return rstd

        def r1(wd, c, ncols, consume, src=None, srcB=None, nk=None):
            src = uT if src is None else src
            srcB = uTB if srcB is None else srcB
            nk = KC if nk is None else nk
            wt, wb = wslot()
            P.dma("pool", lambda e: e.dma_start(out=wt[:, :nk * 128], in_=wd[c]), writes=[wb])
            pt, pb = pbank()
            wv = wt[:, :nk * 128].rearrange("p (k j) -> p k j", j=128)

            def mm(e):
                for k in range(nk):
                    ins = e.matmul(pt[:, :ncols], lhsT=wv[:, k, :], rhs=src[:, k, :ncols], start=(k == 0), stop=(k == nk - 1))
                return ins
            P.add("pe", mm, reads=[wb, srcB], writes=[pb])
            consume(pt, pb)

        def r2(lhs, lhsB, nK, wd, nblocks, cb, nst, consume):
            GK = 4096 // cb
            for n in range(nblocks):
                banks = [pbank() for _ in range(nst)]
                for k0 in range(0, nK, GK):
                    gk = min(GK, nK - k0)
                    wt, wb = wslot()
                    P.dma("pool", lambda e, wt=wt, n=n, k0=k0, gk=gk: e.dma_start(
                        out=wt[:, :gk * cb].rearrange("p (k j) -> p k j", j=cb),
                        in_=wd[n, k0:k0 + gk].rearrange("k p j -> p k j")), writes=[wb])

                    def mm(e, wt=wt, k0=k0, gk=gk, banks=banks):
                        wv = wt[:, :gk * cb].rearrange("p (k j) -> p k j", j=cb)
                        for st in range(nst):
                            for k in range(gk):
                                ins = e.matmul(banks[st][0][:, :cb], lhsT=lhs(k0 + k, st), rhs=wv[:, k, :],
                                               start=(k0 + k == 0), stop=(k0 + k == nK - 1))
                        return ins
                    P.add("pe", mm, reads=[wb] + list(lhsB), writes=[b for (_, b) in banks])
                for st in range(nst):
                    consume(n, st, banks[st][0], banks[st][1])

        def uT_lhs(k, st):
            return uT[:, k, st * 128:(st + 1) * 128]

        def ffn(wi_d, wo_d):
            hid = abf(0, FH * T // 2).rearrange("p (c t) -> p c t", t=T)
            hidB = Buf()
            sil = [af32(FH * T // 2 + i * T, T) for i in range(2)]
            silB = [Buf(), Buf()]
            for half in range(2):
                for ci in range(FH):
                    c = half * FH + ci
                    got = {}
                    r1(wi_d, c, T, lambda pt, pb: got.update(a=(pt, pb)))
                    r1(wi_d, FC + c, T, lambda pt, pb: got.update(b=(pt, pb)))
                    (pa, pab), (pbk, pbb) = got["a"], got["b"]
                    si = ci % 2
                    P.add("act", lambda e, pa=pa, si=si: e.activation(out=sil[si], in_=pa[:], func=AF.Silu),
                          reads=[pab], writes=[silB[si]])
                    P.add("dve", lambda e, pbk=pbk, si=si, ci=ci: e.tensor_tensor(
                        out=hid[:, ci, :], in0=pbk[:], in1=sil[si], op=ALU.mult),
                        reads=[pbb, silB[si]], writes=[hidB])

                def cons(n, st, pt, pb):
                    P.add("dve", lambda e: e.scalar_tensor_tensor(
                        out=h[st][:, n * 512:(n + 1) * 512], in0=pt[:], scalar=0.5,
                        in1=h[st][:, n * 512:(n + 1) * 512], op0=ALU.mult, op1=ALU.add),
                        reads=[pb], writes=[hB[st]])
                wo_half = wo_d[:, half * FH:(half + 1) * FH]
                r2(lambda k, st: hid[:, k, st * 128:(st + 1) * 128], [hidB], FH, wo_half, NB, 512, NST, cons)

        JUNK = ARENA - D // 2

        for m in range(2):
            P.dma("sp", lambda e, m=m: e.dma_start(out=h[m], in_=mem_d[m * 128:(m + 1) * 128, :]), writes=[hB[m]])
            norm_sub(h[m], hB[m], 3, m * 128, JUNK)
        kmT = kmT_t[:].rearrange("p (h m) -> p h m", m=MEM)
        for hh in range(4):
            r1(wck_d, hh, MEM, lambda pt, pb, hh=hh: P.add(
                "act", lambda e: e.activation(out=kmT[:, hh, :], in_=pt[:, :MEM], func=AF.Copy),
                reads=[pb], writes=[memB]))
        vm = vm_t[:].rearrange("p (m c) -> p m c", c=512)
        r2(uT_lhs, [uTB], KC, wcv_d, 1, 512, 2, lambda n, st, pt, pb: P.add(
            "act", lambda e: e.activation(out=vm[:, st, :], in_=pt[:], func=AF.Copy), reads=[pb], writes=[memB]))
        P.barrier()

        stg_off = FH * T // 2 + 2 * T
        def phaseA(i):
            t0 = i * T
            for st in range(NST):
                P.dma("sp", lambda e, st=st: e.dma_start(out=h[st], in_=x_d[t0 + st * 128:t0 + (st + 1) * 128, :]),
                      writes=[hB[st]])
                norm_sub(h[st], hB[st], 0, st * 128, JUNK)
            ffn(w1i_d, w1o_d)
            for st in range(NST):
                P.dma("sp", lambda e, st=st: e.dma_start(out=h1_s[t0 + st * 128:t0 + (st + 1) * 128, :], in_=h[st]),
                      reads=[hB[st]])
                norm_sub(h[st], hB[st], 1, st * 128, JUNK)
            for st in range(NST):
                j = i * NST + st
                yv = st_t[:, 0:64]
                yf = st_t[:, 64:128]
                P.add("dve", lambda e, j=j: e.tensor_scalar(out=yv, in0=invf, scalar1=posf_t[:, j:j + 1],
                                                            scalar2=1.0 / (2 * math.pi), op0=ALU.mult, op1=ALU.mult),
                      reads=[consB], writes=[statB])
                P.add("dve", lambda e: e.tensor_copy(out=ri_t[:], in_=yv), writes=[statB])
                P.add("dve", lambda e: e.tensor_copy(out=yf, in_=ri_t[:]), writes=[statB])
                P.add("dve", lambda e: e.tensor_tensor(out=yv, in0=yv, in1=yf, op=ALU.subtract), writes=[statB])
                sa = st_t[:, 128:192]
                sbh = st_t[:, 192:256]
                P.add("act", lambda e: e.activation(out=sa, in_=yv, func=AF.Sin, scale=math.pi), writes=[statB])
                P.add("act", lambda e: e.activation(out=sbh, in_=yv, func=AF.Sin, scale=math.pi / 2), writes=[statB])
                P.add("dve", lambda e: e.tensor_tensor(out=sbh, in0=sbh, in1=sbh, op=ALU.mult), writes=[statB])
                P.add("dve", lambda e: e.tensor_scalar(out=sbh, in0=sbh, scalar1=-2.0, scalar2=1.0, op0=ALU.mult, op1=ALU.add),
                      writes=[statB])
                P.add("dve", lambda e, st=st: e.scalar_tensor_tensor(out=cs[:, 1, st, :], in0=sa, scalar=2.0, in1=sbh,
                                                                     op0=ALU.mult, op1=ALU.mult), writes=[statB, csB])
                P.add("dve", lambda e: e.tensor_tensor(out=sa, in0=sa, in1=sa, op=ALU.mult), writes=[statB])
                P.add("dve", lambda e, st=st: e.tensor_scalar(out=cs[:, 0, st, :], in0=sa, scalar1=-2.0, scalar2=1.0,
                                                              op0=ALU.mult, op1=ALU.add), writes=[statB, csB])
            HB = CB // 128
            stg = [abf(i2 * 256, 256) for i2 in range(4)]
            stgB = [Buf() for _ in range(4)]
            tmp = [af32(1024 + i2 * 256, 256) for i2 in range(4)]
            tmpB = Buf()
            sc = {"n": 0, "x": 0}
            stg2 = [abf(3072 + i2 * 256, 256) for i2 in range(2)]
            stg2B = [Buf(), Buf()]
            hz = [af32(3584 + i2 * 16, 16) for i2 in range(2)]
            hzB = [Buf(), Buf()]

            def qkv_cons(n, st, pt, pb):
                g, r, qd = n // (3 * QB), (n // QB) % 3, n % QB
                si = sc["n"] % 4
                sc["n"] += 1
                so, sB = stg[si][:, :CB], stgB[si]
                if r == 2:
                    P.add("act", lambda e: e.activation(out=so, in_=pt[:, :CB], func=AF.Copy), reads=[pb], writes=[sB])
                else:
                    pv = pt[:, :CB].rearrange("p (h two j) -> p h two j", two=2, j=64)
                    ov = so.rearrange("p (h two j) -> p h two j", two=2, j=64)
                    cosb = cs[:, 0, st, :].rearrange("p (o j) -> p o j", o=1).to_broadcast([128, HB, 64])
                    sinb = cs[:, 1, st, :].rearrange("p (o j) -> p o j", o=1).to_broadcast([128, HB, 64])
                    tv = [tmp[k][:, :HB * 64].rearrange("p (h j) -> p h j", j=64) for k in range(4)]
                    P.add("dve", lambda e: e.tensor_tensor(out=tv[0], in0=pv[:, :, 0, :], in1=cosb, op=ALU.mult),
                          reads=[pb, csB], writes=[tmpB])
                    P.add("dve", lambda e: e.tensor_tensor(out=tv[1], in0=pv[:, :, 1, :], in1=sinb, op=ALU.mult),
                          reads=[pb, csB], writes=[tmpB])
                    P.add("dve", lambda e: e.tensor_tensor(out=tv[2], in0=pv[:, :, 1, :], in1=cosb, op=ALU.mult),
                          reads=[pb, csB], writes=[tmpB])
                    P.add("dve", lambda e: e.tensor_tensor(out=tv[3], in0=pv[:, :, 0, :], in1=sinb, op=ALU.mult),
                          reads=[pb, csB], writes=[tmpB])
                    P.add("dve", lambda e: e.tensor_tensor(out=ov[:, :, 0, :], in0=tv[0], in1=tv[1], op=ALU.subtract),
                          reads=[tmpB], writes=[sB])
                    P.add("dve", lambda e: e.tensor_tensor(out=ov[:, :, 1, :], in0=tv[2], in1=tv[3], op=ALU.add),
                          reads=[tmpB], writes=[sB])
                P.dma("sp", lambda e: e.dma_start(
                    out=qkv_s[g][r][t0 + st * 128:t0 + (st + 1) * 128, qd * CB:(qd + 1) * CB], in_=so), reads=[sB])
                if r > 0 and (g == 2 or i == NT - 1):
                    xi = sc["x"] % 2
                    sc["x"] += 1
                    so2 = stg2[xi][:, :CB]
                    P.add("dve", lambda e: e.tensor_scalar_mul(out=so2, in0=so, scalar1=flag0), reads=[sB, consB],
                          writes=[stg2B[xi]])
                    r0 = XR[(g, r)] + (t0 if g == 2 else 0) + st * 128
                    P.dma("sp", lambda e: e.dma_start(out=xin[r0:r0 + 128, qd * CB:(qd + 1) * CB], in_=so2),
                          reads=[stg2B[xi]])
            r2(uT_lhs, [uTB], KC, wqkv_d, 9 * QB, CB, NST, qkv_cons)
            zst = [af32(2048 + i2 * T, T) for i2 in range(2)]
            zstB = [Buf(), Buf()]
            for c in range(NPC):
                def zcons(pt, pb, c=c):
                    zi = c % 2
                    P.add("act", lambda e: e.activation(out=zst[zi], in_=pt[:], func=AF.Copy), reads=[pb], writes=[zstB[zi]])
                    P.dma("sp", lambda e: e.dma_start(out=zp_s[c, :, t0:t0 + T], in_=zst[zi]), reads=[zstB[zi]])
                    if i == NT - 1:
                        P.add("dve", lambda e: e.tensor_scalar_mul(out=hz[zi], in0=zst[zi][:, T - 16:T], scalar1=flag0),
                              reads=[zstB[zi], consB], writes=[hzB[zi]])
                        P.dma("sp", lambda e: e.dma_start(out=xzin[c], in_=hz[zi]), reads=[hzB[zi]])
                r1(wzp_d, c, T, zcons)
            P.barrier()

        for i in range(NT):
            phaseA(i)
        CR = max(128, (4 * 1024 * 1024) // (AW * 2))
        for r0 in range(0, XROWS, CR):
            r1_ = min(XROWS, r0 + CR)
            P.cc(lambda e, r0=r0, r1_=r1_: e.collective_compute(
                "AllReduce", ALU.add, replica_groups=pairs,
                ins=[xin_h.ap()[r0:r1_].opt()], outs=[xout_h.ap()[r0:r1_].opt()]), writes=[xoutB], first=(r0 == 0))
        P.cc(lambda e: e.collective_compute("AllReduce", ALU.add, replica_groups=pairs,
                                            ins=[xzin_h.ap().opt()], outs=[xzout_h.ap().opt()]), writes=[xzB], first=False)

        sm = 1.0 / math.sqrt(128.0)
        NR = 3
        QW = AW // 2
        off = 0
        qt = []
        for i2 in range(2):
            qt.append(abf(off, QW)); off += QW
        kt = []
        for i2 in range(2):
            kt.append(abf(off, QW)); off += QW
        pT = []
        for i2 in range(3):
            pT.append(abf(off, 256)); off += 256
        assert off <= ARENA, off
        off = 0
        VW = (H * 129 + 1) // 2
        vt = []
        for i2 in range(NR):
            vt.append(abf(off, VW, h_t)[:, :H * 129].rearrange("p (h j) -> p h j", j=129)); off += VW
        qT = []
        for i2 in range(2):
            qT.append(abf(off, QW, h_t).rearrange("p (h j) -> p h j", j=128)); off += QW
        kT = []
        for i2 in range(NR):
            kT.append(abf(off, QW, h_t).rearrange("p (h j) -> p h j", j=128)); off += QW
        ot = []
        for i2 in range(2):
            ot.append(af32(off, H * 129, h_t).rearrange("p (h j) -> p h j", j=129)); off += H * 129
        assert off <= NST * D, off
        qtB, ktB = [Buf(), Buf()], [Buf(), Buf()]
        vtB = [Buf() for _ in range(NR)]
        NG4 = max(1, H // 4)
        qTB = [[Buf() for _ in range(NG4)] for _ in range(2)]
        kTB = [[Buf() for _ in range(NG4)] for _ in range(NR)]
        pTB = [Buf() for _ in range(3)]
        otB = [Buf(), Buf()]
        for i2 in range(NR):
            P.add("dve", lambda e, i2=i2: e.memset(vt[i2][:, :, 128:129], 1.0), writes=[vtB[i2]])
        blk = 0
        pcount = 0
        for g, dil in enumerate(DILS):
            L = S // dil
            qv_own = [qkv_s[g][r].rearrange("(a d) c -> d a c", d=dil) for r in range(3)]
            nprev = S if g == 2 else T
            qv_prev = [None] + [xout[XR[(g, r)]:XR[(g, r)] + nprev].rearrange("(a d) c -> d a c", d=dil) for r in (1, 2)]
            ov_d = o_s[g].rearrange("(a d) c -> d a c", d=dil)
            for res in range(dil):
                nbq = L // 128
                if g == 0:
                    order = [(0, True)] + [(bq, False) for bq in range(1, nbq)] + [(-1, True), (0, False)]
                else:
                    order = [(-1, True)] + [(bq, False) for bq in range(nbq)]
                for (b, kvonly) in order:
                    q2, k3 = blk % 2, blk % NR
                    kprev = (blk - 1) % NR
                    if b < 0:
                        qv = qv_prev
                        rows = slice(nprev // dil - 128, nprev // dil)
                    else:
                        qv = qv_own
                        rows = slice(b * 128, (b + 1) * 128)
                    xr = [xoutB] if b < 0 else []
                    if not kvonly:
                        P.dma("sp", lambda e, q2=q2, res=res, rows=rows, qv=qv: e.dma_start(out=qt[q2], in_=qv[0][res, rows, :]),
                              writes=[qtB[q2]])
                    P.dma("sp", lambda e, q2=q2, res=res, rows=rows, qv=qv: e.dma_start(out=kt[q2], in_=qv[1][res, rows, :]),
                          reads=xr, writes=[ktB[q2]])
                    P.dma("sp", lambda e, k3=k3, res=res, rows=rows, qv=qv: e.dma_start(
                        out=vt[k3][:, :, 0:128], in_=qv[2][res, rows, :].rearrange("p (h j) -> p h j", j=128)),
                        reads=xr, writes=[vtB[k3]])
                    tl = [(kt[q2], ktB[q2], kT[k3], kTB[k3])]
                    if not kvonly:
                        tl.insert(0, (qt[q2], qtB[q2], qT[q2], qTB[q2]))
                    for (srct, srcB, dstT, dstB) in tl:
                        for h0 in range(0, H, 4):
                            pt, pb = pbank()

                            def mm(e, pt=pt, h0=h0, srct=srct):
                                for j in range(4):
                                    ins = e.matmul(pt[:, j * 128:(j + 1) * 128], lhsT=srct[:, (h0 + j) * 128:(h0 + j + 1) * 128],
                                                   rhs=identb_t[:], start=True, stop=True)
                                return ins
                            P.add("pe", mm, reads=[srcB, consB], writes=[pb])
                            if (h0 // 4) % 2 == 0:
                                P.add("act", lambda e, pt=pt, h0=h0, dstT=dstT: e.activation(
                                    out=dstT[:, h0:h0 + 4, :], in_=pt[:].rearrange("p (a b) -> p a b", b=128), func=AF.Copy),
                                    reads=[pb], writes=[dstB[h0 // 4]])
                            else:
                                P.add("dve", lambda e, pt=pt, h0=h0, dstT=dstT: e.tensor_copy(
                                    out=dstT[:, h0:h0 + 4, :], in_=pt[:].rearrange("p (a b) -> p a b", b=128)),
                                    reads=[pb], writes=[dstB[h0 // 4]])
                    if kvonly:
                        blk += 1
                        continue
                    o2 = blk % 2
                    nblk = 2
                    cm = cmask1 if b == 0 else cmask
                    def stage1(h0):
                        nonlocal pcount
                        pt, pb = pbank()
                        pi = pcount % 3
                        pcount += 1

                        def smm(e, pt=pt, h0=h0, q2=q2, k3=k3, kprev=kprev):
                            for j in range(2):
                                ins = e.matmul(pt[:, j * 256:j * 256 + 128], lhsT=kT[k3][:, h0 + j, :], rhs=qT[q2][:, h0 + j, :],
                                               start=True, stop=True)
                                ins = e.matmul(pt[:, j * 256 + 128:j * 256 + 256], lhsT=kT[kprev][:, h0 + j, :],
                                               rhs=qT[q2][:, h0 + j, :], start=True, stop=True)
                            return ins
                        P.add("pe", smm, reads=[kTB[k3][h0 // 4], qTB[q2][h0 // 4], kTB[kprev][h0 // 4]], writes=[pb])
                        P.add("act", lambda e, pt=pt, pi=pi: e.activation(out=pT[pi], in_=pt[:], func=AF.Exp, scale=sm),
                              reads=[pb], writes=[pTB[pi]])
                        P.add("dve", lambda e, pi=pi, cm=cm: e.tensor_tensor(out=pT[pi], in0=pT[pi], in1=cm, op=ALU.mult),
                              reads=[consB], writes=[pTB[pi]])
                        return pi

                    def stage2(h0, pi):
                        po, pob = pbank()

                        def pvm(e, po=po, pi=pi, h0=h0, k3=k3, kprev=kprev):
                            for j in range(2):
                                ins = e.matmul(po[:, j * 129:(j + 1) * 129], lhsT=pT[pi][:, j * 256:j * 256 + 128],
                                               rhs=vt[k3][:, h0 + j, :], start=True, stop=False)
                                ins = e.matmul(po[:, j * 129:(j + 1) * 129], lhsT=pT[pi][:, j * 256 + 128:j * 256 + 256],
                                               rhs=vt[kprev][:, h0 + j, :], start=False, stop=True)
                            return ins
                        P.add("pe", pvm, reads=[pTB[pi], vtB[k3], vtB[kprev]], writes=[pob])
                        P.add("dve", lambda e, po=po, h0=h0, o2=o2: e.tensor_copy(
                            out=ot[o2][:, h0:h0 + 2, :], in_=po[:, :258].rearrange("p (h j) -> p h j", j=129)),
                            reads=[pob], writes=[otB[o2]])

                    hps = list(range(0, H, 2))
                    pis = {}
                    SK = 2
                    for k_ in range(len(hps) + SK):
                        if k_ < len(hps):
                            pis[k_] = stage1(hps[k_])
                        if k_ >= SK:
                            stage2(hps[k_ - SK], pis[k_ - SK])
                    P.dma("sp", lambda e, o2=o2, res=res, rows=rows, ov_d=ov_d: e.dma_start(
                        out=ov_d[res, rows, :], in_=ot[o2].rearrange("p h j -> p (h j)")), reads=[otB[o2]])
                    blk += 1
        P.barrier()

        opT_off = 0
        opT = abf(opT_off, NPC * T // 2).rearrange("p (c t) -> p c t", t=T)
        opTB = Buf()
        s1 = NPC * T // 2
        ZL = 16 + T
        def phaseD(i):
            t0 = i * T
            for st in range(NST):
                P.dma("sp", lambda e, st=st: e.dma_start(out=h[st], in_=h1_s[t0 + st * 128:t0 + (st + 1) * 128, :]),
                      writes=[hB[st]])
            zb = [af32(s1 + k * PGC * ZL, PGC * ZL).rearrange("p (c t) -> p c t", t=ZL) for k in range(3)]
            zbB = [Buf() for _ in range(3)]
            yb = abf(s1 + 3 * PGC * ZL, PGC * T // 2).rearrange("p (c t) -> p c t", t=T)
            ybB = Buf()
            fx = af32(s1 + 3 * PGC * ZL + PGC * T // 2, PGC * 16).rearrange("p (c t) -> p c t", t=16)
            for g, w in enumerate(POOL_W):
                A, Bz, Cz = zb
                if i == 0:
                    P.dma("sp", lambda e, g=g: e.dma_start(
                        out=A[:, :, 0:16], in_=xzout[g * PGC:(g + 1) * PGC].rearrange("c p t -> p c t")),
                        reads=[xzB], writes=[zbB[0]])
                    P.add("dve", lambda e: e.tensor_scalar_mul(out=A[:, :, 0:16], in0=A[:, :, 0:16], scalar1=flag1),
                          reads=[consB], writes=[zbB[0]])
                    P.dma("sp", lambda e, g=g: e.dma_start(
                        out=A[:, :, 16:], in_=zp_s[g * PGC:(g + 1) * PGC, :, 0:T].rearrange("c p t -> p c t")),
                        writes=[zbB[0]])
                else:
                    P.dma("sp", lambda e, g=g: e.dma_start(
                        out=A[:], in_=zp_s[g * PGC:(g + 1) * PGC, :, t0 - 16:t0 + T].rearrange("c p t -> p c t")),
                        writes=[zbB[0]])
                chain = [(Bz, A, 1, 1), (Cz, Bz, 3, 2), (Bz, Cz, 7, 4), (Cz, Bz, 15, 8)]
                nsteps = {2: 1, 4: 2, 8: 3, 16: 4}[w]
                bufB = {id(A): zbB[0], id(Bz): zbB[1], id(Cz): zbB[2]}
                for (dst, src, lo, sh) in chain[:nsteps]:
                    P.add("dve", lambda e, dst=dst, src=src, lo=lo, sh=sh: e.tensor_tensor(
                        out=dst[:, :, lo:], in0=src[:, :, lo:], in1=src[:, :, lo - sh:ZL - sh], op=ALU.add),
                        reads=[bufB[id(src)]], writes=[bufB[id(dst)]])
                sw = chain[nsteps - 1][0]
                P.add("dve", lambda e, sw=sw, w=w: e.scalar_tensor_tensor(
                    out=yb[:], in0=sw[:, :, 16:], scalar=1.0 / w, in1=A[:, :, 16:], op0=ALU.mult, op1=ALU.subtract),
                    reads=[bufB[id(sw)], zbB[0]], writes=[ybB])
                if i == 0:
                    icb = invc[:, g * 16:(g + 1) * 16].rearrange("p (o t) -> p o t", o=1).to_broadcast([128, PGC, 16])
                    P.add("dve", lambda e, sw=sw, icb=icb: e.tensor_tensor(out=fx, in0=sw[:, :, 16:32], in1=icb, op=ALU.mult),
                          reads=[bufB[id(sw)], consB], writes=[ybB])
                    P.add("dve", lambda e: e.tensor_tensor(out=yb[:, :, 0:16], in0=fx, in1=A[:, :, 16:32], op=ALU.subtract),
                          reads=[zbB[0]], writes=[ybB])
                wt, wb = wslot()
                P.dma("pool", lambda e, wt=wt, g=g: e.dma_start(out=wt[:, :PGC * cfg.PG], in_=wpl_d[g]), writes=[wb])
                wv = wt[:, :PGC * cfg.PG].rearrange("p (c j) -> p c j", j=cfg.PG)
                for dc in range(PGC):
                    pt, pb = pbank()

                    def mm(e, pt=pt, wv=wv, dc=dc):
                        for cc in range(PGC):
                            ins = e.matmul(pt[:], lhsT=wv[:, cc, dc * 128:(dc + 1) * 128], rhs=yb[:, cc, :],
                                           start=(cc == 0), stop=(cc == PGC - 1))
                        return ins
                    P.add("pe", mm, reads=[wb, ybB], writes=[pb])
                    oc = g * PGC + dc
                    P.add("act", lambda e, pt=pt, oc=oc: e.activation(out=opT[:, oc, :], in_=pt[:], func=AF.Copy,
                                                                       scale=psc_t[:, oc:oc + 1]),
                          reads=[pb, consB], writes=[opTB])
            P.barrier()
            oT = uT_t[:, 0:H * T].rearrange("p (c t) -> p c t", t=T)
            oTB = uTB
            ob = uT_t[:, H * T:H * T + H * 128].rearrange("p (h j) -> p h j", j=128)
            s3 = s1
            acc = af32(s3, H * 129).rearrange("p (h j) -> p h j", j=129)
            tm2 = af32(s3 + H * 129, H * 129).rearrange("p (h j) -> p h j", j=129)
            rz = af32(s3 + 2 * H * 129, H)
            assert s3 + 2 * H * 129 + H <= ARENA, "arena"
            accB, tm2B, obB = Buf(), Buf(), Buf()
            for st in range(NST):
                rows = slice(t0 + st * 128, t0 + (st + 1) * 128)
                P.dma("sp", lambda e, rows=rows: e.dma_start(out=acc.rearrange("p h j -> p (h j)"), in_=o_s[0][rows, :]),
                      writes=[accB])
                for g in (1, 2):
                    P.dma("sp", lambda e, rows=rows, g=g: e.dma_start(out=tm2.rearrange("p h j -> p (h j)"), in_=o_s[g][rows, :]),
                          writes=[tm2B])
                    P.add("dve", lambda e: e.tensor_tensor(out=acc, in0=acc, in1=tm2, op=ALU.add),
                          reads=[tm2B], writes=[accB])
                P.add("dve", lambda e: e.reciprocal(out=rz, in_=acc[:, :, 128:129].rearrange("p h o -> p (h o)")),
                      writes=[accB])
                P.add("dve", lambda e: e.tensor_tensor(
                    out=ob, in0=acc[:, :, 0:128], in1=rz.rearrange("p (h o) -> p h o", o=1).to_broadcast([128, H, 128]),
                    op=ALU.mult), reads=[accB], writes=[obB])
                for h0 in range(0, H, 4):
                    pt, pb = pbank()

                    def mm(e, pt=pt, h0=h0):
                        for j in range(4):
                            ins = e.matmul(pt[:, j * 128:(j + 1) * 128], lhsT=ob[:, h0 + j, :], rhs=identb_t[:],
                                           start=True, stop=True)
                        return ins
                    P.add("pe", mm, reads=[obB, consB], writes=[pb])
                    P.add("act", lambda e, pt=pt, h0=h0, st=st: e.activation(
                        out=oT[:, h0:h0 + 4, st * 128:(st + 1) * 128], in_=pt[:].rearrange("p (a b) -> p a b", b=128),
                        func=AF.Copy), reads=[pb], writes=[oTB])

            def mo_lhs(k, st):
                if k < H:
                    return oT[:, k, st * 128:(st + 1) * 128]
                return opT[:, k - H, st * 128:(st + 1) * 128]

            def addh(n, st, pt, pb):
                P.add("dve", lambda e: e.tensor_tensor(out=h[st][:, n * 512:(n + 1) * 512], in0=pt[:],
                                                       in1=h[st][:, n * 512:(n + 1) * 512], op=ALU.add),
                      reads=[pb], writes=[hB[st]])
            r2(mo_lhs, [oTB, opTB], KC, wmo_d, NB, 512, NST, addh)
            for st in range(NST):
                norm_sub(h[st], hB[st], 2, st * 128, JUNK)
            qcT = abf(0, 4 * T // 2).rearrange("p (h t) -> p h t", t=T)
            qcB = Buf()
            ocT = abf(4 * T // 2, 4 * T // 2).rearrange("p (h t) -> p h t", t=T)
            ocB = Buf()
            pc = [abf(4 * T + k * T // 2, T // 2) for k in range(2)]
            pcB = [Buf(), Buf()]
            rzc = af32(4 * T + T, T)
            rzcB = Buf()
            for hh in range(4):
                r1(wcq_d, hh, T, lambda pt, pb, hh=hh: P.add(
                    "act", lambda e: e.activation(out=qcT[:, hh, :], in_=pt[:], func=AF.Copy), reads=[pb], writes=[qcB]))
            for hh in range(4):
                for m in range(2):
                    pt, pb = pbank()
                    P.add("pe", lambda e, pt=pt, hh=hh, m=m: e.matmul(
                        pt[:], lhsT=kmT[:, hh, m * 128:(m + 1) * 128], rhs=qcT[:, hh, :], start=True, stop=True),
                        reads=[memB, qcB], writes=[pb])
                    P.add("act", lambda e, pt=pt, m=m: e.activation(out=pc[m], in_=pt[:], func=AF.Exp, scale=sm),
                          reads=[pb], writes=[pcB[m]])
                po, pob = pbank()
                pz, pzb = pbank()

                def om(e, po=po, hh=hh):
                    for m in range(2):
                        ins = e.matmul(po[:], lhsT=vm[:, m, hh * 128:(hh + 1) * 128], rhs=pc[m], start=(m == 0), stop=(m == 1))
                    return ins

                def zm(e, pz=pz):
                    for m in range(2):
                        ins = e.matmul(pz[:], lhsT=onesb_t[:], rhs=pc[m], start=(m == 0), stop=(m == 1))
                    return ins
                P.add("pe", om, reads=[memB] + pcB, writes=[pob])
                P.add("pe", zm, reads=[consB] + pcB, writes=[pzb])
                P.add("dve", lambda e, pz=pz: e.reciprocal(out=rzc, in_=pz[:]), reads=[pzb], writes=[rzcB])
                P.add("dve", lambda e, po=po, hh=hh: e.tensor_tensor(out=ocT[:, hh, :], in0=po[:], in1=rzc, op=ALU.mult),
                      reads=[pob, rzcB], writes=[ocB])
            r2(lambda k, st: ocT[:, k, st * 128:(st + 1) * 128], [ocB], 4, wco_d, NB, 512, NST, addh)
            for st in range(NST):
                norm_sub(h[st], hB[st], 4, st * 128, JUNK)
            ffn(w2i_d, w2o_d)
            gfin = uT_t[:, KC * T // 2:KC * T].bitcast(F32)
            P.dma("sp", lambda e: e.dma_start(out=gfin, in_=gfin_d.partition_broadcast(128)), writes=[uTB])
            fB = [Buf() for _ in range(NST)]
            for st in range(NST):
                junk = abf(JUNK, D // 2)
                ssq = fin_t[:, st:st + 1]
                P.add("dve", lambda e, ssq=ssq: e.memset(ssq, 0.0), writes=[fB[st]])
                P.add("act", lambda e, st=st, ssq=ssq: e.activation(out=junk, in_=h[st], func=AF.Square, accum_out=ssq),
                      reads=[hB[st]], writes=[fB[st], arB])
                P.add("act", lambda e, ssq=ssq: e.activation(out=ssq, in_=ssq, func=AF.Sqrt, bias=eps_t[:], scale=1.0 / D),
                      reads=[consB], writes=[fB[st]])
                P.add("dve", lambda e, ssq=ssq: e.reciprocal(out=ssq, in_=ssq), writes=[fB[st]])
                P.add("dve", lambda e, st=st, ssq=ssq: e.scalar_tensor_tensor(
                    out=h[st], in0=h[st], scalar=ssq, in1=gfin, op0=ALU.mult, op1=ALU.mult),
                    reads=[fB[st], uTB], writes=[hB[st]])
                P.dma("sp", lambda e, st=st: e.dma_start(out=y_d[t0 + st * 128:t0 + (st + 1) * 128, :], in_=h[st]),
                      reads=[hB[st]], bar=False)
            P.barrier()

        for i in range(NT):
            phaseD(i)
        P.barrier(full=True)
        P.emit(nc, es)
    return nc


def _lhsT_layout(w, KC):
    K, n = w.shape
    return np.ascontiguousarray(w.reshape(KC, 128, n // 128, 128).transpose(2, 1, 0, 3)).reshape(n // 128, 128, KC * 128)


def _rhs_layout(w, cb):
    K, n = w.shape
    return np.ascontiguousarray(w.reshape(K // 128, 128, n // cb, cb).transpose(2, 0, 1, 3))


def prepare(cfg, x, mem, positions, g_ffn1, w_ffn1_in, w_ffn1_out, g_mix, w_mix_in, w_pool, pool_scale, w_mix_out,
            g_cross, g_mem, w_cross_q, w_cross_kv, w_cross_o, g_ffn2, w_ffn2_in, w_ffn2_out, g_final):
    import ml_dtypes
    D, S, KC, PGC = cfg.D, cfg.S, cfg.KC, cfg.PGC
    f = np.float32
    shared = {}
    gs = np.stack([np.asarray(g, f).reshape(D) for g in (g_ffn1[0], g_mix[0], g_cross[0], g_mem[0], g_ffn2[0])])
    shared["gfm"] = np.ascontiguousarray(gs.reshape(5, KC, 128).transpose(2, 0, 1)).reshape(128, 5 * KC)
    shared["gfin"] = np.asarray(g_final, f).reshape(1, D)
    shared["psc"] = np.ascontiguousarray(np.asarray(pool_scale[0], f).reshape(4 * PGC, 128).T)
    kk = np.arange(128)[:, None]
    qq = np.arange(128)[None, :]
    cur = (kk <= qq).astype(f)
    prev = (kk >= qq).astype(f)
    zero = np.zeros_like(prev)
    cm_all = np.concatenate([cur, prev, cur, prev], axis=1)
    cmasks = [np.concatenate([cm_all, np.concatenate([cur, prev * par, cur, prev * par], axis=1)], axis=1).astype(ml_dtypes.bfloat16)
              for par in (0.0, 1.0)]
    inv = (1.0 / (ROPE_THETA ** (np.arange(0, 128, 2, dtype=f) / f(128)))).astype(f)
    cf32s = []
    for par in (0, 1):
        if par == 0:
            invc = np.stack([1.0 / np.minimum(np.arange(16) + 1, w) for w in POOL_W]).astype(f).reshape(64)
        else:
            invc = np.stack([np.full(16, 1.0 / w) for w in POOL_W]).astype(f).reshape(64)
        flags = np.array([1.0 - par, float(par)], dtype=f)
        cf32s.append(np.concatenate([np.eye(128, dtype=f), np.tile(inv[None], (128, 1)), np.tile(invc[None], (128, 1)),
                                     np.tile(flags[None], (128, 1))], axis=1))
    shared["w1i"] = _lhsT_layout(np.asarray(w_ffn1_in[0], f), KC)
    shared["w1o"] = _rhs_layout(np.asarray(w_ffn1_out[0], f), 512)
    shared["w2i"] = _lhsT_layout(np.asarray(w_ffn2_in[0], f), KC)
    shared["w2o"] = _rhs_layout(np.asarray(w_ffn2_out[0], f), 512)
    wmi = np.asarray(w_mix_in[0], f)
    shared["wqkv"] = _rhs_layout(wmi[:, :cfg.ATTN_IN], cfg.CB)
    shared["wzp"] = _lhsT_layout(wmi[:, cfg.ATTN_IN:], KC)
    wp = np.asarray(w_pool[0], f)
    shared["wpl"] = np.ascontiguousarray(wp.reshape(4, PGC, 128, cfg.PG).transpose(0, 2, 1, 3)).reshape(4, 128, PGC * cfg.PG)
    shared["wmo"] = _rhs_layout(np.asarray(w_mix_out[0], f), 512)
    shared["wcq"] = _lhsT_layout(np.asarray(w_cross_q[0], f), KC)
    wkv = np.asarray(w_cross_kv[0], f)
    shared["wck"] = _lhsT_layout(wkv[:, :512], KC)
    shared["wcv"] = _rhs_layout(wkv[:, 512:], 512)
    shared["wco"] = _rhs_layout(np.asarray(w_cross_o[0], f), 512)
    in_maps = []
    for b in range(cfg.B):
        for par in (0, 1):
            m = dict(shared)
            sl = slice(par * S, (par + 1) * S)
            m["x"] = np.ascontiguousarray(np.asarray(x[b], f)[sl])
            m["mem"] = np.ascontiguousarray(np.asarray(mem[b], f))
            m["pos"] = np.ascontiguousarray(np.asarray(positions[b], np.int32)[sl].reshape(S // 128, 128).T)
            m["cmask"] = cmasks[par]
            m["cf32"] = cf32s[par]
            in_maps.append(m)
    return in_maps


def kernel(**inputs):
    x = inputs["x"]
    cfg = Cfg(D=x.shape[2], S=x.shape[1], B=x.shape[0])
    nc = build(cfg)
    in_maps = prepare(cfg, **inputs)
    res = run_bass_kernel_spmd(nc, in_maps, core_ids=list(range(2 * cfg.B)))
    ys = [np.asarray(r["y"], np.float32) for r in res.results]
    return np.stack([np.concatenate([ys[2 * b], ys[2 * b + 1]], axis=0) for b in range(cfg.B)], axis=0)
```

```python
import math
from contextlib import ExitStack

import numpy as np
import concourse.bass as bass
import concourse.mybir as mybir
from concourse.bass_utils import run_bass_kernel_spmd

F32 = mybir.dt.float32
BF16 = mybir.dt.bfloat16
I32 = mybir.dt.int32
AF = mybir.ActivationFunctionType
ALU = mybir.AluOpType

EPS = 1e-6
ROPE_THETA = 10000.0
DILS = (1, 4, 16)
POOL_W = (2, 4, 8, 16)
MEM = 256
T = 512
NST = 4


class Cfg:
    def __init__(self, D=4096, S=4096, B=4):
        self.D, self.SF, self.B = D, S, B
        self.S = S // 2
        S = self.S
        self.KC = D // 128
        self.AW = D // 2
        self.H = self.AW // 128
        self.PW = D - self.AW
        self.PG = self.PW // 4
        self.PGC = self.PG // 128
        self.F = ((8 * D // 3 + 255) // 256) * 256
        self.FC = self.F // 128
        self.FH = self.FC // 2
        self.NB = D // 512
        self.NT = S // T
        self.CB = min(512, self.AW)
        self.QB = self.AW // self.CB
        self.ATTN_IN = 9 * self.AW


class Buf:
    __slots__ = ("w", "rs")

    def __init__(self):
        self.w = None
        self.rs = []


class Op:
    __slots__ = ("eng", "fn", "deps", "sig", "kind", "sem", "val", "prevslot")

    def __init__(self, eng, fn, kind):
        self.eng, self.fn, self.kind = eng, fn, kind
        self.deps = []
        self.sig = False
        self.sem = None
        self.val = 0
        self.prevslot = None


ENGS = ("pe", "act", "dve", "pool", "sp")
KQ = 8


class Prog:
    def __init__(self):
        self.ops = {e: [] for e in ENGS}
        self.lastc = {e: None for e in ENGS}
        self.dmas = {e: [] for e in ENGS}
        self.ccs = []
        self.nobar = set()

    def add(self, eng, fn, reads=(), writes=(), kind="c"):
        op = Op(eng, fn, kind)
        deps = []
        for b in reads:
            if b.w is not None:
                deps.append(b.w)
        for b in writes:
            if b.w is not None:
                deps.append(b.w)
            deps.extend(b.rs)
        for b in reads:
            b.rs.append(op)
        for b in writes:
            b.w = op
            b.rs = []
        seen = set()
        for d in deps:
            if id(d) in seen or d is op:
                continue
            seen.add(id(d))
            if d.eng == "pe" and eng == "pe" and d.kind == "c" and kind == "c":
                continue
            op.deps.append(d)
            d.sig = True
        self.ops[eng].append(op)
        if kind == "c":
            self.lastc[eng] = op
        elif kind == "d":
            self.dmas[eng].append(op)
        return op

    def dma(self, eng, fn, reads=(), writes=(), bar=True):
        op = self.add(eng, fn, reads, writes, kind="d")
        if not bar:
            self.nobar.add(id(op))
        return op

    def cc(self, fn, writes=(), first=True):
        if first:
            self.barrier(full=True)
        op = Op("pool", fn, "cc")
        if self.ccs:
            op.deps.append(self.ccs[-1])
        for b in writes:
            b.w = op
            b.rs = []
        self.ops["pool"].append(op)
        self.ccs.append(op)
        return op

    def barrier(self, full=False):
        deps = []
        for e in ENGS:
            if self.lastc[e] is not None:
                deps.append(self.lastc[e])
            deps.extend(d for d in self.dmas[e][-KQ:] if full or id(d) not in self.nobar)
        deps.extend(self.ccs)
        for e in ENGS:
            op = Op(e, None, "f")
            for d in deps:
                op.deps.append(d)
                d.sig = True
            self.ops[e].append(op)

    def emit(self, nc, es):
        semh = {}
        for e in ENGS:
            semh[("c", e)] = es.enter_context(nc.semaphore(f"c_{e}"))
            if e in ("sp", "pool"):
                for k in range(KQ):
                    semh[("d", e, k)] = es.enter_context(nc.semaphore(f"d_{e}{k}"))
        for k, op in enumerate(self.ccs):
            op.sem = ("cc", k)
            op.val = 1
            semh[op.sem] = es.enter_context(nc.semaphore(f"cc_{k}"))
        for e in ENGS:
            cnt = 0
            dl = []
            for op in self.ops[e]:
                if op.kind == "c" and op.sig:
                    cnt += 1
                    op.sem = ("c", e)
                    op.val = cnt
                elif op.kind == "d":
                    n = len(dl)
                    op.sem = ("d", e, n % KQ)
                    op.val = 16 * (n // KQ + 1)
                    if n >= KQ:
                        op.prevslot = dl[n - KQ]
                    dl.append(op)
        block = es.enter_context(nc.Block())

        def run(e, eng):
            waited = {}
            for op in self.ops[e]:
                deps = op.deps if op.prevslot is None else op.deps + [op.prevslot]
                if len(deps) > 1:
                    deps = sorted(deps, key=lambda d: -d.val)
                for d in deps:
                    if waited.get(d.sem, 0) >= d.val:
                        continue
                    eng.wait_ge(semh[d.sem], d.val)
                    waited[d.sem] = d.val
                if op.kind == "f":
                    continue
                ins = op.fn(eng)
                if op.kind == "d":
                    ins.then_inc(semh[op.sem], 16)
                elif op.kind == "cc":
                    ins.then_inc(semh[op.sem])
                elif op.sig:
                    ins.then_inc(semh[op.sem], 1)

        block.tensor(lambda eng: run("pe", eng))
        block.scalar(lambda eng: run("act", eng))
        block.vector(lambda eng: run("dve", eng))
        block.gpsimd(lambda eng: run("pool", eng))
        block.sync(lambda eng: run("sp", eng))


def build(cfg):
    D, S, KC, H, FC, FH, NB, NT = cfg.D, cfg.S, cfg.KC, cfg.H, cfg.FC, cfg.FH, cfg.NB, cfg.NT
    AW, CB, QB, PGC = cfg.AW, cfg.CB, cfg.QB, cfg.PGC
    NPC = 4 * PGC
    nc = bass.Bass("TRN2", target_bir_lowering=False)
    P = Prog()

    def din(name, shape, dt=F32):
        return nc.dram_tensor(name, list(shape), dt, kind="ExternalInput").ap()

    x_d = din("x", [S, D])
    mem_d = din("mem", [MEM, D])
    pos_d = din("pos", [128, S // 128], I32)
    gfm_d = din("gfm", [128, 5 * KC])
    gfin_d = din("gfin", [1, D])
    psc_d = din("psc", [128, NPC])
    cmask_d = din("cmask", [128, 1024], BF16)
    cf32_d = din("cf32", [128, 128 + 64 + 64 + 2])
    w1i_d = din("w1i", [2 * FC, 128, KC * 128])
    w1o_d = din("w1o", [NB, FC, 128, 512])
    w2i_d = din("w2i", [2 * FC, 128, KC * 128])
    w2o_d = din("w2o", [NB, FC, 128, 512])
    wqkv_d = din("wqkv", [9 * QB, KC, 128, CB])
    wzp_d = din("wzp", [NPC, 128, KC * 128])
    wpl_d = din("wpl", [4, 128, PGC * cfg.PG])
    wmo_d = din("wmo", [NB, KC, 128, 512])
    wcq_d = din("wcq", [4, 128, KC * 128])
    wck_d = din("wck", [4, 128, KC * 128])
    wcv_d = din("wcv", [1, KC, 128, 512])
    wco_d = din("wco", [NB, 4, 128, 512])
    y_d = nc.dram_tensor("y", [S, D], F32, kind="ExternalOutput").ap()

    h1_s = nc.dram_tensor("h1_s", [S, D], F32).ap()
    qkv_s = [[nc.dram_tensor(f"qkv_s{g}{r}", [S, AW], BF16).ap() for r in range(3)] for g in range(3)]
    zp_s = nc.dram_tensor("zp_s", [NPC, 128, S], F32).ap()
    o_s = [nc.dram_tensor(f"o_s{g}", [S, H * 129], F32).ap() for g in range(3)]
    XROWS = 2 * S + 4 * T
    XR = {(2, 1): 0, (2, 2): S, (1, 1): 2 * S, (1, 2): 2 * S + T, (0, 1): 2 * S + 2 * T, (0, 2): 2 * S + 3 * T}
    xin_h = nc.dram_tensor("xin", [XROWS, AW], BF16)
    xout_h = nc.dram_tensor("xout", [XROWS, AW], BF16)
    xzin_h = nc.dram_tensor("xzin", [NPC * 128, 16], F32)
    xzout_h = nc.dram_tensor("xzout", [NPC * 128, 16], F32)
    xin, xout = xin_h.ap(), xout_h.ap()
    xzin = xzin_h.ap().rearrange("(c p) t -> c p t", p=128)
    xzout = xzout_h.ap().rearrange("(c p) t -> c p t", p=128)
    pairs = [[2 * k, 2 * k + 1] for k in range(cfg.B)]

    es = ExitStack()
    with es:
        def sb(name, shape, dt):
            return es.enter_context(nc.sbuf_tensor("s_" + name, list(shape), dt))

        h_t = sb("h", [128, NST * D], F32)
        uT_t = sb("uT", [128, KC * T], BF16)
        WSL = 3
        wr_t = [sb(f"wr{i}", [128, 4096], BF16) for i in range(WSL)]
        ARENA = 13312
        ar_t = sb("arena", [128, ARENA], F32)
        cons_t = sb("consf", [128, 258], F32)
        gfm_t = sb("gfm", [128, 5 * KC], F32)
        psc_t = sb("psc", [128, NPC], F32)
        cmask_t = sb("cmask", [128, 1024], BF16)
        identb_t = sb("identb", [128, 128], BF16)
        onesb_t = sb("onesb", [128, 128], BF16)
        posi_t = sb("posi", [128, S // 128], I32)
        posf_t = sb("posf", [128, S // 128], F32)
        cs_t = sb("cs", [128, 2 * NST * 64], F32)
        st_t = sb("stat", [128, 256], F32)
        ri_t = sb("rint", [128, 64], I32)
        diag_t = sb("diag", [128, 2 * 128], F32)
        kmT_t = sb("kmT", [128, 4 * MEM], BF16)
        vm_t = sb("vm", [128, 2 * 512], BF16)
        negpi_t = sb("negpi", [128, 1], F32)
        eps_t = sb("epst", [128, 1], F32)
        fin_t = sb("fint", [128, 4], F32)
        ps_t = [es.enter_context(nc.psum_tensor(f"ps{i}", [128, 512], F32)) for i in range(8)]

        h = [h_t[:, st * D:(st + 1) * D] for st in range(NST)]
        hB = [Buf() for _ in range(NST)]
        uT = uT_t[:].rearrange("p (k t) -> p k t", t=T)
        uTB = Buf()
        wrB = [Buf() for _ in range(WSL)]
        psB = [Buf() for _ in range(8)]
        consB, statB, diagB, csB, memB, arB = Buf(), Buf(), [Buf(), Buf()], Buf(), Buf(), Buf()
        xoutB, xzB = Buf(), Buf()
        identf = cons_t[:, 0:128]
        invf = cons_t[:, 128:192]
        invc = cons_t[:, 192:256]
        gfm = gfm_t[:].rearrange("p (n k) -> p n k", k=KC)
        cmask = cmask_t[:, 0:512]
        cmask1 = cmask_t[:, 512:1024]
        flag0 = cons_t[:, 256:257]
        flag1 = cons_t[:, 257:258]
        cs = cs_t[:].rearrange("p (c s j) -> p c s j", c=2, s=NST)

        state = {"w": 0, "ps": 0, "diag": 0}

        def wslot():
            i = state["w"] % WSL
            state["w"] += 1
            return wr_t[i], wrB[i]

        def pbank():
            i = state["ps"] % 8
            state["ps"] += 1
            return ps_t[i], psB[i]

        def af32(off, n, t=None):
            t = ar_t if t is None else t
            return t[:, off:off + n]

        def abf(off, n, t=None):
            t = ar_t if t is None else t
            return t[:, off:off + n].bitcast(BF16)

        for (dst, src) in ((cons_t[:], cf32_d), (gfm_t[:], gfm_d), (psc_t[:], psc_d),
                           (cmask_t[:], cmask_d), (posi_t[:], pos_d)):
            P.dma("sp", lambda e, d=dst, s=src: e.dma_start(out=d, in_=s), writes=[consB])
        P.add("dve", lambda e: e.tensor_copy(out=posf_t[:], in_=posi_t[:]), reads=[consB], writes=[consB])
        P.add("dve", lambda e: e.tensor_copy(out=identb_t[:], in_=identf), reads=[consB], writes=[consB])
        P.add("dve", lambda e: e.memset(onesb_t[:], 1.0), writes=[consB])
        P.add("dve", lambda e: e.memset(negpi_t[:], -math.pi), writes=[consB])
        P.add("dve", lambda e: e.memset(eps_t[:], EPS), writes=[consB])

        def norm_sub(hap, hb, gi, col0, junk_off):
            junk = abf(junk_off, D // 2)
            ssq = st_t[:, 0:1]
            rstd = st_t[:, 1:2]
            di = state["diag"] % 2
            state["diag"] += 1
            dg = diag_t[:, di * 128:(di + 1) * 128]
            P.add("dve", lambda e: e.memset(ssq, 0.0), writes=[statB])
            P.add("act", lambda e: e.activation(out=junk, in_=hap, func=AF.Square, accum_out=ssq),
                  reads=[hb], writes=[statB, arB])
            P.add("act", lambda e: e.activation(out=rstd, in_=ssq, func=AF.Sqrt, bias=eps_t[:], scale=1.0 / D),
                  reads=[consB], writes=[statB])
            P.add("dve", lambda e: e.reciprocal(out=rstd, in_=rstd), writes=[statB])
            P.add("dve", lambda e: e.tensor_scalar_mul(out=dg, in0=identf, scalar1=rstd),
                  reads=[statB, consB], writes=[diagB[di]])
            for k0 in range(0, KC, 4):
                pt, pb = pbank()

                def mm(e, k0=k0, pt=pt):
                    for j in range(4):
                        ins = e.matmul(pt[:, j * 128:(j + 1) * 128], lhsT=hap[:, (k0 + j) * 128:(k0 + j + 1) * 128],
                                       rhs=dg, start=True, stop=True)
                    return ins
                P.add("pe", mm, reads=[hb, diagB[di]], writes=[pb])
                if gi is None:
                    P.add("dve", lambda e, k0=k0, pt=pt: e.tensor_copy(
                        out=uT[:, k0:k0 + 4, col0:col0 + 128], in_=pt[:].rearrange("p (a b) -> p a b", b=128)),
                        reads=[pb], writes=[uTB])
                else:
                    P.add("dve", lambda e, k0=k0, pt=pt: e.tensor_tensor(
                        out=uT[:, k0:k0 + 4, col0:col0 + 128], in0=pt[:].rearrange("p (a b) -> p a b", b=128),
                        in1=gfm[:, gi, k0:k0 + 4].to_broadcast([128, 4, 128]), op=ALU.mult),
                        reads=[pb, consB], writes=[uTB])
            return rstd

        def r1(wd, c, ncols, consume, src=None, srcB=None, nk=None):
            src = uT if src is None else src
            srcB = uTB if srcB is None else srcB
            nk = KC if nk is None else nk
            wt, wb = wslot()
            P.dma("pool", lambda e: e.dma_start(out=wt[:, :nk * 128], in_=wd[c]), writes=[wb])
            pt, pb = pbank()
            wv = wt[:, :nk * 128].rearrange("p (k j) -> p k j", j=128)

            def mm(e):
                for k in range(nk):
                    ins = e.matmul(pt[:, :ncols], lhsT=wv[:, k, :], rhs=src[:, k, :ncols], start=(k == 0), stop=(k == nk - 1))
                return ins
            P.add("pe", mm, reads=[wb, srcB], writes=[pb])
            consume(pt, pb)

        def r2(lhs, lhsB, nK, wd, nblocks, cb, nst, consume):
            GK = 4096 // cb
            for n in range(nblocks):
                banks = [pbank() for _ in range(nst)]
                for k0 in range(0, nK, GK):
                    gk = min(GK, nK - k0)
                    wt, wb = wslot()
                    P.dma("pool", lambda e, wt=wt, n=n, k0=k0, gk=gk: e.dma_start(
                        out=wt[:, :gk * cb].rearrange("p (k j) -> p k j", j=cb),
                        in_=wd[n, k0:k0 + gk].rearrange("k p j -> p k j")), writes=[wb])

                    def mm(e, wt=wt, k0=k0, gk=gk, banks=banks):
                        wv = wt[:, :gk * cb].rearrange("p (k j) -> p k j", j=cb)
                        for st in range(nst):
                            for k in range(gk):
                                ins = e.matmul(banks[st][0][:, :cb], lhsT=lhs(k0 + k, st), rhs=wv[:, k, :],
                                               start=(k0 + k == 0), stop=(k0 + k == nK - 1))
                        return ins
                    P.add("pe", mm, reads=[wb] + list(lhsB), writes=[b for (_, b) in banks])
                for st in range(nst):
                    consume(n, st, banks[st][0], banks[st][1])

        def uT_lhs(k, st):
            return uT[:, k, st * 128:(st + 1) * 128]

        def ffn(wi_d, wo_d):
            hid = abf(0, FH * T // 2).rearrange("p (c t) -> p c t", t=T)
            hidB = Buf()
            sil = [af32(FH * T // 2 + i * T, T) for i in range(2)]
            silB = [Buf(), Buf()]
            for half in range(2):
                for ci in range(FH):
                    c = half * FH + ci
                    got = {}
                    r1(wi_d, c, T, lambda pt, pb: got.update(a=(pt, pb)))
                    r1(wi_d, FC + c, T, lambda pt, pb: got.update(b=(pt, pb)))
                    (pa, pab), (pbk, pbb) = got["a"], got["b"]
                    si = ci % 2
                    P.add("act", lambda e, pa=pa, si=si: e.activation(out=sil[si], in_=pa[:], func=AF.Silu),
                          reads=[pab], writes=[silB[si]])
                    P.add("dve", lambda e, pbk=pbk, si=si, ci=ci: e.tensor_tensor(
                        out=hid[:, ci, :], in0=pbk[:], in1=sil[si], op=ALU.mult),
                        reads=[pbb, silB[si]], writes=[hidB])

                def cons(n, st, pt, pb):
                    P.add("dve", lambda e: e.scalar_tensor_tensor(
                        out=h[st][:, n * 512:(n + 1) * 512], in0=pt[:], scalar=0.5,
                        in1=h[st][:, n * 512:(n + 1) * 512], op0=ALU.mult, op1=ALU.add),
                        reads=[pb], writes=[hB[st]])
                wo_half = wo_d[:, half * FH:(half + 1) * FH]
                r2(lambda k, st: hid[:, k, st * 128:(st + 1) * 128], [hidB], FH, wo_half, NB, 512, NST, cons)

        JUNK = ARENA - D // 2

        for m in range(2):
            P.dma("sp", lambda e, m=m: e.dma_start(out=h[m], in_=mem_d[m * 128:(m + 1) * 128, :]), writes=[hB[m]])
            norm_sub(h[m], hB[m], 3, m * 128, JUNK)
        kmT = kmT_t[:].rearrange("p (h m) -> p h m", m=MEM)
        for hh in range(4):
            r1(wck_d, hh, MEM, lambda pt, pb, hh=hh: P.add(
                "act", lambda e: e.activation(out=kmT[:, hh, :], in_=pt[:, :MEM], func=AF.Copy),
                reads=[pb], writes=[memB]))
        vm = vm_t[:].rearrange("p (m c) -> p m c", c=512)
        r2(uT_lhs, [uTB], KC, wcv_d, 1, 512, 2, lambda n, st, pt, pb: P.add(
            "act", lambda e: e.activation(out=vm[:, st, :], in_=pt[:], func=AF.Copy), reads=[pb], writes=[memB]))
        P.barrier()

        stg_off = FH * T // 2 + 2 * T
        def phaseA(i):
            t0 = i * T
            for st in range(NST):
                P.dma("sp", lambda e, st=st: e.dma_start(out=h[st], in_=x_d[t0 + st * 128:t0 + (st + 1) * 128, :]),
                      writes=[hB[st]])
                norm_sub(h[st], hB[st], 0, st * 128, JUNK)
            ffn(w1i_d, w1o_d)
            for st in range(NST):
                P.dma("sp", lambda e, st=st: e.dma_start(out=h1_s[t0 + st * 128:t0 + (st + 1) * 128, :], in_=h[st]),
                      reads=[hB[st]])
                norm_sub(h[st], hB[st], 1, st * 128, JUNK)
            for st in range(NST):
                j = i * NST + st
                yv = st_t[:, 0:64]
                yf = st_t[:, 64:128]
                P.add("dve", lambda e, j=j: e.tensor_scalar(out=yv, in0=invf, scalar1=posf_t[:, j:j + 1],
                                                            scalar2=1.0 / (2 * math.pi), op0=ALU.mult, op1=ALU.mult),
                      reads=[consB], writes=[statB])
                P.add("dve", lambda e: e.tensor_copy(out=ri_t[:], in_=yv), writes=[statB])
                P.add("dve", lambda e: e.tensor_copy(out=yf, in_=ri_t[:]), writes=[statB])
                P.add("dve", lambda e: e.tensor_tensor(out=yv, in0=yv, in1=yf, op=ALU.subtract), writes=[statB])
                sa = st_t[:, 128:192]
                sbh = st_t[:, 192:256]
                P.add("act", lambda e: e.activation(out=sa, in_=yv, func=AF.Sin, scale=math.pi), writes=[statB])
                P.add("act", lambda e: e.activation(out=sbh, in_=yv, func=AF.Sin, scale=math.pi / 2), writes=[statB])
                P.add("dve", lambda e: e.tensor_tensor(out=sbh, in0=sbh, in1=sbh, op=ALU.mult), writes=[statB])
                P.add("dve", lambda e: e.tensor_scalar(out=sbh, in0=sbh, scalar1=-2.0, scalar2=1.0, op0=ALU.mult, op1=ALU.add),
                      writes=[statB])
                P.add("dve", lambda e, st=st: e.scalar_tensor_tensor(out=cs[:, 1, st, :], in0=sa, scalar=2.0, in1=sbh,
                                                                     op0=ALU.mult, op1=ALU.mult), writes=[statB, csB])
                P.add("dve", lambda e: e.tensor_tensor(out=sa, in0=sa, in1=sa, op=ALU.mult), writes=[statB])
                P.add("dve", lambda e, st=st: e.tensor_scalar(out=cs[:, 0, st, :], in0=sa, scalar1=-2.0, scalar2=1.0,
                                                              op0=ALU.mult, op1=ALU.add), writes=[statB, csB])
            HB = CB // 128
            stg = [abf(i2 * 256, 256) for i2 in range(4)]
            stgB = [Buf() for _ in range(4)]
            tmp = [af32(1024 + i2 * 256, 256) for i2 in range(4)]
            tmpB = Buf()
            sc = {"n": 0, "x": 0}
            stg2 = [abf(3072 + i2 * 256, 256) for i2 in range(2)]
            stg2B = [Buf(), Buf()]
            hz = [af32(3584 + i2 * 16, 16) for i2 in range(2)]
            hzB = [Buf(), Buf()]

            def qkv_cons(n, st, pt, pb):
                g, r, qd = n // (3 * QB), (n // QB) % 3, n % QB
                si = sc["n"] % 4
                sc["n"] += 1
                so, sB = stg[si][:, :CB], stgB[si]
                if r == 2:
                    P.add("act", lambda e: e.activation(out=so, in_=pt[:, :CB], func=AF.Copy), reads=[pb], writes=[sB])
                else:
                    pv = pt[:, :CB].rearrange("p (h two j) -> p h two j", two=2, j=64)
                    ov = so.rearrange("p (h two j) -> p h two j", two=2, j=64)
                    cosb = cs[:, 0, st, :].rearrange("p (o j) -> p o j", o=1).to_broadcast([128, HB, 64])
                    sinb = cs[:, 1, st, :].rearrange("p (o j) -> p o j", o=1).to_broadcast([128, HB, 64])
                    tv = [tmp[k][:, :HB * 64].rearrange("p (h j) -> p h j", j=64) for k in range(4)]
                    P.add("dve", lambda e: e.tensor_tensor(out=tv[0], in0=pv[:, :, 0, :], in1=cosb, op=ALU.mult),
                          reads=[pb, csB], writes=[tmpB])
                    P.add("dve", lambda e: e.tensor_tensor(out=tv[1], in0=pv[:, :, 1, :], in1=sinb, op=ALU.mult),
                          reads=[pb, csB], writes=[tmpB])
                    P.add("dve", lambda e: e.tensor_tensor(out=tv[2], in0=pv[:, :, 1, :], in1=cosb, op=ALU.mult),
                          reads=[pb, csB], writes=[tmpB])
                    P.add("dve", lambda e: e.tensor_tensor(out=tv[3], in0=pv[:, :, 0, :], in1=sinb, op=ALU.mult),
                          reads=[pb, csB], writes=[tmpB])
                    P.add("dve", lambda e: e.tensor_tensor(out=ov[:, :, 0, :], in0=tv[0], in1=tv[1], op=ALU.subtract),
                          reads=[tmpB], writes=[sB])
                    P.add("dve", lambda e: e.tensor_tensor(out=ov[:, :, 1, :], in0=tv[2], in1=tv[3], op=ALU.add),
                          reads=[tmpB], writes=[sB])
                P.dma("sp", lambda e: e.dma_start(
                    out=qkv_s[g][r][t0 + st * 128:t0 + (st + 1) * 128, qd * CB:(qd + 1) * CB], in_=so), reads=[sB])
                if r > 0 and (g == 2 or i == NT - 1):
                    xi = sc["x"] % 2
                    sc["x"] += 1
                    so2 = stg2[xi][:, :CB]
                    P.add("dve", lambda e: e.tensor_scalar_mul(out=so2, in0=so, scalar1=flag0), reads=[sB, consB],
                          writes=[stg2B[xi]])
                    r0 = XR[(g, r)] + (t0 if g == 2 else 0) + st * 128
                    P.dma("sp", lambda e: e.dma_start(out=xin[r0:r0 + 128, qd * CB:(qd + 1) * CB], in_=so2),
                          reads=[stg2B[xi]])
            r2(uT_lhs, [uTB], KC, wqkv_d, 9 * QB, CB, NST, qkv_cons)
            zst = [af32(2048 + i2 * T, T) for i2 in range(2)]
            zstB = [Buf(), Buf()]
            for c in range(NPC):
                def zcons(pt, pb, c=c):
                    zi = c % 2
                    P.add("act", lambda e: e.activation(out=zst[zi], in_=pt[:], func=AF.Copy), reads=[pb], writes=[zstB[zi]])
                    P.dma("sp", lambda e: e.dma_start(out=zp_s[c, :, t0:t0 + T], in_=zst[zi]), reads=[zstB[zi]])
                    if i == NT - 1:
                        P.add("dve", lambda e: e.tensor_scalar_mul(out=hz[zi], in0=zst[zi][:, T - 16:T], scalar1=flag0),
                              reads=[zstB[zi], consB], writes=[hzB[zi]])
                        P.dma("sp", lambda e: e.dma_start(out=xzin[c], in_=hz[zi]), reads=[hzB[zi]])
                r1(wzp_d, c, T, zcons)
            P.barrier()

        for i in range(NT):
            phaseA(i)
        CR = max(128, (4 * 1024 * 1024) // (AW * 2))
        for r0 in range(0, XROWS, CR):
            r1_ = min(XROWS, r0 + CR)
            P.cc(lambda e, r0=r0, r1_=r1_: e.collective_compute(
                "AllReduce", ALU.add, replica_groups=pairs,
                ins=[xin_h.ap()[r0:r1_].opt()], outs=[xout_h.ap()[r0:r1_].opt()]), writes=[xoutB], first=(r0 == 0))
        P.cc(lambda e: e.collective_compute("AllReduce", ALU.add, replica_groups=pairs,
                                            ins=[xzin_h.ap().opt()], outs=[xzout_h.ap().opt()]), writes=[xzB], first=False)

        sm = 1.0 / math.sqrt(128.0)
        NR = 3
        QW = AW // 2
        off = 0
        qt = []
        for i2 in range(2):
            qt.append(abf(off, QW)); off += QW
        kt = []
        for i2 in range(2):
            kt.append(abf(off, QW)); off += QW
        pT = []
        for i2 in range(3):
            pT.append(abf(off, 256)); off += 256
        assert off <= ARENA, off
        off = 0
        VW = (H * 129 + 1) // 2
        vt = []
        for i2 in range(NR):
            vt.append(abf(off, VW, h_t)[:, :H * 129].rearrange("p (h j) -> p h j", j=129)); off += VW
        qT = []
        for i2 in range(2):
            qT.append(abf(off, QW, h_t).rearrange("p (h j) -> p h j", j=128)); off += QW
        kT = []
        for i2 in range(NR):
            kT.append(abf(off, QW, h_t).rearrange("p (h j) -> p h j", j=128)); off += QW
        ot = []
        for i2 in range(2):
            ot.append(af32(off, H * 129, h_t).rearrange("p (h j) -> p h j", j=129)); off += H * 129
        assert off <= NST * D, off
        qtB, ktB = [Buf(), Buf()], [Buf(), Buf()]
        vtB = [Buf() for _ in range(NR)]
        NG4 = max(1, H // 4)
        qTB = [[Buf() for _ in range(NG4)] for _ in range(2)]
        kTB = [[Buf() for _ in range(NG4)] for _ in range(NR)]
        pTB = [Buf() for _ in range(3)]
        otB = [Buf(), Buf()]
        for i2 in range(NR):
            P.add("dve", lambda e, i2=i2: e.memset(vt[i2][:, :, 128:129], 1.0), writes=[vtB[i2]])
        blk = 0
        pcount = 0
        for g, dil in enumerate(DILS):
            L = S // dil
            qv_own = [qkv_s[g][r].rearrange("(a d) c -> d a c", d=dil) for r in range(3)]
            nprev = S if g == 2 else T
            qv_prev = [None] + [xout[XR[(g, r)]:XR[(g, r)] + nprev].rearrange("(a d) c -> d a c", d=dil) for r in (1, 2)]
            ov_d = o_s[g].rearrange("(a d) c -> d a c", d=dil)
            for res in range(dil):
                nbq = L // 128
                if g == 0:
                    order = [(0, True)] + [(bq, False) for bq in range(1, nbq)] + [(-1, True), (0, False)]
                else:
                    order = [(-1, True)] + [(bq, False) for bq in range(nbq)]
                for (b, kvonly) in order:
                    q2, k3 = blk % 2, blk % NR
                    kprev = (blk - 1) % NR
                    if b < 0:
                        qv = qv_prev
                        rows = slice(nprev // dil - 128, nprev // dil)
                    else:
                        qv = qv_own
                        rows = slice(b * 128, (b + 1) * 128)
                    xr = [xoutB] if b < 0 else []
                    if not kvonly:
                        P.dma("sp", lambda e, q2=q2, res=res, rows=rows, qv=qv: e.dma_start(out=qt[q2], in_=qv[0][res, rows, :]),
                              writes=[qtB[q2]])
                    P.dma("sp", lambda e, q2=q2, res=res, rows=rows, qv=qv: e.dma_start(out=kt[q2], in_=qv[1][res, rows, :]),
                          reads=xr, writes=[ktB[q2]])
                    P.dma("sp", lambda e, k3=k3, res=res, rows=rows, qv=qv: e.dma_start(
                        out=vt[k3][:, :, 0:128], in_=qv[2][res, rows, :].rearrange("p (h j) -> p h j", j=128)),
                        reads=xr, writes=[vtB[k3]])
                    tl = [(kt[q2], ktB[q2], kT[k3], kTB[k3])]
                    if not kvonly:
                        tl.insert(0, (qt[q2], qtB[q2], qT[q2], qTB[q2]))
                    for (srct, srcB, dstT, dstB) in tl:
                        for h0 in range(0, H, 4):
                            pt, pb = pbank()

                            def mm(e, pt=pt, h0=h0, srct=srct):
                                for j in range(4):
                                    ins = e.matmul(pt[:, j * 128:(j + 1) * 128], lhsT=srct[:, (h0 + j) * 128:(h0 + j + 1) * 128],
                                                   rhs=identb_t[:], start=True, stop=True)
                                return ins
                            P.add("pe", mm, reads=[srcB, consB], writes=[pb])
                            if (h0 // 4) % 2 == 0:
                                P.add("act", lambda e, pt=pt, h0=h0, dstT=dstT: e.activation(
                                    out=dstT[:, h0:h0 + 4, :], in_=pt[:].rearrange("p (a b) -> p a b", b=128), func=AF.Copy),
                                    reads=[pb], writes=[dstB[h0 // 4]])
                            else:
                                P.add("dve", lambda e, pt=pt, h0=h0, dstT=dstT: e.tensor_copy(
                                    out=dstT[:, h0:h0 + 4, :], in_=pt[:].rearrange("p (a b) -> p a b", b=128)),
                                    reads=[pb], writes=[dstB[h0 // 4]])
                    if kvonly:
                        blk += 1
                        continue
                    o2 = blk % 2
                    nblk = 2
                    cm = cmask1 if b == 0 else cmask
                    def stage1(h0):
                        nonlocal pcount
                        pt, pb = pbank()
                        pi = pcount % 3
                        pcount += 1

                        def smm(e, pt=pt, h0=h0, q2=q2, k3=k3, kprev=kprev):
                            for j in range(2):
                                ins = e.matmul(pt[:, j * 256:j * 256 + 128], lhsT=kT[k3][:, h0 + j, :], rhs=qT[q2][:, h0 + j, :],
                                               start=True, stop=True)
                                ins = e.matmul(pt[:, j * 256 + 128:j * 256 + 256], lhsT=kT[kprev][:, h0 + j, :],
                                               rhs=qT[q2][:, h0 + j, :], start=True, stop=True)
                            return ins
                        P.add("pe", smm, reads=[kTB[k3][h0 // 4], qTB[q2][h0 // 4], kTB[kprev][h0 // 4]], writes=[pb])
                        P.add("act", lambda e, pt=pt, pi=pi: e.activation(out=pT[pi], in_=pt[:], func=AF.Exp, scale=sm),
                              reads=[pb], writes=[pTB[pi]])
                        P.add("dve", lambda e, pi=pi, cm=cm: e.tensor_tensor(out=pT[pi], in0=pT[pi], in1=cm, op=ALU.mult),
                              reads=[consB], writes=[pTB[pi]])
                        return pi

                    def stage2(h0, pi):
                        po, pob = pbank()

                        def pvm(e, po=po, pi=pi, h0=h0, k3=k3, kprev=kprev):
                            for j in range(2):
                                ins = e.matmul(po[:, j * 129:(j + 1) * 129], lhsT=pT[pi][:, j * 256:j * 256 + 128],
                                               rhs=vt[k3][:, h0 + j, :], start=True, stop=False)
                                ins = e.matmul(po[:, j * 129:(j + 1) * 129], lhsT=pT[pi][:, j * 256 + 128:j * 256 + 256],
                                               rhs=vt[kprev][:, h0 + j, :], start=False, stop=True)
                            return ins
                        P.add("pe", pvm, reads=[pTB[pi], vtB[k3], vtB[kprev]], writes=[pob])
                        P.add("dve", lambda e, po=po, h0=h0, o2=o2: e.tensor_copy(
                            out=ot[o2][:, h0:h0 + 2, :], in_=po[:, :258].rearrange("p (h j) -> p h j", j=129)),
                            reads=[pob], writes=[otB[o2]])

                    hps = list(range(0, H, 2))
                    pis = {}
                    SK = 2
                    for k_ in range(len(hps) + SK):
                        if k_ < len(hps):
                            pis[k_] = stage1(hps[k_])
                        if k_ >= SK:
                            stage2(hps[k_ - SK], pis[k_ - SK])
                    P.dma("sp", lambda e, o2=o2, res=res, rows=rows, ov_d=ov_d: e.dma_start(
                        out=ov_d[res, rows, :], in_=ot[o2].rearrange("p h j -> p (h j)")), reads=[otB[o2]])
                    blk += 1
        P.barrier()

        opT_off = 0
        opT = abf(opT_off, NPC * T // 2).rearrange("p (c t) -> p c t", t=T)
        opTB = Buf()
        s1 = NPC * T // 2
        ZL = 16 + T
        def phaseD(i):
            t0 = i * T
            zb = [af32(s1 + k * PGC * ZL, PGC * ZL).rearrange("p (c t) -> p c t", t=ZL) for k in range(3)]
            zbB = [Buf() for _ in range(3)]
            yb = abf(s1 + 3 * PGC * ZL, PGC * T // 2).rearrange("p (c t) -> p c t", t=T)
            ybB = Buf()
            fx = af32(s1 + 3 * PGC * ZL + PGC * T // 2, PGC * 16).rearrange("p (c t) -> p c t", t=16)
            for g, w in enumerate(POOL_W):
                A, Bz, Cz = zb
                if i == 0:
                    P.dma("sp", lambda e, g=g: e.dma_start(
                        out=A[:, :, 0:16], in_=xzout[g * PGC:(g + 1) * PGC].rearrange("c p t -> p c t")),
                        reads=[xzB], writes=[zbB[0]])
                    P.add("dve", lambda e: e.tensor_scalar_mul(out=A[:, :, 0:16], in0=A[:, :, 0:16], scalar1=flag1),
                          reads=[consB], writes=[zbB[0]])
                    P.dma("sp", lambda e, g=g: e.dma_start(
                        out=A[:, :, 16:], in_=zp_s[g * PGC:(g + 1) * PGC, :, 0:T].rearrange("c p t -> p c t")),
                        writes=[zbB[0]])
                else:
                    P.dma("sp", lambda e, g=g: e.dma_start(
                        out=A[:], in_=zp_s[g * PGC:(g + 1) * PGC, :, t0 - 16:t0 + T].rearrange("c p t -> p c t")),
                        writes=[zbB[0]])
                chain = [(Bz, A, 1, 1), (Cz, Bz, 3, 2), (Bz, Cz, 7, 4), (Cz, Bz, 15, 8)]
                nsteps = {2: 1, 4: 2, 8: 3, 16: 4}[w]
                bufB = {id(A): zbB[0], id(Bz): zbB[1], id(Cz): zbB[2]}
                for (dst, src, lo, sh) in chain[:nsteps]:
                    P.add("dve", lambda e, dst=dst, src=src, lo=lo, sh=sh: e.tensor_tensor(
                        out=dst[:, :, lo:], in0=src[:, :, lo:], in1=src[:, :, lo - sh:ZL - sh], op=ALU.add),
                        reads=[bufB[id(src)]], writes=[bufB[id(dst)]])
                sw = chain[nsteps - 1][0]
                P.add("dve", lambda e, sw=sw, w=w: e.scalar_tensor_tensor(
                    out=yb[:], in0=sw[:, :, 16:], scalar=1.0 / w, in1=A[:, :, 16:], op0=ALU.mult, op1=ALU.subtract),
                    reads=[bufB[id(sw)], zbB[0]], writes=[ybB])
                if i == 0:
                    icb = invc[:, g * 16:(g + 1) * 16].rearrange("p (o t) -> p o t", o=1).to_broadcast([128, PGC, 16])
                    P.add("dve", lambda e, sw=sw, icb=icb: e.tensor_tensor(out=fx, in0=sw[:, :, 16:32], in1=icb, op=ALU.mult),
                          reads=[bufB[id(sw)], consB], writes=[ybB])
                    P.add("dve", lambda e: e.tensor_tensor(out=yb[:, :, 0:16], in0=fx, in1=A[:, :, 16:32], op=ALU.subtract),
                          reads=[zbB[0]], writes=[ybB])
                wt, wb = wslot()
                P.dma("pool", lambda e, wt=wt, g=g: e.dma_start(out=wt[:, :PGC * cfg.PG], in_=wpl_d[g]), writes=[wb])
                wv = wt[:, :PGC * cfg.PG].rearrange("p (c j) -> p c j", j=cfg.PG)
                for dc in range(PGC):
                    pt, pb = pbank()

                    def mm(e, pt=pt, wv=wv, dc=dc):
                        for cc in range(PGC):
                            ins = e.matmul(pt[:], lhsT=wv[:, cc, dc * 128:(dc + 1) * 128], rhs=yb[:, cc, :],
                                           start=(cc == 0), stop=(cc == PGC - 1))
                        return ins
                    P.add("pe", mm, reads=[wb, ybB], writes=[pb])
                    oc = g * PGC + dc
                    P.add("act", lambda e, pt=pt, oc=oc: e.activation(out=opT[:, oc, :], in_=pt[:], func=AF.Copy,
                                                                       scale=psc_t[:, oc:oc + 1]),
                          reads=[pb, consB], writes=[opTB])
            P.barrier()
            for st in range(NST):
                P.dma("sp", lambda e, st=st: e.dma_start(out=h[st], in_=h1_s[t0 + st * 128:t0 + (st + 1) * 128, :]),
                      writes=[hB[st]])
            oT = uT_t[:, 0:H * T].rearrange("p (c t) -> p c t", t=T)
            oTB = uTB
            ob = uT_t[:, H * T:H * T + H * 128].rearrange("p (h j) -> p h j", j=128)
            s3 = s1
            acc = af32(s3, H * 129).rearrange("p (h j) -> p h j", j=129)
            tm2 = af32(s3 + H * 129, H * 129).rearrange("p (h j) -> p h j", j=129)
            rz = af32(s3 + 2 * H * 129, H)
            assert s3 + 2 * H * 129 + H <= ARENA, "arena"
            accB, tm2B, obB = Buf(), Buf(), Buf()
            for st in range(NST):
                rows = slice(t0 + st * 128, t0 + (st + 1) * 128)
                P.dma("sp", lambda e, rows=rows: e.dma_start(out=acc.rearrange("p h j -> p (h j)"), in_=o_s[0][rows, :]),
                      writes=[accB])
                for g in (1, 2):
                    P.dma("sp", lambda e, rows=rows, g=g: e.dma_start(out=tm2.rearrange("p h j -> p (h j)"), in_=o_s[g][rows, :]),
                          writes=[tm2B])
                    P.add("dve", lambda e: e.tensor_tensor(out=acc, in0=acc, in1=tm2, op=ALU.add),
                          reads=[tm2B], writes=[accB])
                P.add("dve", lambda e: e.reciprocal(out=rz, in_=acc[:, :, 128:129].rearrange("p h o -> p (h o)")),
                      writes=[accB])
                P.add("dve", lambda e: e.tensor_tensor(
                    out=ob, in0=acc[:, :, 0:128], in1=rz.rearrange("p (h o) -> p h o", o=1).to_broadcast([128, H, 128]),
                    op=ALU.mult), reads=[accB], writes=[obB])
                for h0 in range(0, H, 4):
                    pt, pb = pbank()

                    def mm(e, pt=pt, h0=h0):
                        for j in range(4):
                            ins = e.matmul(pt[:, j * 128:(j + 1) * 128], lhsT=ob[:, h0 + j, :], rhs=identb_t[:],
                                           start=True, stop=True)
                        return ins
                    P.add("pe", mm, reads=[obB, consB], writes=[pb])
                    P.add("act", lambda e, pt=pt, h0=h0, st=st: e.activation(
                        out=oT[:, h0:h0 + 4, st * 128:(st + 1) * 128], in_=pt[:].rearrange("p (a b) -> p a b", b=128),
                        func=AF.Copy), reads=[pb], writes=[oTB])

            def mo_lhs(k, st):
                if k < H:
                    return oT[:, k, st * 128:(st + 1) * 128]
                return opT[:, k - H, st * 128:(st + 1) * 128]

            def addh(n, st, pt, pb):
                P.add("dve", lambda e: e.tensor_tensor(out=h[st][:, n * 512:(n + 1) * 512], in0=pt[:],
                                                       in1=h[st][:, n * 512:(n + 1) * 512], op=ALU.add),
                      reads=[pb], writes=[hB[st]])
            r2(mo_lhs, [oTB, opTB], KC, wmo_d, NB, 512, NST, addh)
            for st in range(NST):
                norm_sub(h[st], hB[st], 2, st * 128, JUNK)
            qcT = abf(0, 4 * T // 2).rearrange("p (h t) -> p h t", t=T)
            qcB = Buf()
            ocT = abf(4 * T // 2, 4 * T // 2).rearrange("p (h t) -> p h t", t=T)
            ocB = Buf()
            pc = [abf(4 * T + k * T // 2, T // 2) for k in range(2)]
            pcB = [Buf(), Buf()]
            rzc = af32(4 * T + T, T)
            rzcB = Buf()
            for hh in range(4):
                r1(wcq_d, hh, T, lambda pt, pb, hh=hh: P.add(
                    "act", lambda e: e.activation(out=qcT[:, hh, :], in_=pt[:], func=AF.Copy), reads=[pb], writes=[qcB]))
            for hh in range(4):
                for m in range(2):
                    pt, pb = pbank()
                    P.add("pe", lambda e, pt=pt, hh=hh, m=m: e.matmul(
                        pt[:], lhsT=kmT[:, hh, m * 128:(m + 1) * 128], rhs=qcT[:, hh, :], start=True, stop=True),
                        reads=[memB, qcB], writes=[pb])
                    P.add("act", lambda e, pt=pt, m=m: e.activation(out=pc[m], in_=pt[:], func=AF.Exp, scale=sm),
                          reads=[pb], writes=[pcB[m]])
                po, pob = pbank()
                pz, pzb = pbank()

                def om(e, po=po, hh=hh):
                    for m in range(2):
                        ins = e.matmul(po[:], lhsT=vm[:, m, hh * 128:(hh + 1) * 128], rhs=pc[m], start=(m == 0), stop=(m == 1))
                    return ins

                def zm(e, pz=pz):
                    for m in range(2):
                        ins = e.matmul(pz[:], lhsT=onesb_t[:], rhs=pc[m], start=(m == 0), stop=(m == 1))
                    return ins
                P.add("pe", om, reads=[memB] + pcB, writes=[pob])
                P.add("pe", zm, reads=[consB] + pcB, writes=[pzb])
                P.add("dve", lambda e, pz=pz: e.reciprocal(out=rzc, in_=pz[:]), reads=[pzb], writes=[rzcB])
                P.add("dve", lambda e, po=po, hh=hh: e.tensor_tensor(out=ocT[:, hh, :], in0=po[:], in1=rzc, op=ALU.mult),
                      reads=[pob, rzcB], writes=[ocB])
            r2(lambda k, st: ocT[:, k, st * 128:(st + 1) * 128], [ocB], 4, wco_d, NB, 512, NST, addh)
            for st in range(NST):
                norm_sub(h[st], hB[st], 4, st * 128, JUNK)
            ffn(w2i_d, w2o_d)
            gfin = uT_t[:, KC * T // 2:KC * T].bitcast(F32)
            P.dma("sp", lambda e: e.dma_start(out=gfin, in_=gfin_d.partition_broadcast(128)), writes=[uTB])
            fB = [Buf() for _ in range(NST)]
            for st in range(NST):
                junk = abf(JUNK, D // 2)
                ssq = fin_t[:, st:st + 1]
                P.add("dve", lambda e, ssq=ssq: e.memset(ssq, 0.0), writes=[fB[st]])
                P.add("act", lambda e, st=st, ssq=ssq: e.activation(out=junk, in_=h[st], func=AF.Square, accum_out=ssq),
                      reads=[hB[st]], writes=[fB[st], arB])
                P.add("act", lambda e, ssq=ssq: e.activation(out=ssq, in_=ssq, func=AF.Sqrt, bias=eps_t[:], scale=1.0 / D),
                      reads=[consB], writes=[fB[st]])
                P.add("dve", lambda e, ssq=ssq: e.reciprocal(out=ssq, in_=ssq), writes=[fB[st]])
                P.add("dve", lambda e, st=st, ssq=ssq: e.scalar_tensor_tensor(
                    out=h[st], in0=h[st], scalar=ssq, in1=gfin, op0=ALU.mult, op1=ALU.mult),
                    reads=[fB[st], uTB], writes=[hB[st]])
                P.dma("sp", lambda e, st=st: e.dma_start(out=y_d[t0 + st * 128:t0 + (st + 1) * 128, :], in_=h[st]),
                      reads=[hB[st]], bar=False)
            P.barrier()

        for i in range(NT):
            phaseD(i)
        P.barrier(full=True)
        P.emit(nc, es)
    return nc


def _lhsT_layout(w, KC):
    K, n = w.shape
    return np.ascontiguousarray(w.reshape(KC, 128, n // 128, 128).transpose(2, 1, 0, 3)).reshape(n // 128, 128, KC * 128)


def _rhs_layout(w, cb):
    K, n = w.shape
    return np.ascontiguousarray(w.reshape(K // 128, 128, n // cb, cb).transpose(2, 0, 1, 3))


def prepare(cfg, x, mem, positions, g_ffn1, w_ffn1_in, w_ffn1_out, g_mix, w_mix_in, w_pool, pool_scale, w_mix_out,
            g_cross, g_mem, w_cross_q, w_cross_kv, w_cross_o, g_ffn2, w_ffn2_in, w_ffn2_out, g_final):
    import ml_dtypes
    D, S, KC, PGC = cfg.D, cfg.S, cfg.KC, cfg.PGC
    f = np.float32
    shared = {}
    gs = np.stack([np.asarray(g, f).reshape(D) for g in (g_ffn1[0], g_mix[0], g_cross[0], g_mem[0], g_ffn2[0])])
    shared["gfm"] = np.ascontiguousarray(gs.reshape(5, KC, 128).transpose(2, 0, 1)).reshape(128, 5 * KC)
    shared["gfin"] = np.asarray(g_final, f).reshape(1, D)
    shared["psc"] = np.ascontiguousarray(np.asarray(pool_scale[0], f).reshape(4 * PGC, 128).T)
    kk = np.arange(128)[:, None]
    qq = np.arange(128)[None, :]
    cur = (kk <= qq).astype(f)
    prev = (kk >= qq).astype(f)
    zero = np.zeros_like(prev)
    cm_all = np.concatenate([cur, prev, cur, prev], axis=1)
    cmasks = [np.concatenate([cm_all, np.concatenate([cur, prev * par, cur, prev * par], axis=1)], axis=1).astype(ml_dtypes.bfloat16)
              for par in (0.0, 1.0)]
    inv = (1.0 / (ROPE_THETA ** (np.arange(0, 128, 2, dtype=f) / f(128)))).astype(f)
    cf32s = []
    for par in (0, 1):
        if par == 0:
            invc = np.stack([1.0 / np.minimum(np.arange(16) + 1, w) for w in POOL_W]).astype(f).reshape(64)
        else:
            invc = np.stack([np.full(16, 1.0 / w) for w in POOL_W]).astype(f).reshape(64)
        flags = np.array([1.0 - par, float(par)], dtype=f)
        cf32s.append(np.concatenate([np.eye(128, dtype=f), np.tile(inv[None], (128, 1)), np.tile(invc[None], (128, 1)),
                                     np.tile(flags[None], (128, 1))], axis=1))
    shared["w1i"] = _lhsT_layout(np.asarray(w_ffn1_in[0], f), KC)
    shared["w1o"] = _rhs_layout(np.asarray(w_ffn1_out[0], f), 512)
    shared["w2i"] = _lhsT_layout(np.asarray(w_ffn2_in[0], f), KC)
    shared["w2o"] = _rhs_layout(np.asarray(w_ffn2_out[0], f), 512)
    wmi = np.asarray(w_mix_in[0], f)
    shared["wqkv"] = _rhs_layout(wmi[:, :cfg.ATTN_IN], cfg.CB)
    shared["wzp"] = _lhsT_layout(wmi[:, cfg.ATTN_IN:], KC)
    wp = np.asarray(w_pool[0], f)
    shared["wpl"] = np.ascontiguousarray(wp.reshape(4, PGC, 128, cfg.PG).transpose(0, 2, 1, 3)).reshape(4, 128, PGC * cfg.PG)
    shared["wmo"] = _rhs_layout(np.asarray(w_mix_out[0], f), 512)
    shared["wcq"] = _lhsT_layout(np.asarray(w_cross_q[0], f), KC)
    wkv = np.asarray(w_cross_kv[0], f)
    shared["wck"] = _lhsT_layout(wkv[:, :512], KC)
    shared["wcv"] = _rhs_layout(wkv[:, 512:], 512)
    shared["wco"] = _rhs_layout(np.asarray(w_cross_o[0], f), 512)
    in_maps = []
    for b in range(cfg.B):
        for par in (0, 1):
            m = dict(shared)
            sl = slice(par * S, (par + 1) * S)
            m["x"] = np.ascontiguousarray(np.asarray(x[b], f)[sl])
            m["mem"] = np.ascontiguousarray(np.asarray(mem[b], f))
            m["pos"] = np.ascontiguousarray(np.asarray(positions[b], np.int32)[sl].reshape(S // 128, 128).T)
            m["cmask"] = cmasks[par]
            m["cf32"] = cf32s[par]
            in_maps.append(m)
    return in_maps


def kernel(**inputs):
    x = inputs["x"]
    cfg = Cfg(D=x.shape[2], S=x.shape[1], B=x.shape[0])
    nc = build(cfg)
    in_maps = prepare(cfg, **inputs)
    res = run_bass_kernel_spmd(nc, in_maps, core_ids=list(range(2 * cfg.B)))
    ys = [np.asarray(r["y"], np.float32) for r in res.results]
    return np.stack([np.concatenate([ys[2 * b], ys[2 * b + 1]], axis=0) for b in range(cfg.B)], axis=0)
```

```python
import math
from contextlib import ExitStack

import numpy as np
import concourse.bass as bass
import concourse.mybir as mybir
from concourse.bass_utils import run_bass_kernel_spmd

F32 = mybir.dt.float32
BF16 = mybir.dt.bfloat16
I32 = mybir.dt.int32
AF = mybir.ActivationFunctionType
ALU = mybir.AluOpType

EPS = 1e-6
ROPE_THETA = 10000.0
DILS = (1, 4, 16)
POOL_W = (2, 4, 8, 16)
MEM = 256
T = 512
NST = 4


class Cfg:
    def __init__(self, D=4096, S=4096, B=4):
        self.D, self.SF, self.B = D, S, B
        self.S = S // 2
        S = self.S
        self.KC = D // 128
        self.AW = D // 2
        self.H = self.AW // 128
        self.PW = D - self.AW
        self.PG = self.PW // 4
        self.PGC = self.PG // 128
        self.F = ((8 * D // 3 + 255) // 256) * 256
        self.FC = self.F // 128
        self.FH = self.FC // 2
        self.NB = D // 512
        self.NT = S // T
        self.CB = min(512, self.AW)
        self.QB = self.AW // self.CB
        self.ATTN_IN = 9 * self.AW


class Buf:
    __slots__ = ("w", "rs")

    def __init__(self):
        self.w = None
        self.rs = []


class Op:
    __slots__ = ("eng", "fn", "deps", "sig", "kind", "sem", "val", "prevslot")

    def __init__(self, eng, fn, kind):
        self.eng, self.fn, self.kind = eng, fn, kind
        self.deps = []
        self.sig = False
        self.sem = None
        self.val = 0
        self.prevslot = None


ENGS = ("pe", "act", "dve", "pool", "sp")
KQ = 8


class Prog:
    def __init__(self):
        self.ops = {e: [] for e in ENGS}
        self.lastc = {e: None for e in ENGS}
        self.dmas = {e: [] for e in ENGS}
        self.ccs = []
        self.nobar = set()

    def add(self, eng, fn, reads=(), writes=(), kind="c"):
        op = Op(eng, fn, kind)
        deps = []
        for b in reads:
            if b.w is not None:
                deps.append(b.w)
        for b in writes:
            if b.w is not None:
                deps.append(b.w)
            deps.extend(b.rs)
        for b in reads:
            b.rs.append(op)
        for b in writes:
            b.w = op
            b.rs = []
        seen = set()
        for d in deps:
            if id(d) in seen or d is op:
                continue
            seen.add(id(d))
            if d.eng == "pe" and eng == "pe" and d.kind == "c" and kind == "c":
                continue
            op.deps.append(d)
            d.sig = True
        self.ops[eng].append(op)
        if kind == "c":
            self.lastc[eng] = op
        elif kind == "d":
            self.dmas[eng].append(op)
        return op

    def dma(self, eng, fn, reads=(), writes=(), bar=True):
        op = self.add(eng, fn, reads, writes, kind="d")
        if not bar:
            self.nobar.add(id(op))
        return op

    def cc(self, fn, writes=(), first=True):
        if first:
            self.barrier(full=True)
        op = Op("pool", fn, "cc")
        if self.ccs:
            op.deps.append(self.ccs[-1])
        for b in writes:
            b.w = op
            b.rs = []
        self.ops["pool"].append(op)
        self.ccs.append(op)
        return op

    def barrier(self, full=False):
        deps = []
        for e in ENGS:
            if self.lastc[e] is not None:
                deps.append(self.lastc[e])
            deps.extend(d for d in self.dmas[e][-KQ:] if full or id(d) not in self.nobar)
        deps.extend(self.ccs)
        for e in ENGS:
            op = Op(e, None, "f")
            for d in deps:
                op.deps.append(d)
                d.sig = True
            self.ops[e].append(op)

    def emit(self, nc, es):
        semh = {}
        for e in ENGS:
            semh[("c", e)] = es.enter_context(nc.semaphore(f"c_{e}"))
            if e in ("sp", "pool"):
                for k in range(KQ):
                    semh[("d", e, k)] = es.enter_context(nc.semaphore(f"d_{e}{k}"))
        for k, op in enumerate(self.ccs):
            op.sem = ("cc", k)
            op.val = 1
            semh[op.sem] = es.enter_context(nc.semaphore(f"cc_{k}"))
        for e in ENGS:
            cnt = 0
            dl = []
            for op in self.ops[e]:
                if op.kind == "c" and op.sig:
                    cnt += 1
                    op.sem = ("c", e)
                    op.val = cnt
                elif op.kind == "d":
                    n = len(dl)
                    op.sem = ("d", e, n % KQ)
                    op.val = 16 * (n // KQ + 1)
                    if n >= KQ:
                        op.prevslot = dl[n - KQ]
                    dl.append(op)
        block = es.enter_context(nc.Block())

        def run(e, eng):
            waited = {}
            for op in self.ops[e]:
                deps = op.deps if op.prevslot is None else op.deps + [op.prevslot]
                if len(deps) > 1:
                    deps = sorted(deps, key=lambda d: -d.val)
                for d in deps:
                    if waited.get(d.sem, 0) >= d.val:
                        continue
                    eng.wait_ge(semh[d.sem], d.val)
                    waited[d.sem] = d.val
                if op.kind == "f":
                    continue
                ins = op.fn(eng)
                if op.kind == "d":
                    ins.then_inc(semh[op.sem], 16)
                elif op.kind == "cc":
                    ins.then_inc(semh[op.sem])
                elif op.sig:
                    ins.then_inc(semh[op.sem], 1)

        block.tensor(lambda eng: run("pe", eng))
        block.scalar(lambda eng: run("act", eng))
        block.vector(lambda eng: run("dve", eng))
        block.gpsimd(lambda eng: run("pool", eng))
        block.sync(lambda eng: run("sp", eng))


def build(cfg):
    D, S, KC, H, FC, FH, NB, NT = cfg.D, cfg.S, cfg.KC, cfg.H, cfg.FC, cfg.FH, cfg.NB, cfg.NT
    AW, CB, QB, PGC = cfg.AW, cfg.CB, cfg.QB, cfg.PGC
    NPC = 4 * PGC
    nc = bass.Bass("TRN2", target_bir_lowering=False)
    P = Prog()

    def din(name, shape, dt=F32):
        return nc.dram_tensor(name, list(shape), dt, kind="ExternalInput").ap()

    x_d = din("x", [S, D])
    mem_d = din("mem", [MEM, D])
    pos_d = din("pos", [128, S // 128], I32)
    gfm_d = din("gfm", [128, 5 * KC])
    gfin_d = din("gfin", [1, D])
    psc_d = din("psc", [128, NPC])
    cmask_d = din("cmask", [128, 1024], BF16)
    cf32_d = din("cf32", [128, 128 + 64 + 64 + 2])
    w1i_d = din("w1i", [2 * FC, 128, KC * 128])
    w1o_d = din("w1o", [NB, FC, 128, 512])
    w2i_d = din("w2i", [2 * FC, 128, KC * 128])
    w2o_d = din("w2o", [NB, FC, 128, 512])
    wqkv_d = din("wqkv", [9 * QB, KC, 128, CB])
    wzp_d = din("wzp", [NPC, 128, KC * 128])
    wpl_d = din("wpl", [4, 128, PGC * cfg.PG])
    wmo_d = din("wmo", [NB, KC, 128, 512])
    wcq_d = din("wcq", [4, 128, KC * 128])
    wck_d = din("wck", [4, 128, KC * 128])
    wcv_d = din("wcv", [1, KC, 128, 512])
    wco_d = din("wco", [NB, 4, 128, 512])
    y_d = nc.dram_tensor("y", [S, D], F32, kind="ExternalOutput").ap()

    h1_s = nc.dram_tensor("h1_s", [S, D], F32).ap()
    qkv_s = [[nc.dram_tensor(f"qkv_s{g}{r}", [S, AW], BF16).ap() for r in range(3)] for g in range(3)]
    zp_s = nc.dram_tensor("zp_s", [NPC, 128, S], F32).ap()
    o_s = [nc.dram_tensor(f"o_s{g}", [S, H * 129], F32).ap() for g in range(3)]
    XROWS = 2 * S + 4 * T
    XR = {(2, 1): 0, (2, 2): S, (1, 1): 2 * S, (1, 2): 2 * S + T, (0, 1): 2 * S + 2 * T, (0, 2): 2 * S + 3 * T}
    xin_h = nc.dram_tensor("xin", [XROWS, AW], BF16)
    xout_h = nc.dram_tensor("xout", [XROWS, AW], BF16)
    xzin_h = nc.dram_tensor("xzin", [NPC * 128, 16], F32)
    xzout_h = nc.dram_tensor("xzout", [NPC * 128, 16], F32)
    xin, xout = xin_h.ap(), xout_h.ap()
    xzin = xzin_h.ap().rearrange("(c p) t -> c p t", p=128)
    xzout = xzout_h.ap().rearrange("(c p) t -> c p t", p=128)
    pairs = [[2 * k, 2 * k + 1] for k in range(cfg.B)]

    es = ExitStack()
    with es:
        def sb(name, shape, dt):
            return es.enter_context(nc.sbuf_tensor("s_" + name, list(shape), dt))

        h_t = sb("h", [128, NST * D], F32)
        uT_t = sb("uT", [128, KC * T], BF16)
        WSL = 3
        wr_t = [sb(f"wr{i}", [128, 4096], BF16) for i in range(WSL)]
        ARENA = 13312
        ar_t = sb("arena", [128, ARENA], F32)
        cons_t = sb("consf", [128, 258], F32)
        gfm_t = sb("gfm", [128, 5 * KC], F32)
        psc_t = sb("psc", [128, NPC], F32)
        cmask_t = sb("cmask", [128, 1024], BF16)
        identb_t = sb("identb", [128, 128], BF16)
        onesb_t = sb("onesb", [128, 128], BF16)
        posi_t = sb("posi", [128, S // 128], I32)
        posf_t = sb("posf", [128, S // 128], F32)
        cs_t = sb("cs", [128, 2 * NST * 64], F32)
        st_t = sb("stat", [128, 256], F32)
        ri_t = sb("rint", [128, 64], I32)
        diag_t = sb("diag", [128, 4 * 128], F32)
        nst_t = sb("nstat", [128, 8], F32)
        kmT_t = sb("kmT", [128, 4 * MEM], BF16)
        vm_t = sb("vm", [128, 2 * 512], BF16)
        negpi_t = sb("negpi", [128, 1], F32)
        eps_t = sb("epst", [128, 1], F32)
        fin_t = sb("fint", [128, 4], F32)
        ps_t = [es.enter_context(nc.psum_tensor(f"ps{i}", [128, 512], F32)) for i in range(8)]

        h = [h_t[:, st * D:(st + 1) * D] for st in range(NST)]
        hB = [Buf() for _ in range(NST)]
        uT = uT_t[:].rearrange("p (k t) -> p k t", t=T)
        uTB = Buf()
        wrB = [Buf() for _ in range(WSL)]
        psB = [Buf() for _ in range(8)]
        consB, statB, diagB, csB, memB, arB = Buf(), Buf(), [Buf() for _ in range(4)], Buf(), Buf(), Buf()
        nstB = [Buf() for _ in range(4)]
        xoutB, xzB = Buf(), Buf()
        identf = cons_t[:, 0:128]
        invf = cons_t[:, 128:192]
        invc = cons_t[:, 192:256]
        gfm = gfm_t[:].rearrange("p (n k) -> p n k", k=KC)
        cmask = cmask_t[:, 0:512]
        cmask1 = cmask_t[:, 512:1024]
        flag0 = cons_t[:, 256:257]
        flag1 = cons_t[:, 257:258]
        cs = cs_t[:].rearrange("p (c s j) -> p c s j", c=2, s=NST)

        state = {"w": 0, "ps": 0, "diag": 0}

        def wslot():
            i = state["w"] % WSL
            state["w"] += 1
            return wr_t[i], wrB[i]

        def pbank():
            i = state["ps"] % 8
            state["ps"] += 1
            return ps_t[i], psB[i]

        def af32(off, n, t=None):
            t = ar_t if t is None else t
            return t[:, off:off + n]

        def abf(off, n, t=None):
            t = ar_t if t is None else t
            return t[:, off:off + n].bitcast(BF16)

        for (dst, src) in ((cons_t[:], cf32_d), (gfm_t[:], gfm_d), (psc_t[:], psc_d),
                           (cmask_t[:], cmask_d), (posi_t[:], pos_d)):
            P.dma("sp", lambda e, d=dst, s=src: e.dma_start(out=d, in_=s), writes=[consB])
        P.add("dve", lambda e: e.tensor_copy(out=posf_t[:], in_=posi_t[:]), reads=[consB], writes=[consB])
        P.add("dve", lambda e: e.tensor_copy(out=identb_t[:], in_=identf), reads=[consB], writes=[consB])
        P.add("dve", lambda e: e.memset(onesb_t[:], 1.0), writes=[consB])
        P.add("dve", lambda e: e.memset(negpi_t[:], -math.pi), writes=[consB])
        P.add("dve", lambda e: e.memset(eps_t[:], EPS), writes=[consB])

        def norm_sub(hap, hb, gi, col0, junk_off):
            junk = abf(junk_off, D // 2)
            di = state["diag"] % 4
            state["diag"] += 1
            ssq = nst_t[:, 2 * di:2 * di + 1]
            rstd = nst_t[:, 2 * di + 1:2 * di + 2]
            nB = nstB[di]
            dg = diag_t[:, di * 128:(di + 1) * 128]
            P.add("dve", lambda e: e.memset(ssq, 0.0), writes=[nB])
            P.add("act", lambda e: e.activation(out=junk, in_=hap, func=AF.Square, accum_out=ssq),
                  reads=[hb], writes=[nB, arB])
            P.add("act", lambda e: e.activation(out=rstd, in_=ssq, func=AF.Sqrt, bias=eps_t[:], scale=1.0 / D),
                  reads=[consB], writes=[nB])
            P.add("dve", lambda e: e.reciprocal(out=rstd, in_=rstd), writes=[nB])
            P.add("dve", lambda e: e.tensor_scalar_mul(out=dg, in0=identf, scalar1=rstd),
                  reads=[nB, consB], writes=[diagB[di]])
            for k0 in range(0, KC, 4):
                pt, pb = pbank()

                def mm(e, k0=k0, pt=pt):
                    for j in range(4):
                        ins = e.matmul(pt[:, j * 128:(j + 1) * 128], lhsT=hap[:, (k0 + j) * 128:(k0 + j + 1) * 128],
                                       rhs=dg, start=True, stop=True)
                    return ins
                P.add("pe", mm, reads=[hb, diagB[di]], writes=[pb])
                if gi is None:
                    P.add("dve", lambda e, k0=k0, pt=pt: e.tensor_copy(
                        out=uT[:, k0:k0 + 4, col0:col0 + 128], in_=pt[:].rearrange("p (a b) -> p a b", b=128)),
                        reads=[pb], writes=[uTB])
                else:
                    P.add("dve", lambda e, k0=k0, pt=pt: e.tensor_tensor(
                        out=uT[:, k0:k0 + 4, col0:col0 + 128], in0=pt[:].rearrange("p (a b) -> p a b", b=128),
                        in1=gfm[:, gi, k0:k0 + 4].to_broadcast([128, 4, 128]), op=ALU.mult),
                        reads=[pb, consB], writes=[uTB])
            return rstd

        def r1(wd, c, ncols, consume, src=None, srcB=None, nk=None):
            src = uT if src is None else src
            srcB = uTB if srcB is None else srcB
            nk = KC if nk is None else nk
            wt, wb = wslot()
            P.dma("pool", lambda e: e.dma_start(out=wt[:, :nk * 128], in_=wd[c]), writes=[wb])
            pt, pb = pbank()
            wv = wt[:, :nk * 128].rearrange("p (k j) -> p k j", j=128)

            def mm(e):
                for k in range(nk):
                    ins = e.matmul(pt[:, :ncols], lhsT=wv[:, k, :], rhs=src[:, k, :ncols], start=(k == 0), stop=(k == nk - 1))
                return ins
            P.add("pe", mm, reads=[wb, srcB], writes=[pb])
            consume(pt, pb)

        def r2(lhs, lhsB, nK, wd, nblocks, cb, nst, consume):
            GK = 4096 // cb
            for n in range(nblocks):
                banks = [pbank() for _ in range(nst)]
                for k0 in range(0, nK, GK):
                    gk = min(GK, nK - k0)
                    wt, wb = wslot()
                    P.dma("pool", lambda e, wt=wt, n=n, k0=k0, gk=gk: e.dma_start(
                        out=wt[:, :gk * cb].rearrange("p (k j) -> p k j", j=cb),
                        in_=wd[n, k0:k0 + gk].rearrange("k p j -> p k j")), writes=[wb])

                    def mm(e, wt=wt, k0=k0, gk=gk, banks=banks):
                        wv = wt[:, :gk * cb].rearrange("p (k j) -> p k j", j=cb)
                        for st in range(nst):
                            for k in range(gk):
                                ins = e.matmul(banks[st][0][:, :cb], lhsT=lhs(k0 + k, st), rhs=wv[:, k, :],
                                               start=(k0 + k == 0), stop=(k0 + k == nK - 1))
                        return ins
                    P.add("pe", mm, reads=[wb] + list(lhsB), writes=[b for (_, b) in banks])
                for st in range(nst):
                    consume(n, st, banks[st][0], banks[st][1])

        def uT_lhs(k, st):
            return uT[:, k, st * 128:(st + 1) * 128]

        def ffn(wi_d, wo_d):
            hid = abf(0, FH * T // 2).rearrange("p (c t) -> p c t", t=T)
            hidB = Buf()
            sil = [af32(FH * T // 2 + i * T, T) for i in range(2)]
            silB = [Buf(), Buf()]
            for half in range(2):
                for ci in range(FH):
                    c = half * FH + ci
                    got = {}
                    r1(wi_d, c, T, lambda pt, pb: got.update(a=(pt, pb)))
                    r1(wi_d, FC + c, T, lambda pt, pb: got.update(b=(pt, pb)))
                    (pa, pab), (pbk, pbb) = got["a"], got["b"]
                    si = ci % 2
                    P.add("act", lambda e, pa=pa, si=si: e.activation(out=sil[si], in_=pa[:], func=AF.Silu),
                          reads=[pab], writes=[silB[si]])
                    P.add("dve", lambda e, pbk=pbk, si=si, ci=ci: e.tensor_tensor(
                        out=hid[:, ci, :], in0=pbk[:], in1=sil[si], op=ALU.mult),
                        reads=[pbb, silB[si]], writes=[hidB])

                def cons(n, st, pt, pb):
                    P.add("dve", lambda e: e.scalar_tensor_tensor(
                        out=h[st][:, n * 512:(n + 1) * 512], in0=pt[:], scalar=0.5,
                        in1=h[st][:, n * 512:(n + 1) * 512], op0=ALU.mult, op1=ALU.add),
                        reads=[pb], writes=[hB[st]])
                wo_half = wo_d[:, half * FH:(half + 1) * FH]
                r2(lambda k, st: hid[:, k, st * 128:(st + 1) * 128], [hidB], FH, wo_half, NB, 512, NST, cons)

        JUNK = ARENA - D // 2

        for m in range(2):
            P.dma("sp", lambda e, m=m: e.dma_start(out=h[m], in_=mem_d[m * 128:(m + 1) * 128, :]), writes=[hB[m]])
            norm_sub(h[m], hB[m], 3, m * 128, JUNK)
        kmT = kmT_t[:].rearrange("p (h m) -> p h m", m=MEM)
        for hh in range(4):
            r1(wck_d, hh, MEM, lambda pt, pb, hh=hh: P.add(
                "act", lambda e: e.activation(out=kmT[:, hh, :], in_=pt[:, :MEM], func=AF.Copy),
                reads=[pb], writes=[memB]))
        vm = vm_t[:].rearrange("p (m c) -> p m c", c=512)
        r2(uT_lhs, [uTB], KC, wcv_d, 1, 512, 2, lambda n, st, pt, pb: P.add(
            "act", lambda e: e.activation(out=vm[:, st, :], in_=pt[:], func=AF.Copy), reads=[pb], writes=[memB]))
        P.barrier()

        stg_off = FH * T // 2 + 2 * T
        def phaseA(i):
            t0 = i * T
            for st in range(NST):
                P.dma("sp", lambda e, st=st: e.dma_start(out=h[st], in_=x_d[t0 + st * 128:t0 + (st + 1) * 128, :]),
                      writes=[hB[st]])
                norm_sub(h[st], hB[st], 0, st * 128, JUNK)
            ffn(w1i_d, w1o_d)
            for st in range(NST):
                P.dma("sp", lambda e, st=st: e.dma_start(out=h1_s[t0 + st * 128:t0 + (st + 1) * 128, :], in_=h[st]),
                      reads=[hB[st]])
                norm_sub(h[st], hB[st], 1, st * 128, JUNK)
            for st in range(NST):
                j = i * NST + st
                yv = st_t[:, 0:64]
                yf = st_t[:, 64:128]
                P.add("dve", lambda e, j=j: e.tensor_scalar(out=yv, in0=invf, scalar1=posf_t[:, j:j + 1],
                                                            scalar2=1.0 / (2 * math.pi), op0=ALU.mult, op1=ALU.mult),
                      reads=[consB], writes=[statB])
                P.add("dve", lambda e: e.tensor_copy(out=ri_t[:], in_=yv), writes=[statB])
                P.add("dve", lambda e: e.tensor_copy(out=yf, in_=ri_t[:]), writes=[statB])
                P.add("dve", lambda e: e.tensor_tensor(out=yv, in0=yv, in1=yf, op=ALU.subtract), writes=[statB])
                sa = st_t[:, 128:192]
                sbh = st_t[:, 192:256]
                P.add("act", lambda e: e.activation(out=sa, in_=yv, func=AF.Sin, scale=math.pi), writes=[statB])
                P.add("act", lambda e: e.activation(out=sbh, in_=yv, func=AF.Sin, scale=math.pi / 2), writes=[statB])
                P.add("dve", lambda e: e.tensor_tensor(out=sbh, in0=sbh, in1=sbh, op=ALU.mult), writes=[statB])
                P.add("dve", lambda e: e.tensor_scalar(out=sbh, in0=sbh, scalar1=-2.0, scalar2=1.0, op0=ALU.mult, op1=ALU.add),
                      writes=[statB])
                P.add("dve", lambda e, st=st: e.scalar_tensor_tensor(out=cs[:, 1, st, :], in0=sa, scalar=2.0, in1=sbh,
                                                                     op0=ALU.mult, op1=ALU.mult), writes=[statB, csB])
                P.add("dve", lambda e: e.tensor_tensor(out=sa, in0=sa, in1=sa, op=ALU.mult), writes=[statB])
                P.add("dve", lambda e, st=st: e.tensor_scalar(out=cs[:, 0, st, :], in0=sa, scalar1=-2.0, scalar2=1.0,
                                                              op0=ALU.mult, op1=ALU.add), writes=[statB, csB])
            HB = CB // 128
            stg = [abf(i2 * 256, 256) for i2 in range(4)]
            stgB = [Buf() for _ in range(4)]
            tmp = [af32(1024 + i2 * 256, 256) for i2 in range(4)]
            tmpB = Buf()
            sc = {"n": 0, "x": 0}
            stg2 = [abf(3072 + i2 * 256, 256) for i2 in range(2)]
            stg2B = [Buf(), Buf()]
            hz = [af32(3584 + i2 * 16, 16) for i2 in range(2)]
            hzB = [Buf(), Buf()]

            def qkv_cons(n, st, pt, pb):
                g, r, qd = n // (3 * QB), (n // QB) % 3, n % QB
                si = sc["n"] % 4
                sc["n"] += 1
                so, sB = stg[si][:, :CB], stgB[si]
                if r == 2:
                    P.add("act", lambda e: e.activation(out=so, in_=pt[:, :CB], func=AF.Copy), reads=[pb], writes=[sB])
                else:
                    pv = pt[:, :CB].rearrange("p (h two j) -> p h two j", two=2, j=64)
                    ov = so.rearrange("p (h two j) -> p h two j", two=2, j=64)
                    cosb = cs[:, 0, st, :].rearrange("p (o j) -> p o j", o=1).to_broadcast([128, HB, 64])
                    sinb = cs[:, 1, st, :].rearrange("p (o j) -> p o j", o=1).to_broadcast([128, HB, 64])
                    tv = [tmp[k][:, :HB * 64].rearrange("p (h j) -> p h j", j=64) for k in range(4)]
                    P.add("dve", lambda e: e.tensor_tensor(out=tv[0], in0=pv[:, :, 0, :], in1=cosb, op=ALU.mult),
                          reads=[pb, csB], writes=[tmpB])
                    P.add("dve", lambda e: e.tensor_tensor(out=tv[1], in0=pv[:, :, 1, :], in1=sinb, op=ALU.mult),
                          reads=[pb, csB], writes=[tmpB])
                    P.add("dve", lambda e: e.tensor_tensor(out=tv[2], in0=pv[:, :, 1, :], in1=cosb, op=ALU.mult),
                          reads=[pb, csB], writes=[tmpB])
                    P.add("dve", lambda e: e.tensor_tensor(out=tv[3], in0=pv[:, :, 0, :], in1=sinb, op=ALU.mult),
                          reads=[pb, csB], writes=[tmpB])
                    P.add("dve", lambda e: e.tensor_tensor(out=ov[:, :, 0, :], in0=tv[0], in1=tv[1], op=ALU.subtract),
                          reads=[tmpB], writes=[sB])
                    P.add("dve", lambda e: e.tensor_tensor(out=ov[:, :, 1, :], in0=tv[2], in1=tv[3], op=ALU.add),
                          reads=[tmpB], writes=[sB])
                P.dma("sp", lambda e: e.dma_start(
                    out=qkv_s[g][r][t0 + st * 128:t0 + (st + 1) * 128, qd * CB:(qd + 1) * CB], in_=so), reads=[sB])
                if r > 0 and (g == 2 or i == NT - 1):
                    xi = sc["x"] % 2
                    sc["x"] += 1
                    so2 = stg2[xi][:, :CB]
                    P.add("dve", lambda e: e.tensor_scalar_mul(out=so2, in0=so, scalar1=flag0), reads=[sB, consB],
                          writes=[stg2B[xi]])
                    r0 = XR[(g, r)] + (t0 if g == 2 else 0) + st * 128
                    P.dma("sp", lambda e: e.dma_start(out=xin[r0:r0 + 128, qd * CB:(qd + 1) * CB], in_=so2),
                          reads=[stg2B[xi]])
            r2(uT_lhs, [uTB], KC, wqkv_d, 9 * QB, CB, NST, qkv_cons)
            zst = [af32(2048 + i2 * T, T) for i2 in range(2)]
            zstB = [Buf(), Buf()]
            for c in range(NPC):
                def zcons(pt, pb, c=c):
                    zi = c % 2
                    P.add("act", lambda e: e.activation(out=zst[zi], in_=pt[:], func=AF.Copy), reads=[pb], writes=[zstB[zi]])
                    P.dma("sp", lambda e: e.dma_start(out=zp_s[c, :, t0:t0 + T], in_=zst[zi]), reads=[zstB[zi]])
                    if i == NT - 1:
                        P.add("dve", lambda e: e.tensor_scalar_mul(out=hz[zi], in0=zst[zi][:, T - 16:T], scalar1=flag0),
                              reads=[zstB[zi], consB], writes=[hzB[zi]])
                        P.dma("sp", lambda e: e.dma_start(out=xzin[c], in_=hz[zi]), reads=[hzB[zi]])
                r1(wzp_d, c, T, zcons)
            P.barrier()

        for i in range(NT):
            phaseA(i)
        CR = max(128, (4 * 1024 * 1024) // (AW * 2))
        for r0 in range(0, XROWS, CR):
            r1_ = min(XROWS, r0 + CR)
            P.cc(lambda e, r0=r0, r1_=r1_: e.collective_compute(
                "AllReduce", ALU.add, replica_groups=pairs,
                ins=[xin_h.ap()[r0:r1_].opt()], outs=[xout_h.ap()[r0:r1_].opt()]), writes=[xoutB], first=(r0 == 0))
        P.cc(lambda e: e.collective_compute("AllReduce", ALU.add, replica_groups=pairs,
                                            ins=[xzin_h.ap().opt()], outs=[xzout_h.ap().opt()]), writes=[xzB], first=False)

        sm = 1.0 / math.sqrt(128.0)
        NR = 3
        QW = AW // 2
        off = 0
        NQ = 4
        qt = []
        for i2 in range(NQ):
            qt.append(abf(off, QW)); off += QW
        kt = []
        for i2 in range(NQ):
            kt.append(abf(off, QW)); off += QW
        pT = []
        for i2 in range(3):
            pT.append(abf(off, 256)); off += 256
        assert off <= ARENA, off
        off = 0
        VW = (H * 129 + 1) // 2
        vt = []
        for i2 in range(NR):
            vt.append(abf(off, VW, h_t)[:, :H * 129].rearrange("p (h j) -> p h j", j=129)); off += VW
        qT = []
        for i2 in range(2):
            qT.append(abf(off, QW, h_t).rearrange("p (h j) -> p h j", j=128)); off += QW
        kT = []
        for i2 in range(NR):
            kT.append(abf(off, QW, h_t).rearrange("p (h j) -> p h j", j=128)); off += QW
        ot = []
        for i2 in range(2):
            ot.append(af32(off, H * 129, h_t).rearrange("p (h j) -> p h j", j=129)); off += H * 129
        assert off <= NST * D, off
        qtB, ktB = [Buf() for _ in range(NQ)], [Buf() for _ in range(NQ)]
        vtB = [Buf() for _ in range(NR)]
        NG4 = max(1, H // 4)
        qTB = [[Buf() for _ in range(NG4)] for _ in range(2)]
        kTB = [[Buf() for _ in range(NG4)] for _ in range(NR)]
        pTB = [Buf() for _ in range(3)]
        otB = [Buf(), Buf()]
        for i2 in range(NR):
            P.add("dve", lambda e, i2=i2: e.memset(vt[i2][:, :, 128:129], 1.0), writes=[vtB[i2]])
        blk = 0
        pcount = 0
        for g, dil in enumerate(DILS):
            L = S // dil
            qv_own = [qkv_s[g][r].rearrange("(a d) c -> d a c", d=dil) for r in range(3)]
            nprev = S if g == 2 else T
            qv_prev = [None] + [xout[XR[(g, r)]:XR[(g, r)] + nprev].rearrange("(a d) c -> d a c", d=dil) for r in (1, 2)]
            ov_d = o_s[g].rearrange("(a d) c -> d a c", d=dil)
            for res in range(dil):
                nbq = L // 128
                if g == 0:
                    order = [(0, True)] + [(bq, False) for bq in range(1, nbq)] + [(-1, True), (0, False)]
                else:
                    order = [(-1, True)] + [(bq, False) for bq in range(nbq)]
                for (b, kvonly) in order:
                    q2, k3 = blk % 2, blk % NR
                    q4 = blk % NQ
                    kprev = (blk - 1) % NR
                    if b < 0:
                        qv = qv_prev
                        rows = slice(nprev // dil - 128, nprev // dil)
                    else:
                        qv = qv_own
                        rows = slice(b * 128, (b + 1) * 128)
                    xr = [xoutB] if b < 0 else []
                    if not kvonly:
                        P.dma("sp", lambda e, q4=q4, res=res, rows=rows, qv=qv: e.dma_start(out=qt[q4], in_=qv[0][res, rows, :]),
                              writes=[qtB[q4]])
                    P.dma("sp", lambda e, q4=q4, res=res, rows=rows, qv=qv: e.dma_start(out=kt[q4], in_=qv[1][res, rows, :]),
                          reads=xr, writes=[ktB[q4]])
                    P.dma("sp", lambda e, k3=k3, res=res, rows=rows, qv=qv: e.dma_start(
                        out=vt[k3][:, :, 0:128], in_=qv[2][res, rows, :].rearrange("p (h j) -> p h j", j=128)),
                        reads=xr, writes=[vtB[k3]])
                    tl = [(kt[q4], ktB[q4], kT[k3], kTB[k3])]
                    if not kvonly:
                        tl.insert(0, (qt[q4], qtB[q4], qT[q2], qTB[q2]))
                    for (srct, srcB, dstT, dstB) in tl:
                        for h0 in range(0, H, 4):
                            pt, pb = pbank()

                            def mm(e, pt=pt, h0=h0, srct=srct):
                                for j in range(4):
                                    ins = e.matmul(pt[:, j * 128:(j + 1) * 128], lhsT=srct[:, (h0 + j) * 128:(h0 + j + 1) * 128],
                                                   rhs=identb_t[:], start=True, stop=True)
                                return ins
                            P.add("pe", mm, reads=[srcB, consB], writes=[pb])
                            if (h0 // 4) % 2 == 0:
                                P.add("act", lambda e, pt=pt, h0=h0, dstT=dstT: e.activation(
                                    out=dstT[:, h0:h0 + 4, :], in_=pt[:].rearrange("p (a b) -> p a b", b=128), func=AF.Copy),
                                    reads=[pb], writes=[dstB[h0 // 4]])
                            else:
                                P.add("dve", lambda e, pt=pt, h0=h0, dstT=dstT: e.tensor_copy(
                                    out=dstT[:, h0:h0 + 4, :], in_=pt[:].rearrange("p (a b) -> p a b", b=128)),
                                    reads=[pb], writes=[dstB[h0 // 4]])
                    if kvonly:
                        blk += 1
                        continue
                    o2 = blk % 2
                    nblk = 2
                    cm = cmask1 if b == 0 else cmask
                    def stage1(h0):
                        nonlocal pcount
                        pt, pb = pbank()
                        pi = pcount % 3
                        pcount += 1

                        def smm(e, pt=pt, h0=h0, q2=q2, k3=k3, kprev=kprev):
                            for j in range(2):
                                ins = e.matmul(pt[:, j * 256:j * 256 + 128], lhsT=kT[k3][:, h0 + j, :], rhs=qT[q2][:, h0 + j, :],
                                               start=True, stop=True)
                                ins = e.matmul(pt[:, j * 256 + 128:j * 256 + 256], lhsT=kT[kprev][:, h0 + j, :],
                                               rhs=qT[q2][:, h0 + j, :], start=True, stop=True)
                            return ins
                        P.add("pe", smm, reads=[kTB[k3][h0 // 4], qTB[q2][h0 // 4], kTB[kprev][h0 // 4]], writes=[pb])
                        P.add("act", lambda e, pt=pt, pi=pi: e.activation(out=pT[pi], in_=pt[:], func=AF.Exp, scale=sm),
                              reads=[pb], writes=[pTB[pi]])
                        P.add("dve", lambda e, pi=pi, cm=cm: e.tensor_tensor(out=pT[pi], in0=pT[pi], in1=cm, op=ALU.mult),
                              reads=[consB], writes=[pTB[pi]])
                        return pi

                    def stage2(h0, pi):
                        po, pob = pbank()

                        def pvm(e, po=po, pi=pi, h0=h0, k3=k3, kprev=kprev):
                            for j in range(2):
                                ins = e.matmul(po[:, j * 129:(j + 1) * 129], lhsT=pT[pi][:, j * 256:j * 256 + 128],
                                               rhs=vt[k3][:, h0 + j, :], start=True, stop=False)
                                ins = e.matmul(po[:, j * 129:(j + 1) * 129], lhsT=pT[pi][:, j * 256 + 128:j * 256 + 256],
                                               rhs=vt[kprev][:, h0 + j, :], start=False, stop=True)
                            return ins
                        P.add("pe", pvm, reads=[pTB[pi], vtB[k3], vtB[kprev]], writes=[pob])
                        P.add("dve", lambda e, po=po, h0=h0, o2=o2: e.tensor_copy(
                            out=ot[o2][:, h0:h0 + 2, :], in_=po[:, :258].rearrange("p (h j) -> p h j", j=129)),
                            reads=[pob], writes=[otB[o2]])

                    hps = list(range(0, H, 2))
                    pis = {}
                    SK = 2
                    for k_ in range(len(hps) + SK):
                        if k_ < len(hps):
                            pis[k_] = stage1(hps[k_])
                        if k_ >= SK:
                            stage2(hps[k_ - SK], pis[k_ - SK])
                    P.dma("sp", lambda e, o2=o2, res=res, rows=rows, ov_d=ov_d: e.dma_start(
                        out=ov_d[res, rows, :], in_=ot[o2].rearrange("p h j -> p (h j)")), reads=[otB[o2]])
                    blk += 1
        P.barrier()

        opT_off = 0
        opT = abf(opT_off, NPC * T // 2).rearrange("p (c t) -> p c t", t=T)
        opTB = Buf()
        s1 = NPC * T // 2
        ZL = 16 + T
        def phaseD(i):
            t0 = i * T
            zb = [af32(s1 + k * PGC * ZL, PGC * ZL).rearrange("p (c t) -> p c t", t=ZL) for k in range(3)]
            zbB = [Buf() for _ in range(3)]
            yb = abf(s1 + 3 * PGC * ZL, PGC * T // 2).rearrange("p (c t) -> p c t", t=T)
            ybB = Buf()
            fx = af32(s1 + 3 * PGC * ZL + PGC * T // 2, PGC * 16).rearrange("p (c t) -> p c t", t=16)
            for g, w in enumerate(POOL_W):
                A, Bz, Cz = zb
                if i == 0:
                    P.dma("sp", lambda e, g=g: e.dma_start(
                        out=A[:, :, 0:16], in_=xzout[g * PGC:(g + 1) * PGC].rearrange("c p t -> p c t")),
                        reads=[xzB], writes=[zbB[0]])
                    P.add("dve", lambda e: e.tensor_scalar_mul(out=A[:, :, 0:16], in0=A[:, :, 0:16], scalar1=flag1),
                          reads=[consB], writes=[zbB[0]])
                    P.dma("sp", lambda e, g=g: e.dma_start(
                        out=A[:, :, 16:], in_=zp_s[g * PGC:(g + 1) * PGC, :, 0:T].rearrange("c p t -> p c t")),
                        writes=[zbB[0]])
                else:
                    P.dma("sp", lambda e, g=g: e.dma_start(
                        out=A[:], in_=zp_s[g * PGC:(g + 1) * PGC, :, t0 - 16:t0 + T].rearrange("c p t -> p c t")),
                        writes=[zbB[0]])
                chain = [(Bz, A, 1, 1), (Cz, Bz, 3, 2), (Bz, Cz, 7, 4), (Cz, Bz, 15, 8)]
                nsteps = {2: 1, 4: 2, 8: 3, 16: 4}[w]
                bufB = {id(A): zbB[0], id(Bz): zbB[1], id(Cz): zbB[2]}
                for (dst, src, lo, sh) in chain[:nsteps]:
                    P.add("dve", lambda e, dst=dst, src=src, lo=lo, sh=sh: e.tensor_tensor(
                        out=dst[:, :, lo:], in0=src[:, :, lo:], in1=src[:, :, lo - sh:ZL - sh], op=ALU.add),
                        reads=[bufB[id(src)]], writes=[bufB[id(dst)]])
                sw = chain[nsteps - 1][0]
                P.add("dve", lambda e, sw=sw, w=w: e.scalar_tensor_tensor(
                    out=yb[:], in0=sw[:, :, 16:], scalar=1.0 / w, in1=A[:, :, 16:], op0=ALU.mult, op1=ALU.subtract),
                    reads=[bufB[id(sw)], zbB[0]], writes=[ybB])
                if i == 0:
                    icb = invc[:, g * 16:(g + 1) * 16].rearrange("p (o t) -> p o t", o=1).to_broadcast([128, PGC, 16])
                    P.add("dve", lambda e, sw=sw, icb=icb: e.tensor_tensor(out=fx, in0=sw[:, :, 16:32], in1=icb, op=ALU.mult),
                          reads=[bufB[id(sw)], consB], writes=[ybB])
                    P.add("dve", lambda e: e.tensor_tensor(out=yb[:, :, 0:16], in0=fx, in1=A[:, :, 16:32], op=ALU.subtract),
                          reads=[zbB[0]], writes=[ybB])
                wt, wb = wslot()
                P.dma("pool", lambda e, wt=wt, g=g: e.dma_start(out=wt[:, :PGC * cfg.PG], in_=wpl_d[g]), writes=[wb])
                wv = wt[:, :PGC * cfg.PG].rearrange("p (c j) -> p c j", j=cfg.PG)
                for dc in range(PGC):
                    pt, pb = pbank()

                    def mm(e, pt=pt, wv=wv, dc=dc):
                        for cc in range(PGC):
                            ins = e.matmul(pt[:], lhsT=wv[:, cc, dc * 128:(dc + 1) * 128], rhs=yb[:, cc, :],
                                           start=(cc == 0), stop=(cc == PGC - 1))
                        return ins
                    P.add("pe", mm, reads=[wb, ybB], writes=[pb])
                    oc = g * PGC + dc
                    P.add("act", lambda e, pt=pt, oc=oc: e.activation(out=opT[:, oc, :], in_=pt[:], func=AF.Copy,
                                                                       scale=psc_t[:, oc:oc + 1]),
                          reads=[pb, consB], writes=[opTB])
            P.barrier()
            for st in range(NST):
                P.dma("sp", lambda e, st=st: e.dma_start(out=h[st], in_=h1_s[t0 + st * 128:t0 + (st + 1) * 128, :]),
                      writes=[hB[st]])
            oT = uT_t[:, 0:H * T].rearrange("p (c t) -> p c t", t=T)
            oTB = uTB
            ob = uT_t[:, H * T:H * T + H * 128].rearrange("p (h j) -> p h j", j=128)
            s3 = s1
            acc = af32(s3, H * 129).rearrange("p (h j) -> p h j", j=129)
            tm2 = af32(s3 + H * 129, H * 129).rearrange("p (h j) -> p h j", j=129)
            rz = af32(s3 + 2 * H * 129, H)
            assert s3 + 2 * H * 129 + H <= ARENA, "arena"
            accB, tm2B, obB = Buf(), Buf(), Buf()
            for st in range(NST):
                rows = slice(t0 + st * 128, t0 + (st + 1) * 128)
                P.dma("sp", lambda e, rows=rows: e.dma_start(out=acc.rearrange("p h j -> p (h j)"), in_=o_s[0][rows, :]),
                      writes=[accB])
                for g in (1, 2):
                    P.dma("sp", lambda e, rows=rows, g=g: e.dma_start(out=tm2.rearrange("p h j -> p (h j)"), in_=o_s[g][rows, :]),
                          writes=[tm2B])
                    P.add("dve", lambda e: e.tensor_tensor(out=acc, in0=acc, in1=tm2, op=ALU.add),
                          reads=[tm2B], writes=[accB])
                P.add("dve", lambda e: e.reciprocal(out=rz, in_=acc[:, :, 128:129].rearrange("p h o -> p (h o)")),
                      writes=[accB])
                P.add("dve", lambda e: e.tensor_tensor(
                    out=ob, in0=acc[:, :, 0:128], in1=rz.rearrange("p (h o) -> p h o", o=1).to_broadcast([128, H, 128]),
                    op=ALU.mult), reads=[accB], writes=[obB])
                for h0 in range(0, H, 4):
                    pt, pb = pbank()

                    def mm(e, pt=pt, h0=h0):
                        for j in range(4):
                            ins = e.matmul(pt[:, j * 128:(j + 1) * 128], lhsT=ob[:, h0 + j, :], rhs=identb_t[:],
                                           start=True, stop=True)
                        return ins
                    P.add("pe", mm, reads=[obB, consB], writes=[pb])
                    P.add("act", lambda e, pt=pt, h0=h0, st=st: e.activation(
                        out=oT[:, h0:h0 + 4, st * 128:(st + 1) * 128], in_=pt[:].rearrange("p (a b) -> p a b", b=128),
                        func=AF.Copy), reads=[pb], writes=[oTB])

            def mo_lhs(k, st):
                if k < H:
                    return oT[:, k, st * 128:(st + 1) * 128]
                return opT[:, k - H, st * 128:(st + 1) * 128]

            def addh(n, st, pt, pb):
                P.add("dve", lambda e: e.tensor_tensor(out=h[st][:, n * 512:(n + 1) * 512], in0=pt[:],
                                                       in1=h[st][:, n * 512:(n + 1) * 512], op=ALU.add),
                      reads=[pb], writes=[hB[st]])
            r2(mo_lhs, [oTB, opTB], KC, wmo_d, NB, 512, NST, addh)
            for st in range(NST):
                norm_sub(h[st], hB[st], 2, st * 128, JUNK)
            qcT = abf(0, 4 * T // 2).rearrange("p (h t) -> p h t", t=T)
            qcB = Buf()
            ocT = abf(4 * T // 2, 4 * T // 2).rearrange("p (h t) -> p h t", t=T)
            ocB = Buf()
            pc = [abf(4 * T + k * T // 2, T // 2) for k in range(2)]
            pcB = [Buf(), Buf()]
            rzc = af32(4 * T + T, T)
            rzcB = Buf()
            for hh in range(4):
                r1(wcq_d, hh, T, lambda pt, pb, hh=hh: P.add(
                    "act", lambda e: e.activation(out=qcT[:, hh, :], in_=pt[:], func=AF.Copy), reads=[pb], writes=[qcB]))
            for hh in range(4):
                for m in range(2):
                    pt, pb = pbank()
                    P.add("pe", lambda e, pt=pt, hh=hh, m=m: e.matmul(
                        pt[:], lhsT=kmT[:, hh, m * 128:(m + 1) * 128], rhs=qcT[:, hh, :], start=True, stop=True),
                        reads=[memB, qcB], writes=[pb])
                    P.add("act", lambda e, pt=pt, m=m: e.activation(out=pc[m], in_=pt[:], func=AF.Exp, scale=sm),
                          reads=[pb], writes=[pcB[m]])
                po, pob = pbank()
                pz, pzb = pbank()

                def om(e, po=po, hh=hh):
                    for m in range(2):
                        ins = e.matmul(po[:], lhsT=vm[:, m, hh * 128:(hh + 1) * 128], rhs=pc[m], start=(m == 0), stop=(m == 1))
                    return ins

                def zm(e, pz=pz):
                    for m in range(2):
                        ins = e.matmul(pz[:], lhsT=onesb_t[:], rhs=pc[m], start=(m == 0), stop=(m == 1))
                    return ins
                P.add("pe", om, reads=[memB] + pcB, writes=[pob])
                P.add("pe", zm, reads=[consB] + pcB, writes=[pzb])
                P.add("dve", lambda e, pz=pz: e.reciprocal(out=rzc, in_=pz[:]), reads=[pzb], writes=[rzcB])
                P.add("dve", lambda e, po=po, hh=hh: e.tensor_tensor(out=ocT[:, hh, :], in0=po[:], in1=rzc, op=ALU.mult),
                      reads=[pob, rzcB], writes=[ocB])
            r2(lambda k, st: ocT[:, k, st * 128:(st + 1) * 128], [ocB], 4, wco_d, NB, 512, NST, addh)
            for st in range(NST):
                norm_sub(h[st], hB[st], 4, st * 128, JUNK)
            ffn(w2i_d, w2o_d)
            gfin = uT_t[:, KC * T // 2:KC * T].bitcast(F32)
            P.dma("sp", lambda e: e.dma_start(out=gfin, in_=gfin_d.partition_broadcast(128)), writes=[uTB])
            fB = [Buf() for _ in range(NST)]
            for st in range(NST):
                junk = abf(JUNK, D // 2)
                ssq = fin_t[:, st:st + 1]
                P.add("dve", lambda e, ssq=ssq: e.memset(ssq, 0.0), writes=[fB[st]])
                P.add("act", lambda e, st=st, ssq=ssq: e.activation(out=junk, in_=h[st], func=AF.Square, accum_out=ssq),
                      reads=[hB[st]], writes=[fB[st], arB])
                P.add("act", lambda e, ssq=ssq: e.activation(out=ssq, in_=ssq, func=AF.Sqrt, bias=eps_t[:], scale=1.0 / D),
                      reads=[consB], writes=[fB[st]])
                P.add("dve", lambda e, ssq=ssq: e.reciprocal(out=ssq, in_=ssq), writes=[fB[st]])
                P.add("dve", lambda e, st=st, ssq=ssq: e.scalar_tensor_tensor(
                    out=h[st], in0=h[st], scalar=ssq, in1=gfin, op0=ALU.mult, op1=ALU.mult),
                    reads=[fB[st], uTB], writes=[hB[st]])
                P.dma("sp", lambda e, st=st: e.dma_start(out=y_d[t0 + st * 128:t0 + (st + 1) * 128, :], in_=h[st]),
                      reads=[hB[st]], bar=False)
            P.barrier()

        for i in range(NT):
            phaseD(i)
        P.barrier(full=True)
        P.emit(nc, es)
    return nc


def _lhsT_layout(w, KC):
    K, n = w.shape
    return np.ascontiguousarray(w.reshape(KC, 128, n // 128, 128).transpose(2, 1, 0, 3)).reshape(n // 128, 128, KC * 128)


def _rhs_layout(w, cb):
    K, n = w.shape
    return np.ascontiguousarray(w.reshape(K // 128, 128, n // cb, cb).transpose(2, 0, 1, 3))


def prepare(cfg, x, mem, positions, g_ffn1, w_ffn1_in, w_ffn1_out, g_mix, w_mix_in, w_pool, pool_scale, w_mix_out,
            g_cross, g_mem, w_cross_q, w_cross_kv, w_cross_o, g_ffn2, w_ffn2_in, w_ffn2_out, g_final):
    import ml_dtypes
    D, S, KC, PGC = cfg.D, cfg.S, cfg.KC, cfg.PGC
    f = np.float32
    shared = {}
    gs = np.stack([np.asarray(g, f).reshape(D) for g in (g_ffn1[0], g_mix[0], g_cross[0], g_mem[0], g_ffn2[0])])
    shared["gfm"] = np.ascontiguousarray(gs.reshape(5, KC, 128).transpose(2, 0, 1)).reshape(128, 5 * KC)
    shared["gfin"] = np.asarray(g_final, f).reshape(1, D)
    shared["psc"] = np.ascontiguousarray(np.asarray(pool_scale[0], f).reshape(4 * PGC, 128).T)
    kk = np.arange(128)[:, None]
    qq = np.arange(128)[None, :]
    cur = (kk <= qq).astype(f)
    prev = (kk >= qq).astype(f)
    zero = np.zeros_like(prev)
    cm_all = np.concatenate([cur, prev, cur, prev], axis=1)
    cmasks = [np.concatenate([cm_all, np.concatenate([cur, prev * par, cur, prev * par], axis=1)], axis=1).astype(ml_dtypes.bfloat16)
              for par in (0.0, 1.0)]
    inv = (1.0 / (ROPE_THETA ** (np.arange(0, 128, 2, dtype=f) / f(128)))).astype(f)
    cf32s = []
    for par in (0, 1):
        if par == 0:
            invc = np.stack([1.0 / np.minimum(np.arange(16) + 1, w) for w in POOL_W]).astype(f).reshape(64)
        else:
            invc = np.stack([np.full(16, 1.0 / w) for w in POOL_W]).astype(f).reshape(64)
        flags = np.array([1.0 - par, float(par)], dtype=f)
        cf32s.append(np.concatenate([np.eye(128, dtype=f), np.tile(inv[None], (128, 1)), np.tile(invc[None], (128, 1)),
                                     np.tile(flags[None], (128, 1))], axis=1))
    shared["w1i"] = _lhsT_layout(np.asarray(w_ffn1_in[0], f), KC)
    shared["w1o"] = _rhs_layout(np.asarray(w_ffn1_out[0], f), 512)
    shared["w2i"] = _lhsT_layout(np.asarray(w_ffn2_in[0], f), KC)
    shared["w2o"] = _rhs_layout(np.asarray(w_ffn2_out[0], f), 512)
    wmi = np.asarray(w_mix_in[0], f)
    shared["wqkv"] = _rhs_layout(wmi[:, :cfg.ATTN_IN], cfg.CB)
    shared["wzp"] = _lhsT_layout(wmi[:, cfg.ATTN_IN:], KC)
    wp = np.asarray(w_pool[0], f)
    shared["wpl"] = np.ascontiguousarray(wp.reshape(4, PGC, 128, cfg.PG).transpose(0, 2, 1, 3)).reshape(4, 128, PGC * cfg.PG)
    shared["wmo"] = _rhs_layout(np.asarray(w_mix_out[0], f), 512)
    shared["wcq"] = _lhsT_layout(np.asarray(w_cross_q[0], f), KC)
    wkv = np.asarray(w_cross_kv[0], f)
    shared["wck"] = _lhsT_layout(wkv[:, :512], KC)
    shared["wcv"] = _rhs_layout(wkv[:, 512:], 512)
    shared["wco"] = _rhs_layout(np.asarray(w_cross_o[0], f), 512)
    in_maps = []
    for b in range(cfg.B):
        for par in (0, 1):
            m = dict(shared)
            sl = slice(par * S, (par + 1) * S)
            m["x"] = np.ascontiguousarray(np.asarray(x[b], f)[sl])
            m["mem"] = np.ascontiguousarray(np.asarray(mem[b], f))
            m["pos"] = np.ascontiguousarray(np.asarray(positions[b], np.int32)[sl].reshape(S // 128, 128).T)
            m["cmask"] = cmasks[par]
            m["cf32"] = cf32s[par]
            in_maps.append(m)
    return in_maps


def kernel(**inputs):
    x = inputs["x"]
    cfg = Cfg(D=x.shape[2], S=x.shape[1], B=x.shape[0])
    nc = build(cfg)
    in_maps = prepare(cfg, **inputs)
    res = run_bass_kernel_spmd(nc, in_maps, core_ids=list(range(2 * cfg.B)))
    ys = [np.asarray(r["y"], np.float32) for r in res.results]
    return np.stack([np.concatenate([ys[2 * b], ys[2 * b + 1]], axis=0) for b in range(cfg.B)], axis=0)
```

```python
import math
from contextlib import ExitStack

import numpy as np
import concourse.bass as bass
import concourse.mybir as mybir
from concourse.bass_utils import run_bass_kernel_spmd

F32 = mybir.dt.float32
BF16 = mybir.dt.bfloat16
I32 = mybir.dt.int32
AF = mybir.ActivationFunctionType
ALU = mybir.AluOpType

EPS = 1e-6
ROPE_THETA = 10000.0
DILS = (1, 4, 16)
POOL_W = (2, 4, 8, 16)
MEM = 256
T = 512
NST = 4


class Cfg:
    def __init__(self, D=4096, S=4096, B=4):
        self.D, self.SF, self.B = D, S, B
        self.S = S // 2
        S = self.S
        self.KC = D // 128
        self.AW = D // 2
        self.H = self.AW // 128
        self.PW = D - self.AW
        self.PG = self.PW // 4
        self.PGC = self.PG // 128
        self.F = ((8 * D // 3 + 255) // 256) * 256
        self.FC = self.F // 128
        self.FH = self.FC // 2
        self.NB = D // 512
        self.NT = S // T
        self.CB = min(512, self.AW)
        self.QB = self.AW // self.CB
        self.ATTN_IN = 9 * self.AW


class Buf:
    __slots__ = ("w", "rs")

    def __init__(self):
        self.w = None
        self.rs = []


class Op:
    __slots__ = ("eng", "fn", "deps", "sig", "kind", "sem", "val", "prevslot")

    def __init__(self, eng, fn, kind):
        self.eng, self.fn, self.kind = eng, fn, kind
        self.deps = []
        self.sig = False
        self.sem = None
        self.val = 0
        self.prevslot = None


ENGS = ("pe", "act", "dve", "pool", "sp")
KQ = 8


class Prog:
    def __init__(self):
        self.ops = {e: [] for e in ENGS}
        self.lastc = {e: None for e in ENGS}
        self.dmas = {e: [] for e in ENGS}
        self.ccs = []
        self.nobar = set()

    def add(self, eng, fn, reads=(), writes=(), kind="c"):
        op = Op(eng, fn, kind)
        deps = []
        for b in reads:
            if b.w is not None:
                deps.append(b.w)
        for b in writes:
            if b.w is not None:
                deps.append(b.w)
            deps.extend(b.rs)
        for b in reads:
            b.rs.append(op)
        for b in writes:
            b.w = op
            b.rs = []
        seen = set()
        for d in deps:
            if id(d) in seen or d is op:
                continue
            seen.add(id(d))
            if d.eng == "pe" and eng == "pe" and d.kind == "c" and kind == "c":
                continue
            op.deps.append(d)
            d.sig = True
        self.ops[eng].append(op)
        if kind == "c":
            self.lastc[eng] = op
        elif kind == "d":
            self.dmas[eng].append(op)
        return op

    def dma(self, eng, fn, reads=(), writes=(), bar=True):
        op = self.add(eng, fn, reads, writes, kind="d")
        if not bar:
            self.nobar.add(id(op))
        return op

    def cc(self, fn, writes=(), first=True):
        if first:
            self.barrier(full=True)
        op = Op("pool", fn, "cc")
        if self.ccs:
            op.deps.append(self.ccs[-1])
        for b in writes:
            b.w = op
            b.rs = []
        self.ops["pool"].append(op)
        self.ccs.append(op)
        return op

    def barrier(self, full=False):
        deps = []
        for e in ENGS:
            if self.lastc[e] is not None:
                deps.append(self.lastc[e])
            deps.extend(d for d in self.dmas[e][-KQ:] if full or id(d) not in self.nobar)
        deps.extend(self.ccs)
        for e in ENGS:
            op = Op(e, None, "f")
            for d in deps:
                op.deps.append(d)
                d.sig = True
            self.ops[e].append(op)

    def emit(self, nc, es):
        semh = {}
        for e in ENGS:
            semh[("c", e)] = es.enter_context(nc.semaphore(f"c_{e}"))
            if e in ("sp", "pool"):
                for k in range(KQ):
                    semh[("d", e, k)] = es.enter_context(nc.semaphore(f"d_{e}{k}"))
        for k, op in enumerate(self.ccs):
            op.sem = ("cc", k)
            op.val = 1
            semh[op.sem] = es.enter_context(nc.semaphore(f"cc_{k}"))
        for e in ENGS:
            cnt = 0
            dl = []
            for op in self.ops[e]:
                if op.kind == "c" and op.sig:
                    cnt += 1
                    op.sem = ("c", e)
                    op.val = cnt
                elif op.kind == "d":
                    n = len(dl)
                    op.sem = ("d", e, n % KQ)
                    op.val = 16 * (n // KQ + 1)
                    if n >= KQ:
                        op.prevslot = dl[n - KQ]
                    dl.append(op)
        block = es.enter_context(nc.Block())

        def run(e, eng):
            waited = {}
            for op in self.ops[e]:
                deps = op.deps if op.prevslot is None else op.deps + [op.prevslot]
                if len(deps) > 1:
                    deps = sorted(deps, key=lambda d: -d.val)
                for d in deps:
                    if waited.get(d.sem, 0) >= d.val:
                        continue
                    eng.wait_ge(semh[d.sem], d.val)
                    waited[d.sem] = d.val
                if op.kind == "f":
                    continue
                ins = op.fn(eng)
                if op.kind == "d":
                    ins.then_inc(semh[op.sem], 16)
                elif op.kind == "cc":
                    ins.then_inc(semh[op.sem])
                elif op.sig:
                    ins.then_inc(semh[op.sem], 1)

        block.tensor(lambda eng: run("pe", eng))
        block.scalar(lambda eng: run("act", eng))
        block.vector(lambda eng: run("dve", eng))
        block.gpsimd(lambda eng: run("pool", eng))
        block.sync(lambda eng: run("sp", eng))


def build(cfg):
    D, S, KC, H, FC, FH, NB, NT = cfg.D, cfg.S, cfg.KC, cfg.H, cfg.FC, cfg.FH, cfg.NB, cfg.NT
    AW, CB, QB, PGC = cfg.AW, cfg.CB, cfg.QB, cfg.PGC
    NPC = 4 * PGC
    nc = bass.Bass("TRN2", target_bir_lowering=False)
    P = Prog()

    def din(name, shape, dt=F32):
        return nc.dram_tensor(name, list(shape), dt, kind="ExternalInput").ap()

    x_d = din("x", [S, D])
    mem_d = din("mem", [MEM, D])
    pos_d = din("pos", [128, S // 128], I32)
    gfm_d = din("gfm", [128, 5 * KC])
    gfin_d = din("gfin", [1, D])
    psc_d = din("psc", [128, NPC])
    cmask_d = din("cmask", [128, 1024], BF16)
    cf32_d = din("cf32", [128, 128 + 64 + 64 + 2])
    w1i_d = din("w1i", [2 * FC, 128, KC * 128])
    w1o_d = din("w1o", [NB, FC, 128, 512])
    w2i_d = din("w2i", [2 * FC, 128, KC * 128])
    w2o_d = din("w2o", [NB, FC, 128, 512])
    wqkv_d = din("wqkv", [9 * QB, KC, 128, CB])
    wzp_d = din("wzp", [NPC, 128, KC * 128])
    wpl_d = din("wpl", [4, 128, PGC * cfg.PG])
    wmo_d = din("wmo", [NB, KC, 128, 512])
    wcq_d = din("wcq", [4, 128, KC * 128])
    wck_d = din("wck", [4, 128, KC * 128])
    wcv_d = din("wcv", [1, KC, 128, 512])
    wco_d = din("wco", [NB, 4, 128, 512])
    y_d = nc.dram_tensor("y", [S, D], F32, kind="ExternalOutput").ap()

    h1_s = nc.dram_tensor("h1_s", [S, D], F32).ap()
    qkv_s = [[nc.dram_tensor(f"qkv_s{g}{r}", [S, AW], BF16).ap() for r in range(3)] for g in range(3)]
    zp_s = nc.dram_tensor("zp_s", [NPC, 128, S], F32).ap()
    o_s = [nc.dram_tensor(f"o_s{g}", [S, H * 129], F32).ap() for g in range(3)]
    XROWS = 2 * S + 4 * T
    XR = {(2, 1): 0, (2, 2): S, (1, 1): 2 * S, (1, 2): 2 * S + T, (0, 1): 2 * S + 2 * T, (0, 2): 2 * S + 3 * T}
    xin_h = nc.dram_tensor("xin", [XROWS, AW], BF16)
    xout_h = nc.dram_tensor("xout", [XROWS, AW], BF16)
    xzin_h = nc.dram_tensor("xzin", [NPC * 128, 16], F32)
    xzout_h = nc.dram_tensor("xzout", [NPC * 128, 16], F32)
    xin, xout = xin_h.ap(), xout_h.ap()
    xzin = xzin_h.ap().rearrange("(c p) t -> c p t", p=128)
    xzout = xzout_h.ap().rearrange("(c p) t -> c p t", p=128)
    pairs = [[2 * k, 2 * k + 1] for k in range(cfg.B)]

    es = ExitStack()
    with es:
        def sb(name, shape, dt):
            return es.enter_context(nc.sbuf_tensor("s_" + name, list(shape), dt))

        h_t = sb("h", [128, NST * D], F32)
        uT_t = sb("uT", [128, KC * T], BF16)
        WSL = 3
        wr_t = [sb(f"wr{i}", [128, 4096], BF16) for i in range(WSL)]
        ARENA = 13312
        ar_t = sb("arena", [128, ARENA], F32)
        cons_t = sb("consf", [128, 258], F32)
        gfm_t = sb("gfm", [128, 5 * KC], F32)
        psc_t = sb("psc", [128, NPC], F32)
        cmask_t = sb("cmask", [128, 1024], BF16)
        identb_t = sb("identb", [128, 128], BF16)
        onesb_t = sb("onesb", [128, 128], BF16)
        posi_t = sb("posi", [128, S // 128], I32)
        posf_t = sb("posf", [128, S // 128], F32)
        cs_t = sb("cs", [128, 2 * NST * 64], F32)
        st_t = sb("stat", [128, 256], F32)
        ri_t = sb("rint", [128, 64], I32)
        diag_t = sb("diag", [128, 4 * 128], F32)
        nst_t = sb("nstat", [128, 8], F32)
        kmT_t = sb("kmT", [128, 4 * MEM], BF16)
        vm_t = sb("vm", [128, 2 * 512], BF16)
        negpi_t = sb("negpi", [128, 1], F32)
        eps_t = sb("epst", [128, 1], F32)
        fin_t = sb("fint", [128, 4], F32)
        ps_t = [es.enter_context(nc.psum_tensor(f"ps{i}", [128, 512], F32)) for i in range(8)]

        h = [h_t[:, st * D:(st + 1) * D] for st in range(NST)]
        hB = [Buf() for _ in range(NST)]
        uT = uT_t[:].rearrange("p (k t) -> p k t", t=T)
        uTB = Buf()
        wrB = [Buf() for _ in range(WSL)]
        psB = [Buf() for _ in range(8)]
        consB, statB, diagB, csB, memB, arB = Buf(), Buf(), [Buf() for _ in range(4)], Buf(), Buf(), Buf()
        nstB = [Buf() for _ in range(4)]
        xoutB, xzB = Buf(), Buf()
        identf = cons_t[:, 0:128]
        invf = cons_t[:, 128:192]
        invc = cons_t[:, 192:256]
        gfm = gfm_t[:].rearrange("p (n k) -> p n k", k=KC)
        cmask = cmask_t[:, 0:512]
        cmask1 = cmask_t[:, 512:1024]
        flag0 = cons_t[:, 256:257]
        flag1 = cons_t[:, 257:258]
        cs = cs_t[:].rearrange("p (c s j) -> p c s j", c=2, s=NST)

        state = {"w": 0, "ps": 0, "diag": 0}

        def wslot():
            i = state["w"] % WSL
            state["w"] += 1
            return wr_t[i], wrB[i]

        def pbank():
            i = state["ps"] % 8
            state["ps"] += 1
            return ps_t[i], psB[i]

        def af32(off, n, t=None):
            t = ar_t if t is None else t
            return t[:, off:off + n]

        def abf(off, n, t=None):
            t = ar_t if t is None else t
            return t[:, off:off + n].bitcast(BF16)

        for (dst, src) in ((cons_t[:], cf32_d), (gfm_t[:], gfm_d), (psc_t[:], psc_d),
                           (cmask_t[:], cmask_d), (posi_t[:], pos_d)):
            P.dma("sp", lambda e, d=dst, s=src: e.dma_start(out=d, in_=s), writes=[consB])
        P.add("dve", lambda e: e.tensor_copy(out=posf_t[:], in_=posi_t[:]), reads=[consB], writes=[consB])
        P.add("dve", lambda e: e.tensor_copy(out=identb_t[:], in_=identf), reads=[consB], writes=[consB])
        P.add("dve", lambda e: e.memset(onesb_t[:], 1.0), writes=[consB])
        P.add("dve", lambda e: e.memset(negpi_t[:], -math.pi), writes=[consB])
        P.add("dve", lambda e: e.memset(eps_t[:], EPS), writes=[consB])

        def norm_sub(hap, hb, gi, col0, junk_off):
            junk = abf(junk_off, D // 2)
            di = state["diag"] % 4
            state["diag"] += 1
            ssq = nst_t[:, 2 * di:2 * di + 1]
            rstd = nst_t[:, 2 * di + 1:2 * di + 2]
            nB = nstB[di]
            dg = diag_t[:, di * 128:(di + 1) * 128]
            P.add("dve", lambda e: e.memset(ssq, 0.0), writes=[nB])
            P.add("act", lambda e: e.activation(out=junk, in_=hap, func=AF.Square, accum_out=ssq),
                  reads=[hb], writes=[nB, arB])
            P.add("act", lambda e: e.activation(out=rstd, in_=ssq, func=AF.Sqrt, bias=eps_t[:], scale=1.0 / D),
                  reads=[consB], writes=[nB])
            P.add("dve", lambda e: e.reciprocal(out=rstd, in_=rstd), writes=[nB])
            P.add("dve", lambda e: e.tensor_scalar_mul(out=dg, in0=identf, scalar1=rstd),
                  reads=[nB, consB], writes=[diagB[di]])
            for k0 in range(0, KC, 4):
                pt, pb = pbank()

                def mm(e, k0=k0, pt=pt):
                    for j in range(4):
                        ins = e.matmul(pt[:, j * 128:(j + 1) * 128], lhsT=hap[:, (k0 + j) * 128:(k0 + j + 1) * 128],
                                       rhs=dg, start=True, stop=True)
                    return ins
                P.add("pe", mm, reads=[hb, diagB[di]], writes=[pb])
                if gi is None:
                    P.add("dve", lambda e, k0=k0, pt=pt: e.tensor_copy(
                        out=uT[:, k0:k0 + 4, col0:col0 + 128], in_=pt[:].rearrange("p (a b) -> p a b", b=128)),
                        reads=[pb], writes=[uTB])
                else:
                    P.add("dve", lambda e, k0=k0, pt=pt: e.tensor_tensor(
                        out=uT[:, k0:k0 + 4, col0:col0 + 128], in0=pt[:].rearrange("p (a b) -> p a b", b=128),
                        in1=gfm[:, gi, k0:k0 + 4].to_broadcast([128, 4, 128]), op=ALU.mult),
                        reads=[pb, consB], writes=[uTB])
            return rstd

        def r1(wd, c, ncols, consume, src=None, srcB=None, nk=None):
            src = uT if src is None else src
            srcB = uTB if srcB is None else srcB
            nk = KC if nk is None else nk
            wt, wb = wslot()
            P.dma("pool", lambda e: e.dma_start(out=wt[:, :nk * 128], in_=wd[c]), writes=[wb])
            pt, pb = pbank()
            wv = wt[:, :nk * 128].rearrange("p (k j) -> p k j", j=128)

            def mm(e):
                for k in range(nk):
                    ins = e.matmul(pt[:, :ncols], lhsT=wv[:, k, :], rhs=src[:, k, :ncols], start=(k == 0), stop=(k == nk - 1))
                return ins
            P.add("pe", mm, reads=[wb, srcB], writes=[pb])
            consume(pt, pb)

        def r2(lhs, lhsB, nK, wd, nblocks, cb, nst, consume):
            GK = 4096 // cb
            for n in range(nblocks):
                banks = [pbank() for _ in range(nst)]
                for k0 in range(0, nK, GK):
                    gk = min(GK, nK - k0)
                    wt, wb = wslot()
                    P.dma("pool", lambda e, wt=wt, n=n, k0=k0, gk=gk: e.dma_start(
                        out=wt[:, :gk * cb].rearrange("p (k j) -> p k j", j=cb),
                        in_=wd[n, k0:k0 + gk].rearrange("k p j -> p k j")), writes=[wb])

                    def mm(e, wt=wt, k0=k0, gk=gk, banks=banks):
                        wv = wt[:, :gk * cb].rearrange("p (k j) -> p k j", j=cb)
                        for st in range(nst):
                            for k in range(gk):
                                ins = e.matmul(banks[st][0][:, :cb], lhsT=lhs(k0 + k, st), rhs=wv[:, k, :],
                                               start=(k0 + k == 0), stop=(k0 + k == nK - 1))
                        return ins
                    P.add("pe", mm, reads=[wb] + list(lhsB), writes=[b for (_, b) in banks])
                for st in range(nst):
                    consume(n, st, banks[st][0], banks[st][1])

        def uT_lhs(k, st):
            return uT[:, k, st * 128:(st + 1) * 128]

        def ffn(wi_d, wo_d):
            hid = abf(0, FH * T // 2).rearrange("p (c t) -> p c t", t=T)
            hidB = Buf()
            sil = [af32(FH * T // 2 + i * T, T) for i in range(2)]
            silB = [Buf(), Buf()]
            for half in range(2):
                for ci in range(FH):
                    c = half * FH + ci
                    got = {}
                    r1(wi_d, c, T, lambda pt, pb: got.update(a=(pt, pb)))
                    r1(wi_d, FC + c, T, lambda pt, pb: got.update(b=(pt, pb)))
                    (pa, pab), (pbk, pbb) = got["a"], got["b"]
                    si = ci % 2
                    P.add("act", lambda e, pa=pa, si=si: e.activation(out=sil[si], in_=pa[:], func=AF.Silu),
                          reads=[pab], writes=[silB[si]])
                    P.add("dve", lambda e, pbk=pbk, si=si, ci=ci: e.tensor_tensor(
                        out=hid[:, ci, :], in0=pbk[:], in1=sil[si], op=ALU.mult),
                        reads=[pbb, silB[si]], writes=[hidB])

                def cons(n, st, pt, pb):
                    P.add("dve", lambda e: e.scalar_tensor_tensor(
                        out=h[st][:, n * 512:(n + 1) * 512], in0=pt[:], scalar=0.5,
                        in1=h[st][:, n * 512:(n + 1) * 512], op0=ALU.mult, op1=ALU.add),
                        reads=[pb], writes=[hB[st]])
                wo_half = wo_d[:, half * FH:(half + 1) * FH]
                r2(lambda k, st: hid[:, k, st * 128:(st + 1) * 128], [hidB], FH, wo_half, NB, 512, NST, cons)

        JUNK = ARENA - D // 2

        for m in range(2):
            P.dma("sp", lambda e, m=m: e.dma_start(out=h[m], in_=mem_d[m * 128:(m + 1) * 128, :]), writes=[hB[m]])
            norm_sub(h[m], hB[m], 3, m * 128, JUNK)
        kmT = kmT_t[:].rearrange("p (h m) -> p h m", m=MEM)
        for hh in range(4):
            r1(wck_d, hh, MEM, lambda pt, pb, hh=hh: P.add(
                "act", lambda e: e.activation(out=kmT[:, hh, :], in_=pt[:, :MEM], func=AF.Copy),
                reads=[pb], writes=[memB]))
        vm = vm_t[:].rearrange("p (m c) -> p m c", c=512)
        r2(uT_lhs, [uTB], KC, wcv_d, 1, 512, 2, lambda n, st, pt, pb: P.add(
            "act", lambda e: e.activation(out=vm[:, st, :], in_=pt[:], func=AF.Copy), reads=[pb], writes=[memB]))
        P.barrier()

        stg_off = FH * T // 2 + 2 * T
        def phaseA(i):
            t0 = i * T
            for st in range(NST):
                P.dma("sp", lambda e, st=st: e.dma_start(out=h[st], in_=x_d[t0 + st * 128:t0 + (st + 1) * 128, :]),
                      writes=[hB[st]])
                norm_sub(h[st], hB[st], 0, st * 128, JUNK)
            ffn(w1i_d, w1o_d)
            for st in range(NST):
                P.dma("sp", lambda e, st=st: e.dma_start(out=h1_s[t0 + st * 128:t0 + (st + 1) * 128, :], in_=h[st]),
                      reads=[hB[st]])
                norm_sub(h[st], hB[st], 1, st * 128, JUNK)
            for st in range(NST):
                j = i * NST + st
                yv = st_t[:, 0:64]
                yf = st_t[:, 64:128]
                P.add("dve", lambda e, j=j: e.tensor_scalar(out=yv, in0=invf, scalar1=posf_t[:, j:j + 1],
                                                            scalar2=1.0 / (2 * math.pi), op0=ALU.mult, op1=ALU.mult),
                      reads=[consB], writes=[statB])
                P.add("dve", lambda e: e.tensor_copy(out=ri_t[:], in_=yv), writes=[statB])
                P.add("dve", lambda e: e.tensor_copy(out=yf, in_=ri_t[:]), writes=[statB])
                P.add("dve", lambda e: e.tensor_tensor(out=yv, in0=yv, in1=yf, op=ALU.subtract), writes=[statB])
                sa = st_t[:, 128:192]
                sbh = st_t[:, 192:256]
                P.add("act", lambda e: e.activation(out=sa, in_=yv, func=AF.Sin, scale=math.pi), writes=[statB])
                P.add("act", lambda e: e.activation(out=sbh, in_=yv, func=AF.Sin, scale=math.pi / 2), writes=[statB])
                P.add("dve", lambda e: e.tensor_tensor(out=sbh, in0=sbh, in1=sbh, op=ALU.mult), writes=[statB])
                P.add("dve", lambda e: e.tensor_scalar(out=sbh, in0=sbh, scalar1=-2.0, scalar2=1.0, op0=ALU.mult, op1=ALU.add),
                      writes=[statB])
                P.add("dve", lambda e, st=st: e.scalar_tensor_tensor(out=cs[:, 1, st, :], in0=sa, scalar=2.0, in1=sbh,
                                                                     op0=ALU.mult, op1=ALU.mult), writes=[statB, csB])
                P.add("dve", lambda e: e.tensor_tensor(out=sa, in0=sa, in1=sa, op=ALU.mult), writes=[statB])
                P.add("dve", lambda e, st=st: e.tensor_scalar(out=cs[:, 0, st, :], in0=sa, scalar1=-2.0, scalar2=1.0,
                                                              op0=ALU.mult, op1=ALU.add), writes=[statB, csB])
            HB = CB // 128
            stg = [abf(i2 * 256, 256) for i2 in range(4)]
            stgB = [Buf() for _ in range(4)]
            tmp = [af32(1024 + i2 * 256, 256) for i2 in range(4)]
            tmpB = Buf()
            sc = {"n": 0, "x": 0}
            stg2 = [abf(3072 + i2 * 256, 256) for i2 in range(2)]
            stg2B = [Buf(), Buf()]
            hz = [af32(3584 + i2 * 16, 16) for i2 in range(2)]
            hzB = [Buf(), Buf()]

            def qkv_cons(n, st, pt, pb):
                g, r, qd = n // (3 * QB), (n // QB) % 3, n % QB
                si = sc["n"] % 4
                sc["n"] += 1
                so, sB = stg[si][:, :CB], stgB[si]
                if r == 2:
                    P.add("act", lambda e: e.activation(out=so, in_=pt[:, :CB], func=AF.Copy), reads=[pb], writes=[sB])
                else:
                    pv = pt[:, :CB].rearrange("p (h two j) -> p h two j", two=2, j=64)
                    ov = so.rearrange("p (h two j) -> p h two j", two=2, j=64)
                    cosb = cs[:, 0, st, :].rearrange("p (o j) -> p o j", o=1).to_broadcast([128, HB, 64])
                    sinb = cs[:, 1, st, :].rearrange("p (o j) -> p o j", o=1).to_broadcast([128, HB, 64])
                    tv = [tmp[k][:, :HB * 64].rearrange("p (h j) -> p h j", j=64) for k in range(4)]
                    P.add("dve", lambda e: e.tensor_tensor(out=tv[0], in0=pv[:, :, 0, :], in1=cosb, op=ALU.mult),
                          reads=[pb, csB], writes=[tmpB])
                    P.add("dve", lambda e: e.tensor_tensor(out=tv[1], in0=pv[:, :, 1, :], in1=sinb, op=ALU.mult),
                          reads=[pb, csB], writes=[tmpB])
                    P.add("dve", lambda e: e.tensor_tensor(out=tv[2], in0=pv[:, :, 1, :], in1=cosb, op=ALU.mult),
                          reads=[pb, csB], writes=[tmpB])
                    P.add("dve", lambda e: e.tensor_tensor(out=tv[3], in0=pv[:, :, 0, :], in1=sinb, op=ALU.mult),
                          reads=[pb, csB], writes=[tmpB])
                    P.add("dve", lambda e: e.tensor_tensor(out=ov[:, :, 0, :], in0=tv[0], in1=tv[1], op=ALU.subtract),
                          reads=[tmpB], writes=[sB])
                    P.add("dve", lambda e: e.tensor_tensor(out=ov[:, :, 1, :], in0=tv[2], in1=tv[3], op=ALU.add),
                          reads=[tmpB], writes=[sB])
                P.dma("sp", lambda e: e.dma_start(
                    out=qkv_s[g][r][t0 + st * 128:t0 + (st + 1) * 128, qd * CB:(qd + 1) * CB], in_=so), reads=[sB])
                if r > 0 and (g == 2 or i == NT - 1):
                    xi = sc["x"] % 2
                    sc["x"] += 1
                    so2 = stg2[xi][:, :CB]
                    P.add("dve", lambda e: e.tensor_scalar_mul(out=so2, in0=so, scalar1=flag0), reads=[sB, consB],
                          writes=[stg2B[xi]])
                    r0 = XR[(g, r)] + (t0 if g == 2 else 0) + st * 128
                    P.dma("sp", lambda e: e.dma_start(out=xin[r0:r0 + 128, qd * CB:(qd + 1) * CB], in_=so2),
                          reads=[stg2B[xi]])
            r2(uT_lhs, [uTB], KC, wqkv_d, 9 * QB, CB, NST, qkv_cons)
            zst = [af32(2048 + i2 * T, T) for i2 in range(2)]
            zstB = [Buf(), Buf()]
            for c in range(NPC):
                def zcons(pt, pb, c=c):
                    zi = c % 2
                    P.add("act", lambda e: e.activation(out=zst[zi], in_=pt[:], func=AF.Copy), reads=[pb], writes=[zstB[zi]])
                    P.dma("sp", lambda e: e.dma_start(out=zp_s[c, :, t0:t0 + T], in_=zst[zi]), reads=[zstB[zi]])
                    if i == NT - 1:
                        P.add("dve", lambda e: e.tensor_scalar_mul(out=hz[zi], in0=zst[zi][:, T - 16:T], scalar1=flag0),
                              reads=[zstB[zi], consB], writes=[hzB[zi]])
                        P.dma("sp", lambda e: e.dma_start(out=xzin[c], in_=hz[zi]), reads=[hzB[zi]])
                r1(wzp_d, c, T, zcons)
            P.barrier()

        for i in range(NT):
            phaseA(i)
        CR = max(128, (4 * 1024 * 1024) // (AW * 2))
        for r0 in range(0, XROWS, CR):
            r1_ = min(XROWS, r0 + CR)
            P.cc(lambda e, r0=r0, r1_=r1_: e.collective_compute(
                "AllReduce", ALU.add, replica_groups=pairs,
                ins=[xin_h.ap()[r0:r1_].opt()], outs=[xout_h.ap()[r0:r1_].opt()]), writes=[xoutB], first=(r0 == 0))
        P.cc(lambda e: e.collective_compute("AllReduce", ALU.add, replica_groups=pairs,
                                            ins=[xzin_h.ap().opt()], outs=[xzout_h.ap().opt()]), writes=[xzB], first=False)

        sm = 1.0 / math.sqrt(128.0)
        NR = 3
        QW = AW // 2
        off = 0
        NQ = 4
        qt = []
        for i2 in range(NQ):
            qt.append(abf(off, QW)); off += QW
        kt = []
        for i2 in range(NQ):
            kt.append(abf(off, QW)); off += QW
        pT = []
        for i2 in range(3):
            pT.append(abf(off, 256)); off += 256
        assert off <= ARENA, off
        off = 0
        VW = (H * 129 + 1) // 2
        vt = []
        for i2 in range(NR):
            vt.append(abf(off, VW, h_t)[:, :H * 129].rearrange("p (h j) -> p h j", j=129)); off += VW
        qT = []
        for i2 in range(2):
            qT.append(abf(off, QW, h_t).rearrange("p (h j) -> p h j", j=128)); off += QW
        kT = []
        for i2 in range(NR):
            kT.append(abf(off, QW, h_t).rearrange("p (h j) -> p h j", j=128)); off += QW
        ot = []
        for i2 in range(2):
            ot.append(af32(off, H * 129, h_t).rearrange("p (h j) -> p h j", j=129)); off += H * 129
        assert off <= NST * D, off
        qtB, ktB = [Buf() for _ in range(NQ)], [Buf() for _ in range(NQ)]
        vtB = [Buf() for _ in range(NR)]
        NG4 = max(1, H // 4)
        qTB = [[Buf() for _ in range(NG4)] for _ in range(2)]
        kTB = [[Buf() for _ in range(NG4)] for _ in range(NR)]
        pTB = [Buf() for _ in range(3)]
        otB = [Buf(), Buf()]
        for i2 in range(NR):
            P.add("dve", lambda e, i2=i2: e.memset(vt[i2][:, :, 128:129], 1.0), writes=[vtB[i2]])
        blk = 0
        pcount = 0
        for g, dil in enumerate(DILS):
            L = S // dil
            qv_own = [qkv_s[g][r].rearrange("(a d) c -> d a c", d=dil) for r in range(3)]
            nprev = S if g == 2 else T
            qv_prev = [None] + [xout[XR[(g, r)]:XR[(g, r)] + nprev].rearrange("(a d) c -> d a c", d=dil) for r in (1, 2)]
            ov_d = o_s[g].rearrange("(a d) c -> d a c", d=dil)
            for res in range(dil):
                nbq = L // 128
                if g == 0:
                    order = [(0, True)] + [(bq, False) for bq in range(1, nbq)] + [(-1, True), (0, False)]
                else:
                    order = [(-1, True)] + [(bq, False) for bq in range(nbq)]
                for (b, kvonly) in order:
                    q2, k3 = blk % 2, blk % NR
                    q4 = blk % NQ
                    kprev = (blk - 1) % NR
                    if b < 0:
                        qv = qv_prev
                        rows = slice(nprev // dil - 128, nprev // dil)
                    else:
                        qv = qv_own
                        rows = slice(b * 128, (b + 1) * 128)
                    xr = [xoutB] if b < 0 else []
                    if not kvonly:
                        P.dma("sp", lambda e, q4=q4, res=res, rows=rows, qv=qv: e.dma_start(out=qt[q4], in_=qv[0][res, rows, :]),
                              writes=[qtB[q4]])
                    P.dma("sp", lambda e, q4=q4, res=res, rows=rows, qv=qv: e.dma_start(out=kt[q4], in_=qv[1][res, rows, :]),
                          reads=xr, writes=[ktB[q4]])
                    P.dma("sp", lambda e, k3=k3, res=res, rows=rows, qv=qv: e.dma_start(
                        out=vt[k3][:, :, 0:128], in_=qv[2][res, rows, :].rearrange("p (h j) -> p h j", j=128)),
                        reads=xr, writes=[vtB[k3]])
                    tl = [(kt[q4], ktB[q4], kT[k3], kTB[k3])]
                    if not kvonly:
                        tl.insert(0, (qt[q4], qtB[q4], qT[q2], qTB[q2]))
                    for (srct, srcB, dstT, dstB) in tl:
                        for h0 in range(0, H, 4):
                            pt, pb = pbank()

                            def mm(e, pt=pt, h0=h0, srct=srct):
                                for j in range(4):
                                    ins = e.matmul(pt[:, j * 128:(j + 1) * 128], lhsT=srct[:, (h0 + j) * 128:(h0 + j + 1) * 128],
                                                   rhs=identb_t[:], start=True, stop=True)
                                return ins
                            P.add("pe", mm, reads=[srcB, consB], writes=[pb])
                            if (h0 // 4) % 2 == 0:
                                P.add("act", lambda e, pt=pt, h0=h0, dstT=dstT: e.activation(
                                    out=dstT[:, h0:h0 + 4, :], in_=pt[:].rearrange("p (a b) -> p a b", b=128), func=AF.Copy),
                                    reads=[pb], writes=[dstB[h0 // 4]])
                            else:
                                P.add("dve", lambda e, pt=pt, h0=h0, dstT=dstT: e.tensor_copy(
                                    out=dstT[:, h0:h0 + 4, :], in_=pt[:].rearrange("p (a b) -> p a b", b=128)),
                                    reads=[pb], writes=[dstB[h0 // 4]])
                    if kvonly:
                        blk += 1
                        continue
                    o2 = blk % 2
                    nblk = 2
                    cm = cmask1 if b == 0 else cmask
                    def stage1(h0):
                        nonlocal pcount
                        pt, pb = pbank()
                        pi = pcount % 3
                        pcount += 1

                        def smm(e, pt=pt, h0=h0, q2=q2, k3=k3, kprev=kprev):
                            for j in range(2):
                                ins = e.matmul(pt[:, j * 256:j * 256 + 128], lhsT=kT[k3][:, h0 + j, :], rhs=qT[q2][:, h0 + j, :],
                                               start=True, stop=True)
                                ins = e.matmul(pt[:, j * 256 + 128:j * 256 + 256], lhsT=kT[kprev][:, h0 + j, :],
                                               rhs=qT[q2][:, h0 + j, :], start=True, stop=True)
                            return ins
                        P.add("pe", smm, reads=[kTB[k3][h0 // 4], qTB[q2][h0 // 4], kTB[kprev][h0 // 4]], writes=[pb])
                        P.add("act", lambda e, pt=pt, pi=pi: e.activation(out=pT[pi], in_=pt[:], func=AF.Exp, scale=sm),
                              reads=[pb], writes=[pTB[pi]])
                        P.add("dve", lambda e, pi=pi, cm=cm: e.tensor_tensor(out=pT[pi], in0=pT[pi], in1=cm, op=ALU.mult),
                              reads=[consB], writes=[pTB[pi]])
                        return pi

                    def stage2(h0, pi):
                        po, pob = pbank()

                        def pvm(e, po=po, pi=pi, h0=h0, k3=k3, kprev=kprev):
                            for j in range(2):
                                ins = e.matmul(po[:, j * 129:(j + 1) * 129], lhsT=pT[pi][:, j * 256:j * 256 + 128],
                                               rhs=vt[k3][:, h0 + j, :], start=True, stop=False)
                                ins = e.matmul(po[:, j * 129:(j + 1) * 129], lhsT=pT[pi][:, j * 256 + 128:j * 256 + 256],
                                               rhs=vt[kprev][:, h0 + j, :], start=False, stop=True)
                            return ins
                        P.add("pe", pvm, reads=[pTB[pi], vtB[k3], vtB[kprev]], writes=[pob])
                        P.add("dve", lambda e, po=po, h0=h0, o2=o2: e.tensor_copy(
                            out=ot[o2][:, h0:h0 + 2, :], in_=po[:, :258].rearrange("p (h j) -> p h j", j=129)),
                            reads=[pob], writes=[otB[o2]])

                    hps = list(range(0, H, 2))
                    pis = {}
                    SK = 2
                    for k_ in range(len(hps) + SK):
                        if k_ < len(hps):
                            pis[k_] = stage1(hps[k_])
                        if k_ >= SK:
                            stage2(hps[k_ - SK], pis[k_ - SK])
                    P.dma("sp", lambda e, o2=o2, res=res, rows=rows, ov_d=ov_d: e.dma_start(
                        out=ov_d[res, rows, :], in_=ot[o2].rearrange("p h j -> p (h j)")), reads=[otB[o2]])
                    blk += 1
        P.barrier()

        opT_off = 0
        opT = abf(opT_off, NPC * T // 2).rearrange("p (c t) -> p c t", t=T)
        opTB = Buf()
        s1 = NPC * T // 2
        ZL = 16 + T
        def phaseD(i):
            t0 = i * T
            zb = [af32(s1 + k * PGC * ZL, PGC * ZL).rearrange("p (c t) -> p c t", t=ZL) for k in range(3)]
            zbB = [Buf() for _ in range(3)]
            yb = abf(s1 + 3 * PGC * ZL, PGC * T // 2).rearrange("p (c t) -> p c t", t=T)
            ybB = Buf()
            fx = af32(s1 + 3 * PGC * ZL + PGC * T // 2, PGC * 16).rearrange("p (c t) -> p c t", t=16)
            for g, w in enumerate(POOL_W):
                A, Bz, Cz = zb
                if i == 0:
                    P.dma("sp", lambda e, g=g: e.dma_start(
                        out=A[:, :, 0:16], in_=xzout[g * PGC:(g + 1) * PGC].rearrange("c p t -> p c t")),
                        reads=[xzB], writes=[zbB[0]])
                    P.add("dve", lambda e: e.tensor_scalar_mul(out=A[:, :, 0:16], in0=A[:, :, 0:16], scalar1=flag1),
                          reads=[consB], writes=[zbB[0]])
                    P.dma("sp", lambda e, g=g: e.dma_start(
                        out=A[:, :, 16:], in_=zp_s[g * PGC:(g + 1) * PGC, :, 0:T].rearrange("c p t -> p c t")),
                        writes=[zbB[0]])
                else:
                    P.dma("sp", lambda e, g=g: e.dma_start(
                        out=A[:], in_=zp_s[g * PGC:(g + 1) * PGC, :, t0 - 16:t0 + T].rearrange("c p t -> p c t")),
                        writes=[zbB[0]])
                chain = [(Bz, A, 1, 1), (Cz, Bz, 3, 2), (Bz, Cz, 7, 4), (Cz, Bz, 15, 8)]
                nsteps = {2: 1, 4: 2, 8: 3, 16: 4}[w]
                bufB = {id(A): zbB[0], id(Bz): zbB[1], id(Cz): zbB[2]}
                for (dst, src, lo, sh) in chain[:nsteps]:
                    P.add("dve", lambda e, dst=dst, src=src, lo=lo, sh=sh: e.tensor_tensor(
                        out=dst[:, :, lo:], in0=src[:, :, lo:], in1=src[:, :, lo - sh:ZL - sh], op=ALU.add),
                        reads=[bufB[id(src)]], writes=[bufB[id(dst)]])
                sw = chain[nsteps - 1][0]
                P.add("dve", lambda e, sw=sw, w=w: e.scalar_tensor_tensor(
                    out=yb[:], in0=sw[:, :, 16:], scalar=1.0 / w, in1=A[:, :, 16:], op0=ALU.mult, op1=ALU.subtract),
                    reads=[bufB[id(sw)], zbB[0]], writes=[ybB])
                if i == 0:
                    icb = invc[:, g * 16:(g + 1) * 16].rearrange("p (o t) -> p o t", o=1).to_broadcast([128, PGC, 16])
                    P.add("dve", lambda e, sw=sw, icb=icb: e.tensor_tensor(out=fx, in0=sw[:, :, 16:32], in1=icb, op=ALU.mult),
                          reads=[bufB[id(sw)], consB], writes=[ybB])
                    P.add("dve", lambda e: e.tensor_tensor(out=yb[:, :, 0:16], in0=fx, in1=A[:, :, 16:32], op=ALU.subtract),
                          reads=[zbB[0]], writes=[ybB])
                wt, wb = wslot()
                P.dma("pool", lambda e, wt=wt, g=g: e.dma_start(out=wt[:, :PGC * cfg.PG], in_=wpl_d[g]), writes=[wb])
                wv = wt[:, :PGC * cfg.PG].rearrange("p (c j) -> p c j", j=cfg.PG)
                for dc in range(PGC):
                    pt, pb = pbank()

                    def mm(e, pt=pt, wv=wv, dc=dc):
                        for cc in range(PGC):
                            ins = e.matmul(pt[:], lhsT=wv[:, cc, dc * 128:(dc + 1) * 128], rhs=yb[:, cc, :],
                                           start=(cc == 0), stop=(cc == PGC - 1))
                        return ins
                    P.add("pe", mm, reads=[wb, ybB], writes=[pb])
                    oc = g * PGC + dc
                    P.add("act", lambda e, pt=pt, oc=oc: e.activation(out=opT[:, oc, :], in_=pt[:], func=AF.Copy,
                                                                       scale=psc_t[:, oc:oc + 1]),
                          reads=[pb, consB], writes=[opTB])
            P.barrier()
            for st in range(NST):
                P.dma("sp", lambda e, st=st: e.dma_start(out=h[st], in_=h1_s[t0 + st * 128:t0 + (st + 1) * 128, :]),
                      writes=[hB[st]])
            oT = uT_t[:, 0:H * T].rearrange("p (c t) -> p c t", t=T)
            oTB = uTB
            obs = [uT_t[:, H * T + k * H * 128:H * T + (k + 1) * H * 128].rearrange("p (h j) -> p h j", j=128) for k in range(2)]
            s3 = s1
            OW = H * 129
            accs = [af32(s3 + k * OW, OW).rearrange("p (h j) -> p h j", j=129) for k in range(2)]
            tms = [af32(s3 + (2 + k) * OW, OW).rearrange("p (h j) -> p h j", j=129) for k in range(2)]
            rzs = [af32(s3 + 4 * OW + k * H, H) for k in range(2)]
            assert s3 + 4 * OW + 2 * H <= ARENA, "arena"
            accBs, tmBs, obBs = [Buf(), Buf()], [Buf(), Buf()], [Buf(), Buf()]
            for st in range(NST):
                rows = slice(t0 + st * 128, t0 + (st + 1) * 128)
                acc, accB = accs[st % 2], accBs[st % 2]
                rz = rzs[st % 2]
                ob, obB = obs[st % 2], obBs[st % 2]
                P.dma("sp", lambda e, rows=rows, acc=acc: e.dma_start(out=acc.rearrange("p h j -> p (h j)"), in_=o_s[0][rows, :]),
                      writes=[accB])
                for g in (1, 2):
                    tm2, tm2B = tms[g - 1], tmBs[g - 1]
                    P.dma("sp", lambda e, rows=rows, g=g, tm2=tm2: e.dma_start(out=tm2.rearrange("p h j -> p (h j)"), in_=o_s[g][rows, :]),
                          writes=[tm2B])
                    P.add("dve", lambda e, acc=acc, tm2=tm2: e.tensor_tensor(out=acc, in0=acc, in1=tm2, op=ALU.add),
                          reads=[tm2B], writes=[accB])
                P.add("dve", lambda e, acc=acc, rz=rz: e.reciprocal(out=rz, in_=acc[:, :, 128:129].rearrange("p h o -> p (h o)")),
                      writes=[accB])
                P.add("dve", lambda e, acc=acc, rz=rz, ob=ob: e.tensor_tensor(
                    out=ob, in0=acc[:, :, 0:128], in1=rz.rearrange("p (h o) -> p h o", o=1).to_broadcast([128, H, 128]),
                    op=ALU.mult), reads=[accB], writes=[obB])
                for h0 in range(0, H, 4):
                    pt, pb = pbank()

                    def mm(e, pt=pt, h0=h0, ob=ob):
                        for j in range(4):
                            ins = e.matmul(pt[:, j * 128:(j + 1) * 128], lhsT=ob[:, h0 + j, :], rhs=identb_t[:],
                                           start=True, stop=True)
                        return ins
                    P.add("pe", mm, reads=[obB, consB], writes=[pb])
                    P.add("act", lambda e, pt=pt, h0=h0, st=st: e.activation(
                        out=oT[:, h0:h0 + 4, st * 128:(st + 1) * 128], in_=pt[:].rearrange("p (a b) -> p a b", b=128),
                        func=AF.Copy), reads=[pb], writes=[oTB])

            def mo_lhs(k, st):
                if k < H:
                    return oT[:, k, st * 128:(st + 1) * 128]
                return opT[:, k - H, st * 128:(st + 1) * 128]

            def addh(n, st, pt, pb):
                P.add("dve", lambda e: e.tensor_tensor(out=h[st][:, n * 512:(n + 1) * 512], in0=pt[:],
                                                       in1=h[st][:, n * 512:(n + 1) * 512], op=ALU.add),
                      reads=[pb], writes=[hB[st]])
            r2(mo_lhs, [oTB, opTB], KC, wmo_d, NB, 512, NST, addh)
            for st in range(NST):
                norm_sub(h[st], hB[st], 2, st * 128, JUNK)
            qcT = abf(0, 4 * T // 2).rearrange("p (h t) -> p h t", t=T)
            qcB = Buf()
            ocT = abf(4 * T // 2, 4 * T // 2).rearrange("p (h t) -> p h t", t=T)
            ocB = Buf()
            pc = [abf(4 * T + k * T // 2, T // 2) for k in range(2)]
            pcB = [Buf(), Buf()]
            rzc = af32(4 * T + T, T)
            rzcB = Buf()
            for hh in range(4):
                r1(wcq_d, hh, T, lambda pt, pb, hh=hh: P.add(
                    "act", lambda e: e.activation(out=qcT[:, hh, :], in_=pt[:], func=AF.Copy), reads=[pb], writes=[qcB]))
            for hh in range(4):
                for m in range(2):
                    pt, pb = pbank()
                    P.add("pe", lambda e, pt=pt, hh=hh, m=m: e.matmul(
                        pt[:], lhsT=kmT[:, hh, m * 128:(m + 1) * 128], rhs=qcT[:, hh, :], start=True, stop=True),
                        reads=[memB, qcB], writes=[pb])
                    P.add("act", lambda e, pt=pt, m=m: e.activation(out=pc[m], in_=pt[:], func=AF.Exp, scale=sm),
                          reads=[pb], writes=[pcB[m]])
                po, pob = pbank()
                pz, pzb = pbank()

                def om(e, po=po, hh=hh):
                    for m in range(2):
                        ins = e.matmul(po[:], lhsT=vm[:, m, hh * 128:(hh + 1) * 128], rhs=pc[m], start=(m == 0), stop=(m == 1))
                    return ins

                def zm(e, pz=pz):
                    for m in range(2):
                        ins = e.matmul(pz[:], lhsT=onesb_t[:], rhs=pc[m], start=(m == 0), stop=(m == 1))
                    return ins
                P.add("pe", om, reads=[memB] + pcB, writes=[pob])
                P.add("pe", zm, reads=[consB] + pcB, writes=[pzb])
                P.add("dve", lambda e, pz=pz: e.reciprocal(out=rzc, in_=pz[:]), reads=[pzb], writes=[rzcB])
                P.add("dve", lambda e, po=po, hh=hh: e.tensor_tensor(out=ocT[:, hh, :], in0=po[:], in1=rzc, op=ALU.mult),
                      reads=[pob, rzcB], writes=[ocB])
            r2(lambda k, st: ocT[:, k, st * 128:(st + 1) * 128], [ocB], 4, wco_d, NB, 512, NST, addh)
            for st in range(NST):
                norm_sub(h[st], hB[st], 4, st * 128, JUNK)
            ffn(w2i_d, w2o_d)
            gfin = uT_t[:, KC * T // 2:KC * T].bitcast(F32)
            P.dma("sp", lambda e: e.dma_start(out=gfin, in_=gfin_d.partition_broadcast(128)), writes=[uTB])
            fB = [Buf() for _ in range(NST)]
            for st in range(NST):
                junk = abf(JUNK, D // 2)
                ssq = fin_t[:, st:st + 1]
                P.add("dve", lambda e, ssq=ssq: e.memset(ssq, 0.0), writes=[fB[st]])
                P.add("act", lambda e, st=st, ssq=ssq: e.activation(out=junk, in_=h[st], func=AF.Square, accum_out=ssq),
                      reads=[hB[st]], writes=[fB[st], arB])
                P.add("act", lambda e, ssq=ssq: e.activation(out=ssq, in_=ssq, func=AF.Sqrt, bias=eps_t[:], scale=1.0 / D),
                      reads=[consB], writes=[fB[st]])
                P.add("dve", lambda e, ssq=ssq: e.reciprocal(out=ssq, in_=ssq), writes=[fB[st]])
                P.add("dve", lambda e, st=st, ssq=ssq: e.scalar_tensor_tensor(
                    out=h[st], in0=h[st], scalar=ssq, in1=gfin, op0=ALU.mult, op1=ALU.mult),
                    reads=[fB[st], uTB], writes=[hB[st]])
                P.dma("sp", lambda e, st=st: e.dma_start(out=y_d[t0 + st * 128:t0 + (st + 1) * 128, :], in_=h[st]),
                      reads=[hB[st]], bar=False)
            P.barrier()

        for i in range(NT):
            phaseD(i)
        P.barrier(full=True)
        P.emit(nc, es)
    return nc


def _lhsT_layout(w, KC):
    K, n = w.shape
    return np.ascontiguousarray(w.reshape(KC, 128, n // 128, 128).transpose(2, 1, 0, 3)).reshape(n // 128, 128, KC * 128)


def _rhs_layout(w, cb):
    K, n = w.shape
    return np.ascontiguousarray(w.reshape(K // 128, 128, n // cb, cb).transpose(2, 0, 1, 3))


def prepare(cfg, x, mem, positions, g_ffn1, w_ffn1_in, w_ffn1_out, g_mix, w_mix_in, w_pool, pool_scale, w_mix_out,
            g_cross, g_mem, w_cross_q, w_cross_kv, w_cross_o, g_ffn2, w_ffn2_in, w_ffn2_out, g_final):
    import ml_dtypes
    D, S, KC, PGC = cfg.D, cfg.S, cfg.KC, cfg.PGC
    f = np.float32
    shared = {}
    gs = np.stack([np.asarray(g, f).reshape(D) for g in (g_ffn1[0], g_mix[0], g_cross[0], g_mem[0], g_ffn2[0])])
    shared["gfm"] = np.ascontiguousarray(gs.reshape(5, KC, 128).transpose(2, 0, 1)).reshape(128, 5 * KC)
    shared["gfin"] = np.asarray(g_final, f).reshape(1, D)
    shared["psc"] = np.ascontiguousarray(np.asarray(pool_scale[0], f).reshape(4 * PGC, 128).T)
    kk = np.arange(128)[:, None]
    qq = np.arange(128)[None, :]
    cur = (kk <= qq).astype(f)
    prev = (kk >= qq).astype(f)
    zero = np.zeros_like(prev)
    cm_all = np.concatenate([cur, prev, cur, prev], axis=1)
    cmasks = [np.concatenate([cm_all, np.concatenate([cur, prev * par, cur, prev * par], axis=1)], axis=1).astype(ml_dtypes.bfloat16)
              for par in (0.0, 1.0)]
    inv = (1.0 / (ROPE_THETA ** (np.arange(0, 128, 2, dtype=f) / f(128)))).astype(f)
    cf32s = []
    for par in (0, 1):
        if par == 0:
            invc = np.stack([1.0 / np.minimum(np.arange(16) + 1, w) for w in POOL_W]).astype(f).reshape(64)
        else:
            invc = np.stack([np.full(16, 1.0 / w) for w in POOL_W]).astype(f).reshape(64)
        flags = np.array([1.0 - par, float(par)], dtype=f)
        cf32s.append(np.concatenate([np.eye(128, dtype=f), np.tile(inv[None], (128, 1)), np.tile(invc[None], (128, 1)),
                                     np.tile(flags[None], (128, 1))], axis=1))
    shared["w1i"] = _lhsT_layout(np.asarray(w_ffn1_in[0], f), KC)
    shared["w1o"] = _rhs_layout(np.asarray(w_ffn1_out[0], f), 512)
    shared["w2i"] = _lhsT_layout(np.asarray(w_ffn2_in[0], f), KC)
    shared["w2o"] = _rhs_layout(np.asarray(w_ffn2_out[0], f), 512)
    wmi = np.asarray(w_mix_in[0], f)
    shared["wqkv"] = _rhs_layout(wmi[:, :cfg.ATTN_IN], cfg.CB)
    shared["wzp"] = _lhsT_layout(wmi[:, cfg.ATTN_IN:], KC)
    wp = np.asarray(w_pool[0], f)
    shared["wpl"] = np.ascontiguousarray(wp.reshape(4, PGC, 128, cfg.PG).transpose(0, 2, 1, 3)).reshape(4, 128, PGC * cfg.PG)
    shared["wmo"] = _rhs_layout(np.asarray(w_mix_out[0], f), 512)
    shared["wcq"] = _lhsT_layout(np.asarray(w_cross_q[0], f), KC)
    wkv = np.asarray(w_cross_kv[0], f)
    shared["wck"] = _lhsT_layout(wkv[:, :512], KC)
    shared["wcv"] = _rhs_layout(wkv[:, 512:], 512)
    shared["wco"] = _rhs_layout(np.asarray(w_cross_o[0], f), 512)
    in_maps = []
    for b in range(cfg.B):
        for par in (0, 1):
            m = dict(shared)
            sl = slice(par * S, (par + 1) * S)
            m["x"] = np.ascontiguousarray(np.asarray(x[b], f)[sl])
            m["mem"] = np.ascontiguousarray(np.asarray(mem[b], f))
            m["pos"] = np.ascontiguousarray(np.asarray(positions[b], np.int32)[sl].reshape(S // 128, 128).T)
            m["cmask"] = cmasks[par]
            m["cf32"] = cf32s[par]
            in_maps.append(m)
    return in_maps


def kernel(**inputs):
    x = inputs["x"]
    cfg = Cfg(D=x.shape[2], S=x.shape[1], B=x.shape[0])
    nc = build(cfg)
    in_maps = prepare(cfg, **inputs)
    res = run_bass_kernel_spmd(nc, in_maps, core_ids=list(range(2 * cfg.B)))
    ys = [np.asarray(r["y"], np.float32) for r in res.results]
    return np.stack([np.concatenate([ys[2 * b], ys[2 * b + 1]], axis=0) for b in range(cfg.B)], axis=0)
```
